# Optimizing a Trainium2 kernel written in Bass

```python
import jax, jax.numpy as jnp
from jax import lax
import numpy as np

D_MODEL = 2048
BATCH = 2
SEQ = 4096
DEPTH = 1

CONV_WIDTH = D_MODEL // 2
CONV_GROUPS = 8
CONV_GROUP_DIM = CONV_WIDTH // CONV_GROUPS
CONV_KERNEL = 31
HGRN_WIDTH = D_MODEL - CONV_WIDTH
HGRN_HEADS = 8
HGRN_EXPAND = HGRN_WIDTH // HGRN_HEADS
HGRN_HEAD_DIM = HGRN_WIDTH // HGRN_HEADS
HGRN_CHUNK = 64
HGRN_SUB = 8
IN_PROJ_DIM = 2 * CONV_WIDTH + 4 * HGRN_WIDTH
D_FF = 5632
FFN_KERNEL = 3
LN_EPS = 1e-5
RMS_EPS = 1e-6
ALPHA = (2.0 * DEPTH) ** 0.25
BETA = (8.0 * DEPTH) ** -0.25

kernel_name = "hymba_conformer_hgrn2_deepnorm"


def layer_norm(x, g, b):
    xf = x.astype(jnp.float32)
    mu = jnp.mean(xf, axis=-1, keepdims=True)
    var = jnp.mean(jnp.square(xf - mu), axis=-1, keepdims=True)
    y = (xf - mu) * lax.rsqrt(var + LN_EPS)
    return (y * g.astype(jnp.float32) + b.astype(jnp.float32)).astype(x.dtype)


def causal_dwconv(x, w, b):
    k = w.shape[0]
    c = x.shape[-1]
    y = lax.conv_general_dilated(
        x, w[:, None, :].astype(x.dtype), window_strides=(1,), padding=[(k - 1, 0)],
        dimension_numbers=("NWC", "WIO", "NWC"), feature_group_count=c)
    return y + b.astype(x.dtype)


def hgrn2_recurrence(q, k, v, log_f):
    B, T, H, N = q.shape
    Dv = v.shape[-1]
    C, L = HGRN_CHUNK, HGRN_SUB
    nc, ns = T // C, C // L

    def blocks(t):
        return t.astype(jnp.float32).reshape(B, nc, ns, L, H, -1).transpose(0, 4, 1, 2, 3, 5)

    qb, kb, vb = blocks(q), blocks(k), blocks(v)
    gb = blocks(log_f)
    cum = jnp.cumsum(gb.reshape(B, H, nc, C, N), axis=3)
    bb = cum.reshape(B, H, nc, ns, L, N)
    bend = bb[..., -1, :]

    idx = jnp.arange(ns)
    blk_lower = idx[:, None] > idx[None, :]
    e_off = bb[:, :, :, :, None, :, :] - bend[:, :, :, None, :, None, :]
    q_off = qb[:, :, :, :, None] * jnp.exp(jnp.where(blk_lower[:, :, None, None], e_off, -jnp.inf))
    k_off = kb * jnp.exp(bend[..., None, :] - bb)
    a_off = jnp.einsum("bhcijtn,bhcjsn->bhcijts", q_off, k_off)
    o_off = jnp.einsum("bhcijts,bhcjsd->bhcitd", a_off, vb)

    pos = jnp.arange(L)
    causal = pos[:, None] >= pos[None, :]
    e_d = bb[..., :, None, :] - bb[..., None, :, :]
    decay = jnp.exp(jnp.where(causal[:, :, None], e_d, -jnp.inf))
    a_d = jnp.einsum("bhcitn,bhcitsn,bhcisn->bhcits", qb, decay, kb)
    o_d = jnp.einsum("bhcits,bhcisd->bhcitd", a_d, vb)
    o_intra = (o_off + o_d).reshape(B, H, nc, C, Dv)

    qc = qb.reshape(B, H, nc, C, N)
    kc = kb.reshape(B, H, nc, C, N)
    vc = vb.reshape(B, H, nc, C, Dv)
    blast = cum[..., -1, :]
    q_in = qc * jnp.exp(cum)
    k_up = kc * jnp.exp(blast[..., None, :] - cum)

    def step(S, xs):
        q_i, k_i, v_i, bl_i = xs
        o_i = jnp.einsum("bhtn,bhnd->bhtd", q_i, S)
        S = S * jnp.exp(bl_i)[..., None] + jnp.einsum("bhtn,bhtd->bhnd", k_i, v_i)
        return S, o_i

    xs = (jnp.moveaxis(q_in, 2, 0), jnp.moveaxis(k_up, 2, 0),
          jnp.moveaxis(vc, 2, 0), jnp.moveaxis(blast, 2, 0))
    S0 = jnp.zeros((B, H, N, Dv), jnp.float32)
    _, o_inter = lax.scan(step, S0, xs)
    o = o_intra + jnp.moveaxis(o_inter, 0, 2)
    return o.reshape(B, H, T, Dv).transpose(0, 2, 1, 3)


def hybrid_mixer(x, w_in, conv_w, conv_b, conv_norm_g, conv_norm_b, lb_logits, hgrn_norm_g, w_out, layer):
    B, T, _ = x.shape
    h = x @ w_in
    c1, c2 = CONV_WIDTH, 2 * CONV_WIDTH
    a, gate = h[..., :c1], h[..., c1:c2]
    q, f, i, og = (h[..., c2 + n * HGRN_WIDTH: c2 + (n + 1) * HGRN_WIDTH] for n in range(4))

    u = a * jax.nn.sigmoid(gate)
    u = causal_dwconv(u, conv_w, conv_b)
    u = layer_norm(u.reshape(B, T, CONV_GROUPS, CONV_GROUP_DIM),
                   conv_norm_g.reshape(CONV_GROUPS, CONV_GROUP_DIM),
                   conv_norm_b.reshape(CONV_GROUPS, CONV_GROUP_DIM)).reshape(B, T, CONV_WIDTH)
    u = jax.nn.silu(u)

    lb_table = jnp.cumsum(jax.nn.softmax(lb_logits.astype(jnp.float32), axis=0), axis=0)
    lb = lb_table[layer]
    fg = lb + (1.0 - lb) * jax.nn.sigmoid(f.astype(jnp.float32))
    log_f = jnp.log(fg)
    kk = 1.0 - fg
    qh = jax.nn.silu(q.astype(jnp.float32))
    shp = (B, T, HGRN_HEADS, HGRN_EXPAND)
    o = hgrn2_recurrence(qh.reshape(shp), kk.reshape(shp),
                         i.reshape(B, T, HGRN_HEADS, HGRN_HEAD_DIM), log_f.reshape(shp))
    o = o * lax.rsqrt(jnp.mean(jnp.square(o), axis=-1, keepdims=True) + RMS_EPS)
    o = o.reshape(B, T, HGRN_WIDTH) * hgrn_norm_g.astype(jnp.float32)
    o = (o * jax.nn.silu(og.astype(jnp.float32))).astype(x.dtype)

    return jnp.concatenate([u, o], axis=-1) @ w_out


def conv_ffn(x, w_up, conv_w, conv_b, w_down):
    h = x @ w_up
    g, v = h[..., :D_FF], h[..., D_FF:]
    g = causal_dwconv(g, conv_w, conv_b)
    return (jax.nn.silu(g) * v) @ w_down


def setup_inputs(seed: int = 0) -> dict:
    key = jax.random.key(seed)
    ks = jax.random.split(key, 20)
    f32 = jnp.float32
    nrm = lambda k, s: jax.random.normal(k, s, f32)
    return {
        "x": nrm(ks[0], (BATCH, SEQ, D_MODEL)),
        "emb_ln_g": 1.0 + 0.02 * nrm(ks[1], (D_MODEL,)),
        "emb_ln_b": 0.02 * nrm(ks[2], (D_MODEL,)),
        "w_in": nrm(ks[3], (DEPTH, D_MODEL, IN_PROJ_DIM)) * D_MODEL ** -0.5,
        "conv_w": nrm(ks[4], (DEPTH, CONV_KERNEL, CONV_WIDTH)) * CONV_KERNEL ** -0.5,
        "conv_b": 0.01 * nrm(ks[5], (DEPTH, CONV_WIDTH)),
        "conv_norm_g": 1.0 + 0.02 * nrm(ks[6], (DEPTH, CONV_WIDTH)),
        "conv_norm_b": 0.02 * nrm(ks[7], (DEPTH, CONV_WIDTH)),
        "lb_logits": 0.5 * nrm(ks[8], (DEPTH + 1, HGRN_WIDTH)),
        "hgrn_norm_g": 1.0 + 0.02 * nrm(ks[9], (DEPTH, HGRN_WIDTH)),
        "w_out": nrm(ks[10], (DEPTH, D_MODEL, D_MODEL)) * (D_MODEL ** -0.5) * BETA,
        "ln1_g": 1.0 + 0.02 * nrm(ks[11], (DEPTH, D_MODEL)),
        "ln1_b": 0.02 * nrm(ks[12], (DEPTH, D_MODEL)),
        "w_ffn_up": nrm(ks[13], (DEPTH, D_MODEL, 2 * D_FF)) * D_MODEL ** -0.5,
        "ffn_conv_w": nrm(ks[14], (DEPTH, FFN_KERNEL, D_FF)) * FFN_KERNEL ** -0.5,
        "ffn_conv_b": 0.01 * nrm(ks[15], (DEPTH, D_FF)),
        "w_ffn_down": nrm(ks[16], (DEPTH, D_FF, D_MODEL)) * (D_FF ** -0.5) * BETA,
        "ln2_g": 1.0 + 0.02 * nrm(ks[17], (DEPTH, D_MODEL)),
        "ln2_b": 0.02 * nrm(ks[18], (DEPTH, D_MODEL)),
    }


def reference(x, emb_ln_g, emb_ln_b, w_in, conv_w, conv_b, conv_norm_g, conv_norm_b, lb_logits,
              hgrn_norm_g, w_out, ln1_g, ln1_b, w_ffn_up, ffn_conv_w, ffn_conv_b, w_ffn_down,
              ln2_g, ln2_b):
    h = layer_norm(x, emb_ln_g, emb_ln_b)
    for l in range(DEPTH):
        mix = hybrid_mixer(h, w_in[l], conv_w[l], conv_b[l], conv_norm_g[l], conv_norm_b[l],
                           lb_logits, hgrn_norm_g[l], w_out[l], l)
        h = layer_norm(ALPHA * h + mix, ln1_g[l], ln1_b[l])
        ffn = conv_ffn(h, w_ffn_up[l], ffn_conv_w[l], ffn_conv_b[l], w_ffn_down[l])
        h = layer_norm(ALPHA * h + ffn, ln2_g[l], ln2_b[l])
    return h
```

```python
import numpy as np
from contextlib import ExitStack
import concourse.bass as bass
import concourse.mybir as mybir
from concourse.bass_utils import run_bass_kernel_spmd

F32 = mybir.dt.float32
BF16 = mybir.dt.bfloat16
AF = mybir.ActivationFunctionType
ALU = mybir.AluOpType

D = 2048
KC = 16
T = 1024
HALO = 32
TB = T + HALO
NPRE = 3072
DFF = 5632
FC = 44
NG = 4
GC = FC // NG
ALPHA = 2.0 ** 0.25
LN_EPS = 1e-5
RMS_EPS = 1e-6
CK = 31

P_G0, P_B0, P_CW, P_CB, P_CNG, P_CNB, P_LBL, P_HG = 0, 16, 32, 280, 288, 296, 304, 320
P_G1, P_B1, P_FW, P_FB, P_G2, P_B2, P_HM = 328, 344, 360, 492, 536, 552, 568
NPP = 576
C_ID, C_CM, C_RM = 0, 128, 256
C_MA = 256 + TB
C_MB = C_MA + 1
NCC = 256 + TB + 2

BLOCKS = [(0, 512), (512, 1024), (1024, TB)]
TILES = [(0, 32)] + [(32 + 128 * i, 128) for i in range(8)]
CHUNKS = [(0, 0, 32, 0)]
for _i in range(8):
    CHUNKS.append((_i + 1, 0, 64, 32 + 128 * _i))
    CHUNKS.append((_i + 1, 64, 64, 32 + 128 * _i + 64))


class Sem:
    def __init__(self, nc, es, name):
        self.h = es.enter_context(nc.semaphore(name))
        self.count = 0
        self.name = name


class Buf:
    __slots__ = ("name", "w", "r", "excl")

    def __init__(self, name, excl=False):
        self.name = name
        self.w = {}
        self.r = {}
        self.excl = excl


def _merge(d, src):
    for k, v in src.items():
        if d.get(k, 0) < v:
            d[k] = v


class Eng:
    def __init__(self, kb, name, beng, is_pe=False):
        self.kb = kb
        self.name = name
        self.e = beng
        self.sem = Sem(kb.nc, kb.es, "s_" + name)
        self.known = {}
        self.is_pe = is_pe
        self.pending = False

    def wait(self, sem, val):
        if val <= 0 or self.known.get(sem, 0) >= val:
            return
        self.e.wait_ge(sem.h, val)
        self.known[sem] = val

    def _deps(self, reads, writes):
        deps = {}
        for b in reads:
            _merge(deps, b.w)
            if b.excl:
                for k, v in b.r.items():
                    if k is not self.sem and deps.get(k, 0) < v:
                        deps[k] = v
        for b in writes:
            _merge(deps, b.w)
            _merge(deps, b.r)
        for sem, val in deps.items():
            if sem is self.sem:
                if self.is_pe:
                    continue
                if val > sem.count:
                    continue
            self.wait(sem, val)

    def op(self, fn, reads=(), writes=(), signal=True):
        self._deps(reads, writes)
        inst = fn()
        val = self.sem.count + 1
        if signal:
            self.sem.count = val
            inst.then_inc(self.sem.h, 1)
            self.pending = False
        else:
            self.pending = True
        for b in reads:
            if b.r.get(self.sem, 0) < val:
                b.r[self.sem] = val
        for b in writes:
            b.w = {self.sem: val}
            b.r = {}
        return inst

    def dma(self, out, in_, sem, reads=(), writes=()):
        self._deps(reads, writes)
        self.wait(sem, sem.count)
        inst = self.e.dma_start(out=out, in_=in_)
        sem.count += 16
        inst.then_inc(sem.h, 16)
        for b in reads:
            b.r[sem] = sem.count
        for b in writes:
            b.w = {sem: sem.count}
            b.r = {}
        return inst


class Ring:
    def __init__(self, kb, name, shape, dtype, n, dma=False):
        self.tiles = [kb.sb(f"{name}{i}", shape, dtype) for i in range(n)]
        self.bufs = [Buf(f"{name}{i}") for i in range(n)]
        self.sems = [Sem(kb.nc, kb.es, f"d_{name}{i}") for i in range(n)] if dma else None
        self.i = 0
        self.n = n

    def next(self):
        k = self.i % self.n
        self.i += 1
        if self.sems:
            return self.tiles[k], self.bufs[k], self.sems[k]
        return self.tiles[k], self.bufs[k]


class KB:
    def __init__(self, dbg=()):
        self.dbg = list(dbg)
        self.dbg_out = {}

    def sb(self, name, shape, dtype):
        self._uid = getattr(self, "_uid", 0) + 1
        return self.es_cur.enter_context(self.nc.sbuf_tensor(f"{name}_{self._uid}", list(shape), dtype))

    def banks(self, n):
        if self.bank_ptr + n > 8:
            self.bank_ptr = 0
        b0 = self.bank_ptr
        self.bank_ptr = (self.bank_ptr + n) % 8
        return b0, self.bank_bufs[b0:b0 + n]

    def barrier(self):
        sems = [e.sem for e in self.engs] + self.all_dma_sems
        for e in self.engs:
            assert not e.pending, e.name
            for s in sems:
                if s is e.sem:
                    continue
                e.wait(s, s.count)

    def dsem(self, name):
        s = Sem(self.nc, self.es, name)
        self.all_dma_sems.append(s)
        return s

    def ring(self, name, shape, dtype, n, dma=False):
        r = Ring(self, name, shape, dtype, n, dma)
        if dma:
            self.all_dma_sems.extend(r.sems)
        return r

    def dump(self, name, tile_ap, shape, buf):
        if name not in self.dbg:
            return
        dt = tile_ap.dtype
        o = self.nc.dram_tensor("dbg_" + name, list(shape), dt, kind="ExternalOutput").ap()
        self.dbg_out[name] = (list(shape), dt)
        s = self.dsem("dbg_" + name)
        idx = tuple(slice(None) for _ in shape)
        self.sp.dma(o[idx], tile_ap, s, reads=[buf])
        self.final_waits.append(s)

    def build(self, stop_after=None):
        nc = bass.Bass("TRN2", target_bir_lowering=False)
        self.nc = nc
        self.stop_after = stop_after
        dr = lambda name, shape, kind="ExternalInput", dt=F32: nc.dram_tensor(name, list(shape), dt, kind=kind).ap()
        self.x_own = dr("x_own", [TB, D])
        self.x_pre = dr("x_pre", [NPRE, D])
        self.pmask_d = dr("pmask", [128, NPRE // 128])
        self.params_d = dr("params", [128, NPP])
        self.consts_d = dr("consts", [128, NCC])
        self.w_in = dr("w_in", [D, 6144])
        self.w_out = dr("w_out", [D, D])
        self.w_up = dr("w_up", [D, 2 * DFF])
        self.w_down = dr("w_down", [DFF, D])
        self.out_d = dr("out", [T, D], kind="ExternalOutput")
        self.h0s = nc.dram_tensor("h0s", [128, KC, TB], F32).ap()
        self.final_waits = []
        self.all_dma_sems = []
        with ExitStack() as es:
            self.es = es
            self.es_cur = es
            self.pe = Eng(self, "pe", nc.tensor, is_pe=True)
            self.act = Eng(self, "act", nc.scalar)
            self.dve = Eng(self, "dve", nc.vector)
            self.gp = Eng(self, "gp", nc.gpsimd)
            self.sp = Eng(self, "sp", nc.sync)
            self.engs = [self.pe, self.act, self.dve, self.gp, self.sp]
            self.ps = es.enter_context(nc.psum_tensor("ps", [128, 8 * 512], F32))
            self.psb = self.ps[:, :].bitcast(BF16)
            self.bank_bufs = [Buf(f"bank{i}", excl=True) for i in range(8)]
            self.bank_ptr = 0
            self.setup()
            if stop_after != "setup":
                self.run_phases()
            for s in self.final_waits:
                self.sp.wait(s, s.count)
            for e in self.engs:
                assert not e.pending, e.name
        return nc

    def setup(self):
        nc = self.nc
        self.par = self.sb("par", [128, NPP], F32)
        self.par_b = Buf("par")
        self.cst = self.sb("cst", [128, NCC], F32)
        self.cst_b = Buf("cst")
        self.pm = self.sb("pm", [128, NPRE // 128], F32)
        s = self.dsem("ld_setup")
        self.sp.dma(self.par[:], self.params_d[:, :], s, writes=[self.par_b])
        self.sp.dma(self.cst[:], self.consts_d[:, :], s, writes=[self.cst_b])
        self.sp.dma(self.pm[:], self.pmask_d[:, :], s, writes=[self.par_b])
        self.ident = self.cst[:, C_ID:C_ID + 128]
        self.cmask = self.cst[:, C_CM:C_CM + 128]
        self.rmask = self.cst[:, C_RM:C_RM + TB]
        self.ident_bf = self.sb("ident_bf", [128, 128], BF16)
        self.ones128 = self.sb("ones128", [128, 128], F32)
        self.onesD = self.sb("onesD", [128, 128], F32)
        self.ones512 = self.sb("ones512", [128, 512], F32)
        self.lbv = self.sb("lbv", [128, 8], F32)
        self.oml = self.sb("oml", [128, 8], F32)
        self.noml = self.sb("noml", [128, 8], F32)
        self.misc_b = Buf("misc")
        self.Sin = self.sb("Sin", [128, 8, 128], F32)
        self.Sin_b = Buf("Sin")
        self.Sin_bf = self.sb("Sin_bf", [128, 8, 128], BF16)
        self.Sinbf_b = Buf("Sin_bf")
        d = self.dve
        d.op(lambda: nc.vector.tensor_copy(self.ident_bf[:], self.ident), reads=[self.cst_b], writes=[self.misc_b])
        d.op(lambda: nc.vector.memset(self.ones128[:], 1.0 / 128.0), writes=[self.misc_b])
        d.op(lambda: nc.vector.memset(self.onesD[:], 1.0 / D), writes=[self.misc_b])
        d.op(lambda: nc.vector.memset(self.ones512[:], 1.0), writes=[self.misc_b])
        d.op(lambda: nc.vector.memset(self.Sin[:], 0.0), writes=[self.Sin_b])
        lbl = self.par[:, P_LBL:P_LBL + 16].rearrange("p (j two) -> p j two", two=2)
        d.op(lambda: nc.vector.tensor_tensor(self.noml[:], lbl[:, :, 0], lbl[:, :, 1], ALU.subtract),
             reads=[self.par_b], writes=[self.misc_b])
        self.act.op(lambda: nc.scalar.activation(out=self.lbv[:], in_=self.noml[:], func=AF.Sigmoid),
                    reads=[self.misc_b], writes=[self.misc_b])
        d.op(lambda: nc.vector.tensor_scalar(self.oml[:], self.lbv[:], -1.0, 1.0, ALU.mult, ALU.add),
             reads=[self.misc_b], writes=[self.misc_b])
        d.op(lambda: nc.vector.tensor_scalar(self.noml[:], self.oml[:], -1.0, None, ALU.mult),
             reads=[self.misc_b], writes=[self.misc_b])

    def pcol(self, off, j=0):
        return self.par[:, off + j:off + j + 1]

    def load_w(self, ring, dram_ap_2d, ncols, nk=KC):
        t, b, s = ring.next()
        self.gp.dma(t[:, 0:nk, 0:ncols], dram_ap_2d.rearrange("(kc p) c -> p kc c", p=128), s, writes=[b])
        return t, b

    def ln0_tile(self, x_rows_ap, ntok, xt_ring, tmp):
        nc = self.nc
        xt, xb, xs = xt_ring.next()
        self.sp.dma(xt[0:ntok, :], x_rows_ap, xs, writes=[xb])
        st, stb = tmp["st"].next()
        for q in range(4):
            self.dve.op(lambda q=q: nc.vector.bn_stats(st[0:ntok, q, :], xt[0:ntok, q * 512:(q + 1) * 512]),
                        reads=[xb], writes=[stb])
        mv, mvb = tmp["mv"].next()
        self.dve.op(lambda: nc.vector.bn_aggr(mv[0:ntok, 0:2], st[0:ntok, :, :].rearrange("p a b -> p (a b)")),
                    reads=[stb], writes=[mvb])
        self.dve.op(lambda: nc.vector.tensor_scalar(mv[0:ntok, 2:3], mv[0:ntok, 1:2], LN_EPS, None, ALU.add),
                    reads=[mvb], writes=[mvb])
        self.act.op(lambda: nc.scalar.activation(out=mv[0:ntok, 3:4], in_=mv[0:ntok, 2:3], func=AF.Sqrt),
                    reads=[mvb], writes=[mvb])
        self.dve.op(lambda: nc.vector.reciprocal(mv[0:ntok, 4:5], mv[0:ntok, 3:4]), reads=[mvb], writes=[mvb])
        self.dve.op(lambda: nc.vector.tensor_scalar(mv[0:ntok, 5:6], mv[0:ntok, 0:1], -1.0, mv[0:ntok, 4:5],
                                                    ALU.mult, ALU.mult), reads=[mvb], writes=[mvb])
        return xt, xb, mv, mvb

    def run_phases(self):
        self.phase_prefix()
        if self.stop_after == "prefix":
            return
        self.phase_mixer()

    def phase_prefix(self):
        nc = self.nc
        pe, act, dve = self.pe, self.act, self.dve
        with ExitStack() as es:
            self.es_cur = es
            wring = self.ring("pw", [128, KC, 512], BF16, 4, dma=True)
            wts = []
            for c0 in (3072, 3072 + 512, 4096, 4096 + 512):
                wts.append(self.load_w(wring, self.w_in[:, c0:c0 + 512], 512))
            xt_ring = self.ring("pxt", [128, D], F32, 2, dma=True)
            tmp = {"st": self.ring("pst", [128, 4, 6], F32, 2), "mv": self.ring("pmv", [128, 8], F32, 4)}
            xn_ring = self.ring("pxn", [128, 4, D], BF16, 1)
            h0p_ring = self.ring("ph0", [128, KC, 512], BF16, 2)
            t_ring = self.ring("pt", [128, 512], F32, 8)
            wT_ring = self.ring("pwT", [128, 512], BF16, 2)
            wtok_ring = self.ring("pwk", [128, 4, 1024], BF16, 2)
            v_ring = self.ring("pv", [128, 4, 1024], BF16, 2)
            carry = self.sb("pcarry", [128, 8], F32)
            carry_b = Buf("carry")
            dve.op(lambda: nc.vector.memset(carry[:], 0.0), writes=[carry_b])
            first = True
            for pb in range(NPRE // 512 - 1, -1, -1):
                xn, xnb = xn_ring.next()
                for tt in range(4):
                    g = pb * 4 + tt
                    xt, xb, mv, mvb = self.ln0_tile(self.x_pre[g * 128:(g + 1) * 128, :], 128, xt_ring, tmp)
                    act.op(lambda: nc.scalar.activation(out=xn[:, tt, :], in_=xt[:, :], func=AF.Identity,
                                                        scale=mv[:, 4:5], bias=mv[:, 5:6]),
                           reads=[xb, mvb], writes=[xnb])
                h0p, h0b = h0p_ring.next()
                for kg in range(4):
                    b0, bb = self.banks(2)
                    for kk in range(4):
                        kc = kg * 4 + kk
                        for tt in range(4):
                            last = (kk == 3 and tt == 3)
                            o = self.psb[:, b0 * 1024 + kk * 512 + tt * 128: b0 * 1024 + kk * 512 + (tt + 1) * 128]
                            pe.op(lambda o=o, kc=kc, tt=tt: nc.tensor.transpose(o, xn[:, tt, kc * 128:(kc + 1) * 128],
                                                                                 self.ident_bf[:]),
                                  reads=[xnb, self.misc_b], writes=bb, signal=last)
                    for kk in range(4):
                        kc = kg * 4 + kk
                        src = self.psb[:, b0 * 1024 + kk * 512: b0 * 1024 + (kk + 1) * 512]
                        act.op(lambda src=src, kc=kc: nc.scalar.activation(
                            out=h0p[:, kc, :], in_=src, func=AF.Identity,
                            scale=self.pcol(P_G0, kc), bias=self.pcol(P_B0, kc)),
                            reads=bb + [self.par_b], writes=[h0b])
                vt, vb = v_ring.next()
                for tt in range(4):
                    b0, bb = self.banks(2)
                    for half in range(2):
                        w_t, w_b = wts[2 + half]
                        for kc in range(KC):
                            pe.op(lambda half=half, kc=kc, w_t=w_t: nc.tensor.matmul(
                                self.ps[:, (b0 + half) * 512:(b0 + half + 1) * 512],
                                lhsT=h0p[:, kc, tt * 128:(tt + 1) * 128], rhs=w_t[:, kc, :],
                                start=(kc == 0), stop=(kc == KC - 1)),
                                reads=[h0b, w_b], writes=bb, signal=(half == 1 and kc == KC - 1))
                    dve.op(lambda tt=tt, b0=b0: nc.vector.tensor_copy(vt[:, tt, :], self.ps[:, b0 * 512:(b0 + 2) * 512]),
                           reads=bb, writes=[vb])
                wk, wkb = wtok_ring.next()
                for j in range(8):
                    w_t, w_b = wts[j // 4]
                    b0, bb = self.banks(1)
                    for kc in range(KC):
                        pe.op(lambda kc=kc, w_t=w_t, j=j, b0=b0: nc.tensor.matmul(
                            self.ps[:, b0 * 512:(b0 + 1) * 512],
                            lhsT=w_t[:, kc, (j % 4) * 128:(j % 4 + 1) * 128], rhs=h0p[:, kc, :],
                            start=(kc == 0), stop=(kc == KC - 1)),
                            reads=[h0b, w_b], writes=bb, signal=(kc == KC - 1))
                    sig, sigb = t_ring.next()
                    act.op(lambda b0=b0: nc.scalar.activation(out=sig[:], in_=self.ps[:, b0 * 512:(b0 + 1) * 512],
                                                              func=AF.Sigmoid), reads=bb, writes=[sigb])
                    kkt, kkb = t_ring.next()
                    dve.op(lambda j=j: nc.vector.tensor_scalar(kkt[:], sig[:], self.noml[:, j:j + 1], self.oml[:, j:j + 1],
                                                               ALU.mult, ALU.add), reads=[sigb, self.misc_b], writes=[kkb])
                    lf, lfb = t_ring.next()
                    act.op(lambda j=j: nc.scalar.activation(out=lf[:], in_=sig[:], func=AF.Ln,
                                                            scale=self.oml[:, j:j + 1], bias=self.lbv[:, j:j + 1]),
                           reads=[sigb, self.misc_b], writes=[lfb])
                    cb_, cbb = t_ring.next()
                    dve.op(lambda: nc.vector.tensor_tensor_scan(cb_[:], self.ones512[:], lf[:], 0.0, ALU.mult, ALU.add),
                           reads=[lfb, self.misc_b], writes=[cbb])
                    dve.op(lambda j=j: nc.vector.tensor_tensor(carry[:, j:j + 1], carry[:, j:j + 1], cb_[:, 511:512], ALU.add),
                           reads=[cbb], writes=[carry_b])
                    act.op(lambda j=j: nc.scalar.activation(out=lf[:], in_=cb_[:], func=AF.Exp, scale=-1.0,
                                                            bias=carry[:, j:j + 1]),
                           reads=[cbb, carry_b], writes=[lfb])
                    wT, wTb = wT_ring.next()
                    dve.op(lambda: nc.vector.tensor_tensor(wT[:], kkt[:], lf[:], ALU.mult), reads=[kkb, lfb], writes=[wTb])
                    b1, bb1 = self.banks(1)
                    for tt in range(4):
                        o = self.psb[:, b1 * 1024 + tt * 128: b1 * 1024 + (tt + 1) * 128]
                        pe.op(lambda o=o, tt=tt: nc.tensor.transpose(o, wT[:, tt * 128:(tt + 1) * 128], self.ident_bf[:]),
                              reads=[wTb, self.misc_b], writes=bb1, signal=(tt == 3))
                    src = self.psb[:, b1 * 1024: b1 * 1024 + 512].rearrange("p (a b) -> p a b", a=4)
                    msk = self.pm[:, pb * 4:pb * 4 + 4].unsqueeze(2).to_broadcast([128, 4, 128])
                    dve.op(lambda src=src, msk=msk, j=j: nc.vector.tensor_tensor(wk[:, :, j * 128:(j + 1) * 128], src, msk, ALU.mult),
                           reads=bb1 + [self.par_b], writes=[wkb])
                for hb in range(2):
                    b0, bb = self.banks(1)
                    for hh in range(4):
                        h = hb * 4 + hh
                        for tt in range(4):
                            pe.op(lambda h=h, hh=hh, tt=tt, b0=b0: nc.tensor.matmul(
                                self.ps[:, b0 * 512 + hh * 128: b0 * 512 + (hh + 1) * 128],
                                lhsT=wk[:, tt, h * 128:(h + 1) * 128], rhs=vt[:, tt, h * 128:(h + 1) * 128],
                                start=(tt == 0), stop=(tt == 3)),
                                reads=[wkb, vb], writes=bb, signal=(hh == 3 and tt == 3))
                    sl = self.Sin[:, hb * 4:(hb + 1) * 4, :].rearrange("p a b -> p (a b)")
                    dve.op(lambda sl=sl, b0=b0: nc.vector.tensor_tensor(sl, sl, self.ps[:, b0 * 512:(b0 + 1) * 512], ALU.add),
                           reads=bb + [self.Sin_b], writes=[self.Sin_b])
            dve.op(lambda: nc.vector.tensor_copy(self.Sin_bf[:], self.Sin[:]), reads=[self.Sin_b], writes=[self.Sinbf_b])
            self.dump("Sin", self.Sin[:], [128, 8, 128], self.Sin_b)
            self.barrier()
        self.es_cur = self.es

    def fm_job(self, w_t, w_b, src, src_bufs, lo=0, hi=TB, ncol=128, wcol0=0):
        nc = self.nc
        b0, bb = self.banks(3)
        blks = [(max(a, lo), min(b, hi)) for (a, b) in BLOCKS if max(a, lo) < min(b, hi)]
        for bi, (a, b) in enumerate(blks):
            for kc in range(KC):
                self.pe.op(lambda a=a, b=b, kc=kc: nc.tensor.matmul(
                    self.ps[0:ncol, b0 * 512 + a: b0 * 512 + b], lhsT=w_t[:, kc, wcol0:wcol0 + ncol], rhs=src[:, kc, a:b],
                    start=(kc == 0), stop=(kc == KC - 1)),
                    reads=src_bufs + [w_b], writes=bb, signal=(bi == len(blks) - 1 and kc == KC - 1))
        return b0, bb

    def phase_mixer(self):
        nc = self.nc
        pe, act, dve = self.pe, self.act, self.dve
        with ExitStack() as esB:
            self.es_cur = esB
            self.hT = self.sb("hT", [128, KC, TB], BF16)
            self.hT_b = [Buf(f"hT{k}") for k in range(KC)]
            self.cat = self.sb("cat", [128, KC, TB], BF16)
            self.cat_b = [Buf(f"cat{k}") for k in range(KC)]
            with ExitStack() as esA:
                self.es_cur = esA
                wring = self.ring("mw", [128, KC, 128], BF16, 4, dma=True)
                vbf = self.sb("vbf", [128, 9, 1024], BF16)
                vbf_b = [Buf(f"vbf{i}") for i in range(9)]
                dve.op(lambda: nc.vector.memset(vbf[:, 0, :], 0.0), writes=[vbf_b[0]])
                with ExitStack() as es1:
                    self.es_cur = es1
                    xt_ring = self.ring("mxt", [128, D], F32, 2, dma=True)
                    tmp = {"st": self.ring("mst", [128, 4, 6], F32, 2), "mv": self.ring("mmv", [128, 8], F32, 4)}
                    xn_ring = self.ring("mxn", [128, D], F32, 2)
                    stg_ring = self.ring("mstg", [128, KC, 128], F32, 2, dma=True)
                    for ti, (tlo, ntok) in enumerate(TILES):
                        xt, xb, mv, mvb = self.ln0_tile(self.x_own[tlo:tlo + ntok, :], ntok, xt_ring, tmp)
                        xn, xnb = xn_ring.next()
                        act.op(lambda: nc.scalar.activation(out=xn[0:ntok, :], in_=xt[0:ntok, :], func=AF.Identity,
                                                            scale=mv[0:ntok, 4:5], bias=mv[0:ntok, 5:6]),
                               reads=[xb, mvb], writes=[xnb])
                        stg, stgb, stgs = stg_ring.next()
                        for kg in range(4):
                            b0, bb = self.banks(1)
                            for kk in range(4):
                                kc = kg * 4 + kk
                                pe.op(lambda kc=kc, kk=kk, b0=b0: nc.tensor.transpose(
                                    self.ps[:, b0 * 512 + kk * 128: b0 * 512 + kk * 128 + ntok],
                                    xn[0:ntok, kc * 128:(kc + 1) * 128], self.ident[0:ntok, 0:ntok]),
                                    reads=[xnb, self.cst_b], writes=bb, signal=(kk == 3))
                            for kk in range(4):
                                kc = kg * 4 + kk
                                src = self.ps[:, b0 * 512 + kk * 128: b0 * 512 + kk * 128 + ntok]
                                act.op(lambda src=src, kc=kc: nc.scalar.activation(
                                    out=stg[:, kc, 0:ntok], in_=src, func=AF.Identity,
                                    scale=self.pcol(P_G0, kc), bias=self.pcol(P_B0, kc)),
                                    reads=bb + [self.par_b], writes=[stgb])
                                dve.op(lambda kc=kc, stg=stg: nc.vector.tensor_copy(
                                    self.hT[:, kc, tlo:tlo + ntok], stg[:, kc, 0:ntok]),
                                    reads=[stgb], writes=[self.hT_b[kc]])
                        self.sp.dma(self.h0s[:, :, tlo:tlo + ntok], stg[:, :, 0:ntok], stgs, reads=[stgb])
                    self.h0s_sems = list(stg_ring.sems)
                    self.dump("hT0", self.hT[:], [128, KC, TB], self.hT_b[KC - 1])
                    self.barrier()
                self.es_cur = esA
                if self.stop_after == "ln0":
                    self.barrier()
                    self.es_cur = self.es
                    return
                with ExitStack() as es2:
                    self.es_cur = es2
                    iring = self.ring("mwi", [128, KC, 512], BF16, 2, dma=True)
                    wi = [self.load_w(iring, self.w_in[:, 4096 + hf * 512:4096 + (hf + 1) * 512], 512) for hf in range(2)]
                    for ti, (tlo, ntok) in enumerate(TILES):
                        b0, bb = self.banks(2)
                        for half in range(2):
                            w_t, w_b = wi[half]
                            for kc in range(KC):
                                pe.op(lambda half=half, kc=kc, w_t=w_t, b0=b0, tlo=tlo, ntok=ntok: nc.tensor.matmul(
                                    self.ps[0:ntok, (b0 + half) * 512:(b0 + half + 1) * 512],
                                    lhsT=self.hT[:, kc, tlo:tlo + ntok], rhs=w_t[:, kc, :],
                                    start=(kc == 0), stop=(kc == KC - 1)),
                                    reads=self.hT_b + [w_b], writes=bb, signal=(half == 1 and kc == KC - 1))
                        act.op(lambda ti=ti, b0=b0, ntok=ntok: nc.scalar.activation(
                            out=vbf[0:ntok, ti, :], in_=self.ps[0:ntok, b0 * 512:(b0 + 2) * 512], func=AF.Copy),
                            reads=bb, writes=[vbf_b[ti]])
                    self.barrier()
                self.es_cur = esA
                with ExitStack() as es3:
                    self.es_cur = es3
                    self.conv_branch(wring)
                    self.barrier()
                self.es_cur = esA
                if self.stop_after == "conv":
                    self.es_cur = self.es
                    return
                with ExitStack() as es4:
                    self.es_cur = es4
                    self.hgrn_branch(wring, vbf, vbf_b)
                    self.barrier()
                self.es_cur = esA
            self.es_cur = esB
            self.dump("cat", self.cat[:], [128, KC, TB], self.cat_b[KC - 1])
            if self.stop_after == "hgrn":
                self.barrier()
                self.es_cur = self.es
                return
            self.phase_out_ffn()
        self.es_cur = self.es

    def conv_branch(self, wring):
        nc = self.nc
        pe, act, dve = self.pe, self.act, self.dve
        t_ring = self.ring("ct", [128, TB], F32, 6)
        pend = None
        jobs = []
        for j in range(8):
            jobs.append((j, 0))
            jobs.append((j, 1))
        loaded = {}
        def ensure(idx):
            if idx < len(jobs) and idx not in loaded:
                jj, which = jobs[idx]
                c0 = which * 1024 + jj * 128
                loaded[idx] = self.load_w(wring, self.w_in[:, c0:c0 + 128], 128)
        ensure(0)
        ensure(1)
        for j in range(8):
            ensure(2 * j + 2)
            ensure(2 * j + 3)
            wa, wab = loaded[2 * j]
            wg, wgb = loaded[2 * j + 1]
            ba, bba = self.fm_job(wa, wab, self.hT, self.hT_b)
            bg, bbg = self.fm_job(wg, wgb, self.hT, self.hT_b)
            sg, sgb = t_ring.next()
            act.op(lambda: nc.scalar.activation(out=sg[:], in_=self.ps[:, bg * 512: bg * 512 + TB], func=AF.Sigmoid),
                   reads=bbg, writes=[sgb])
            u, ub = t_ring.next()
            dve.op(lambda: nc.vector.tensor_tensor(u[:], self.ps[:, ba * 512: ba * 512 + TB], sg[:], ALU.mult),
                   reads=bba + [sgb], writes=[ub])
            dve.op(lambda: nc.vector.tensor_scalar(u[:, 0:HALO], u[:, 0:HALO], self.pcol(P_HM), None, ALU.mult),
                   reads=[ub, self.par_b], writes=[ub])
            NO = TB - 30
            acc, accb = t_ring.next()
            cw = lambda k: self.par[:, P_CW + j * CK + k: P_CW + j * CK + k + 1]
            dve.op(lambda: nc.vector.tensor_scalar(acc[:, 30:TB], u[:, 0:NO], cw(0), self.pcol(P_CB, j), ALU.mult, ALU.add),
                   reads=[ub, self.par_b], writes=[accb])
            for k in range(1, CK):
                dve.op(lambda k=k: nc.vector.scalar_tensor_tensor(acc[:, 30:TB], u[:, k:k + NO], cw(k), acc[:, 30:TB],
                                                                   ALU.mult, ALU.add),
                       reads=[ub, accb, self.par_b], writes=[accb])
            sq, sqb = t_ring.next()
            act.op(lambda: nc.scalar.activation(out=sq[:, 30:TB], in_=acc[:, 30:TB], func=AF.Square), reads=[accb], writes=[sqb])
            bm, bbm = self.stat_mm(self.ones128, acc, accb, 30, TB)
            bq, bbq = self.stat_mm(self.ones128, sq, sqb, 30, TB)
            mean = self.ps[:, bm * 512 + 30: bm * 512 + TB]
            ex2 = self.ps[:, bq * 512 + 30: bq * 512 + TB]
            rstd, rstdb = self.rstd_from(mean, bbm, ex2, bbq, t_ring, NO, LN_EPS)
            dve.op(lambda: nc.vector.tensor_tensor(acc[:, 30:TB], acc[:, 30:TB], mean, ALU.subtract),
                   reads=bbm + [accb], writes=[accb])
            dve.op(lambda: nc.vector.scalar_tensor_tensor(acc[:, 30:TB], acc[:, 30:TB], self.pcol(P_CNG, j), rstd[:, 0:NO],
                                                          ALU.mult, ALU.mult),
                   reads=[accb, rstdb, self.par_b], writes=[accb])
            act.op(lambda: nc.scalar.activation(out=self.cat[:, j, 30:TB], in_=acc[:, 30:TB], func=AF.Silu,
                                                bias=self.pcol(P_CNB, j)),
                   reads=[accb, self.par_b], writes=[self.cat_b[j]])
            if j == 0:
                self.dump("u0", u[:], [128, TB], ub)
        for j in range(8):
            dve.op(lambda j=j: nc.vector.memset(self.cat[:, j, 0:30], 0.0), reads=[self.cat_b[j]], writes=[self.cat_b[j]])

    def stat_mm(self, ones, src, srcb, lo, hi):
        nc = self.nc
        b0, bb = self.banks(3)
        blks = [(max(a, lo), min(b, hi)) for (a, b) in BLOCKS if max(a, lo) < min(b, hi)]
        for bi, (a, b) in enumerate(blks):
            self.pe.op(lambda a=a, b=b: nc.tensor.matmul(self.ps[:, b0 * 512 + a: b0 * 512 + b], lhsT=ones[:], rhs=src[:, a:b],
                                                         start=True, stop=True),
                       reads=[srcb, self.misc_b], writes=bb, signal=(bi == len(blks) - 1))
        return b0, bb

    def rstd_from(self, mean, bbm, ex2, bbq, t_ring, n, eps):
        nc = self.nc
        m2, m2b = t_ring.next()
        self.act.op(lambda: nc.scalar.activation(out=m2[:, 0:n], in_=mean, func=AF.Square), reads=bbm, writes=[m2b])
        self.dve.op(lambda: nc.vector.scalar_tensor_tensor(m2[:, 0:n], ex2, eps, m2[:, 0:n], ALU.add, ALU.subtract),
                    reads=bbq + [m2b], writes=[m2b])
        self.act.op(lambda: nc.scalar.activation(out=m2[:, 0:n], in_=m2[:, 0:n], func=AF.Sqrt), reads=[m2b], writes=[m2b])
        self.dve.op(lambda: nc.vector.reciprocal(m2[:, 0:n], m2[:, 0:n]), reads=[m2b], writes=[m2b])
        return m2, m2b

    def hgrn_branch(self, wring, vbf, vbf_b):
        nc = self.nc
        pe, act, dve = self.pe, self.act, self.dve
        t_ring = self.ring("ht", [128, TB], F32, 7)
        q_ring = self.ring("hq", [128, TB], BF16, 2)
        k_ring = self.ring("hk", [128, TB], BF16, 2)
        ktok_ring = self.ring("hkt", [128, 2, 9, 128], BF16, 2)
        at_ring = self.ring("hat", [128, 9, 128], BF16, 2)
        for i_ in range(2):
            dve.op(lambda i_=i_: nc.vector.memset(ktok_ring.tiles[i_][:, 0, 0, :], 0.0), writes=[ktok_ring.bufs[i_]])
            dve.op(lambda i_=i_: nc.vector.memset(at_ring.tiles[i_][:, 0, :], 0.0), writes=[at_ring.bufs[i_]])
        sbf_ring = self.ring("hsb", [128, 17, 128], BF16, 2)
        sf_ring = self.ring("hsf", [128, 2, 128], F32, 2)
        cols = {"q": 2048, "f": 3072, "og": 5120}
        order = []
        for h in range(8):
            order += [(h, "f"), (h, "q"), (h, "og")]
        loaded = {}
        def ensure(idx):
            if idx < len(order) and idx not in loaded:
                hh, nm = order[idx]
                c0 = cols[nm] + hh * 128
                loaded[idx] = self.load_w(wring, self.w_in[:, c0:c0 + 128], 128)
        for idx in range(3):
            ensure(idx)
        for h in range(8):
            wf_, wfb = loaded[3 * h]
            bf_, bbf = self.fm_job(wf_, wfb, self.hT, self.hT_b)
            ensure(3 * h + 3)
            sig, sigb = t_ring.next()
            act.op(lambda: nc.scalar.activation(out=sig[:], in_=self.ps[:, bf_ * 512: bf_ * 512 + TB], func=AF.Sigmoid),
                   reads=bbf, writes=[sigb])
            kk, kkb = t_ring.next()
            dve.op(lambda: nc.vector.tensor_scalar(kk[:], sig[:], self.noml[:, h:h + 1], self.oml[:, h:h + 1], ALU.mult, ALU.add),
                   reads=[sigb, self.misc_b], writes=[kkb])
            lf, lfb = t_ring.next()
            act.op(lambda: nc.scalar.activation(out=lf[:], in_=sig[:], func=AF.Ln, scale=self.oml[:, h:h + 1],
                                                bias=self.lbv[:, h:h + 1]), reads=[sigb, self.misc_b], writes=[lfb])
            cum, cumb = t_ring.next()
            dve.op(lambda: nc.vector.tensor_tensor_scan(cum[:], self.rmask, lf[:], 0.0, ALU.mult, ALU.add),
                   reads=[lfb, self.cst_b], writes=[cumb])
            eq, eqb = t_ring.next()
            act.op(lambda: nc.scalar.activation(out=eq[:], in_=cum[:], func=AF.Exp), reads=[cumb], writes=[eqb])
            act.op(lambda: nc.scalar.activation(out=lf[:], in_=cum[:], func=AF.Exp, scale=-1.0), reads=[cumb], writes=[lfb])
            kT, kTb = k_ring.next()
            dve.op(lambda: nc.vector.tensor_tensor(kT[:], kk[:], lf[:], ALU.mult), reads=[kkb, lfb], writes=[kTb])
            dve.op(lambda: nc.vector.tensor_scalar(kT[:, 0:HALO], kT[:, 0:HALO], self.pcol(P_HM), None, ALU.mult),
                   reads=[kTb, self.par_b], writes=[kTb])
            wq_, wqb = loaded[3 * h + 1]
            bq_, bbq = self.fm_job(wq_, wqb, self.hT, self.hT_b)
            ensure(3 * h + 4)
            qs, qsb = t_ring.next()
            act.op(lambda: nc.scalar.activation(out=qs[:], in_=self.ps[:, bq_ * 512: bq_ * 512 + TB], func=AF.Silu),
                   reads=bbq, writes=[qsb])
            qT, qTb = q_ring.next()
            dve.op(lambda: nc.vector.tensor_tensor(qT[:], qs[:], eq[:], ALU.mult), reads=[qsb, eqb], writes=[qTb])
            wo_, wob = loaded[3 * h + 2]
            bo_, bbo = self.fm_job(wo_, wob, self.hT, self.hT_b)
            ensure(3 * h + 5)
            ogs, ogsb = t_ring.next()
            act.op(lambda: nc.scalar.activation(out=ogs[:], in_=self.ps[:, bo_ * 512: bo_ * 512 + TB], func=AF.Silu),
                   reads=bbo, writes=[ogsb])
            ktok, ktokb = ktok_ring.next()
            b0, bb = self.banks(2)
            for ti, (tlo, ntok) in enumerate(TILES):
                o = self.psb[0:ntok, b0 * 1024 + ti * 128: b0 * 1024 + (ti + 1) * 128]
                pe.op(lambda o=o, tlo=tlo, ntok=ntok: nc.tensor.transpose(o, kT[:, tlo:tlo + ntok], self.ident_bf[:]),
                      reads=[kTb, self.misc_b], writes=bb, signal=(ti == 8))
            act.op(lambda: nc.scalar.activation(out=ktok[0:32, 0, 0, :], in_=self.psb[0:32, b0 * 1024: b0 * 1024 + 128], func=AF.Copy),
                   reads=bb, writes=[ktokb])
            for ab, mcol in ((0, C_MA), (1, C_MB)):
                act.op(lambda ab=ab, mcol=mcol: nc.scalar.activation(
                    out=ktok[:, ab, 1:9, :].rearrange("p a b -> p (a b)"),
                    in_=self.psb[:, b0 * 1024 + 128: b0 * 1024 + 9 * 128], func=AF.Identity,
                    scale=self.cst[:, mcol:mcol + 1]),
                    reads=bb + [self.cst_b], writes=[ktokb])
            at, atb = at_ring.next()
            b0, bba = self.banks(3)
            for ti, (tlo, ntok) in enumerate(TILES):
                pe.op(lambda ti=ti, tlo=tlo, ntok=ntok: nc.tensor.matmul(
                    self.ps[0:ntok, b0 * 512 + ti * 128: b0 * 512 + ti * 128 + ntok],
                    lhsT=kT[:, tlo:tlo + ntok], rhs=qT[:, tlo:tlo + ntok], start=True, stop=True),
                    reads=[kTb, qTb], writes=bba, signal=(ti == 8))
            dve.op(lambda: nc.vector.tensor_tensor(at[0:32, 0, 0:32], self.ps[0:32, b0 * 512: b0 * 512 + 32], self.cmask[0:32, 0:32],
                                                   ALU.mult), reads=bba + [self.cst_b], writes=[atb])
            dve.op(lambda: nc.vector.tensor_tensor(
                at[:, 1:9, :], self.ps[:, b0 * 512 + 128: b0 * 512 + 9 * 128].rearrange("p (a b) -> p a b", a=8),
                self.cmask.unsqueeze(1).to_broadcast([128, 8, 128]), ALU.mult), reads=bba + [self.cst_b], writes=[atb])
            sbf, sbfb = sbf_ring.next()
            sf, sfb = sf_ring.next()
            kvbanks = []
            for g4 in range(4):
                b0k, bbk = self.banks(1)
                kvbanks.append((b0k, bbk))
                for cc in range(4):
                    c = g4 * 4 + cc
                    ti, plo, n, tlo = CHUNKS[c]
                    pe.op(lambda ti=ti, plo=plo, n=n, cc=cc, b0k=b0k: nc.tensor.matmul(
                        self.ps[:, b0k * 512 + cc * 128: b0k * 512 + (cc + 1) * 128],
                        lhsT=ktok[:, (0 if plo == 0 else 1), ti, :], rhs=vbf[:, ti, h * 128:(h + 1) * 128], start=True, stop=True),
                        reads=[ktokb, vbf_b[ti]], writes=bbk, signal=(cc == 3))
            prev = self.Sin[:, h, :]
            prevb = self.Sin_b
            for c in range(16):
                b0k, bbk = kvbanks[c // 4]
                kv = self.ps[:, b0k * 512 + (c % 4) * 128: b0k * 512 + (c % 4 + 1) * 128]
                tlo_end = CHUNKS[c][3] + CHUNKS[c][2] - 1
                eb = eq[:, tlo_end:tlo_end + 1]
                tmpS = sf[:, c % 2, :]
                dve.op(lambda tmpS=tmpS, prev=prev, kv=kv: nc.vector.tensor_tensor(tmpS, prev, kv, ALU.add),
                       reads=bbk + [prevb, sfb], writes=[sfb])
                dve.op(lambda tmpS=tmpS, eb=eb: nc.vector.tensor_scalar(tmpS, tmpS, eb, None, ALU.mult),
                       reads=[sfb, eqb], writes=[sfb])
                act.op(lambda tmpS=tmpS, c=c: nc.scalar.activation(out=sbf[:, c + 1, :], in_=tmpS, func=AF.Copy),
                       reads=[sfb], writes=[sbfb])
                prev = tmpS
                prevb = sfb
            b0o, bbo2 = self.banks(3)
            for ti, (tlo, ntok) in enumerate(TILES):
                oc = b0o * 512 + ti * 128
                pe.op(lambda ti=ti, ntok=ntok, oc=oc: nc.tensor.matmul(
                    self.ps[:, oc: oc + ntok], lhsT=vbf[:, ti, h * 128:(h + 1) * 128], rhs=at[:, ti, 0:ntok],
                    start=True, stop=False, skip_group_check=True), reads=[vbf_b[ti], atb], writes=bbo2, signal=False)
                cs = [c for c in range(17) if CHUNKS[c][0] == ti]
                for ci, c in enumerate(cs):
                    _, plo, n, ctlo = CHUNKS[c]
                    lhs = self.Sin_bf[:, h, :] if c == 0 else sbf[:, c, :]
                    rb = [self.Sinbf_b] if c == 0 else [sbfb]
                    pe.op(lambda lhs=lhs, oc=oc, plo=plo, n=n, ctlo=ctlo: nc.tensor.matmul(
                        self.ps[:, oc + plo: oc + plo + n], lhsT=lhs, rhs=qT[:, ctlo:ctlo + n], start=False,
                        stop=True, skip_group_check=True), reads=rb + [qTb], writes=bbo2,
                        signal=(ti == 8 and ci == len(cs) - 1))
            o_sb, osb = t_ring.next()
            act.op(lambda: nc.scalar.activation(out=o_sb[:, 0:32], in_=self.ps[:, b0o * 512: b0o * 512 + 32], func=AF.Copy),
                   reads=bbo2, writes=[osb])
            act.op(lambda: nc.scalar.activation(out=o_sb[:, 32:TB], in_=self.ps[:, b0o * 512 + 128: b0o * 512 + 9 * 128], func=AF.Copy),
                   reads=bbo2, writes=[osb])
            if h == 0:
                self.dump("o0", o_sb[:], [128, TB], osb)
            act.op(lambda: nc.scalar.activation(out=cum[:], in_=o_sb[:], func=AF.Square), reads=[osb], writes=[cumb])
            bm, bbm = self.stat_mm(self.ones128, cum, cumb, 0, TB)
            dve.op(lambda: nc.vector.tensor_scalar(cum[:], self.ps[:, bm * 512: bm * 512 + TB], RMS_EPS, None, ALU.add),
                   reads=bbm, writes=[cumb])
            act.op(lambda: nc.scalar.activation(out=cum[:], in_=cum[:], func=AF.Sqrt), reads=[cumb], writes=[cumb])
            dve.op(lambda: nc.vector.reciprocal(cum[:], cum[:]), reads=[cumb], writes=[cumb])
            dve.op(lambda: nc.vector.scalar_tensor_tensor(o_sb[:], o_sb[:], self.pcol(P_HG, h), cum[:], ALU.mult, ALU.mult),
                   reads=[osb, cumb, self.par_b], writes=[osb])
            dve.op(lambda: nc.vector.tensor_tensor(self.cat[:, 8 + h, :], o_sb[:], ogs[:], ALU.mult),
                   reads=[osb, ogsb], writes=[self.cat_b[8 + h]])

    def ln_fm(self, Y, Yb, lo, hi, g_off, b_off, write_bf=None, mask_halo=False):
        nc = self.nc
        pe, act, dve = self.pe, self.act, self.dve
        n = hi - lo
        t_ring = self.ring("lnt", [128, TB], F32, 3)
        sq_ring = self.ring("lnq", [128, TB], F32, 2)
        sy, syb = t_ring.next()
        ss, ssb = t_ring.next()
        for m in range(KC):
            sq, sqb = sq_ring.next()
            act.op(lambda m=m: nc.scalar.activation(out=sq[:, lo:hi], in_=Y[:, m, lo:hi], func=AF.Square),
                   reads=[Yb[m]], writes=[sqb])
            if m == 0:
                dve.op(lambda: nc.vector.tensor_copy(sy[:, lo:hi], Y[:, 0, lo:hi]), reads=[Yb[0]], writes=[syb])
                dve.op(lambda: nc.vector.tensor_copy(ss[:, lo:hi], sq[:, lo:hi]), reads=[sqb], writes=[ssb])
            else:
                dve.op(lambda m=m: nc.vector.tensor_tensor(sy[:, lo:hi], sy[:, lo:hi], Y[:, m, lo:hi], ALU.add),
                       reads=[Yb[m], syb], writes=[syb])
                dve.op(lambda: nc.vector.tensor_tensor(ss[:, lo:hi], ss[:, lo:hi], sq[:, lo:hi], ALU.add),
                       reads=[sqb, ssb], writes=[ssb])
        bm, bbm = self.stat_mm(self.onesD, sy, syb, lo, hi)
        bq, bbq = self.stat_mm(self.onesD, ss, ssb, lo, hi)
        mean = self.ps[:, bm * 512 + lo: bm * 512 + hi]
        ex2 = self.ps[:, bq * 512 + lo: bq * 512 + hi]
        rstd, rstdb = self.rstd_from(mean, bbm, ex2, bbq, t_ring, n, LN_EPS)
        dve.op(lambda: nc.vector.scalar_tensor_tensor(sy[:, lo:hi], mean, -1.0, rstd[:, 0:n], ALU.mult, ALU.mult),
               reads=bbm + [rstdb, syb], writes=[syb])
        for m in range(KC):
            dve.op(lambda m=m: nc.vector.tensor_tensor(Y[:, m, lo:hi], Y[:, m, lo:hi], rstd[:, 0:n], ALU.mult),
                   reads=[Yb[m], rstdb], writes=[Yb[m]])
            dve.op(lambda m=m: nc.vector.tensor_tensor(Y[:, m, lo:hi], Y[:, m, lo:hi], sy[:, lo:hi], ALU.add),
                   reads=[Yb[m], syb], writes=[Yb[m]])
            act.op(lambda m=m: nc.scalar.activation(out=Y[:, m, lo:hi], in_=Y[:, m, lo:hi], func=AF.Identity,
                                                    scale=self.pcol(g_off, m), bias=self.pcol(b_off, m)),
                   reads=[Yb[m], self.par_b], writes=[Yb[m]])
            if write_bf is not None:
                wt, wb = write_bf
                dve.op(lambda m=m: nc.vector.tensor_copy(wt[:, m, lo:hi], Y[:, m, lo:hi]), reads=[Yb[m]], writes=[wb[m]])
                if mask_halo:
                    dve.op(lambda m=m: nc.vector.tensor_scalar(wt[:, m, lo:HALO], wt[:, m, lo:HALO], self.pcol(P_HM), None, ALU.mult),
                           reads=[wb[m], self.par_b], writes=[wb[m]])

    def phase_out_ffn(self):
        nc = self.nc
        pe, act, dve = self.pe, self.act, self.dve
        with ExitStack() as esO:
            self.es_cur = esO
            Y = self.sb("Y", [128, KC, TB], F32)
            Yb = [Buf(f"Y{m}") for m in range(KC)]
            wring = self.ring("ow", [128, KC, 128], BF16, 4, dma=True)
            dring = self.ring("od", [128, GC, 128], BF16, 3, dma=True)
            with ExitStack() as e5:
                self.es_cur = e5
                h0_ring = self.ring("oh0", [128, TB], F32, 2, dma=True)
                loaded = {}
                def ensure(m):
                    if m < KC and m not in loaded:
                        loaded[m] = self.load_w(wring, self.w_out[:, m * 128:(m + 1) * 128], 128)
                for m in range(3):
                    ensure(m)
                for s_ in self.h0s_sems:
                    self.sp.wait(s_, s_.count)
                for m in range(KC):
                    ensure(m + 3)
                    h0t, h0b, h0sem = h0_ring.next()
                    self.sp.dma(h0t[:], self.h0s[:, m, :], h0sem, writes=[h0b])
                    w_t, w_b = loaded[m]
                    b0, bb = self.fm_job(w_t, w_b, self.cat, self.cat_b, lo=30)
                    dve.op(lambda m=m, b0=b0, h0t=h0t: nc.vector.scalar_tensor_tensor(
                        Y[:, m, 30:TB], h0t[:, 30:TB], ALPHA, self.ps[:, b0 * 512 + 30: b0 * 512 + TB], ALU.mult, ALU.add),
                        reads=bb + [h0b], writes=[Yb[m]])
                self.dump("y1", Y[:], [128, KC, TB], Yb[KC - 1])
                self.barrier()
            self.es_cur = esO
            with ExitStack() as e6:
                self.es_cur = e6
                self.ln_fm(Y, Yb, 30, TB, P_G1, P_B1, write_bf=(self.hT, self.hT_b), mask_halo=True)
                self.dump("h1", Y[:], [128, KC, TB], Yb[KC - 1])
                self.barrier()
            self.es_cur = esO
            if self.stop_after == "mixer":
                self.es_cur = self.es
                return
            actb = self.cat[:, :, :].rearrange("p a b -> p (a b)")
            act_bufs = [Buf(f"act{j}") for j in range(GC)]
            with ExitStack() as e7:
                self.es_cur = e7
                t_ring = self.ring("ft", [128, TB], F32, 4)
                jobs = [(g, jj) for g in range(NG) for jj in range(GC)]
                upl = {}
                def ensure_up(idx):
                    if idx < len(jobs) and idx not in upl:
                        j = idx
                        wg = self.load_w(wring, self.w_up[:, j * 128:(j + 1) * 128], 128)
                        wv = self.load_w(wring, self.w_up[:, DFF + j * 128: DFF + (j + 1) * 128], 128)
                        upl[idx] = (wg, wv)
                dnl = {}
                def ensure_dn(g, m):
                    if g < NG and m < KC and (g, m) not in dnl:
                        dnl[(g, m)] = self.load_w(dring, self.w_down[g * GC * 128:(g + 1) * GC * 128, m * 128:(m + 1) * 128], 128, nk=GC)
                ensure_up(0)
                for g in range(NG):
                    for jj in range(GC):
                        idx = g * GC + jj
                        j = idx
                        ensure_up(idx + 1)
                        if jj == GC - 1:
                            ensure_dn(g, 0)
                            ensure_dn(g, 1)
                        (wg, wgb), (wv, wvb) = upl.pop(idx)
                        bg, bbg = self.fm_job(wg, wgb, self.hT, self.hT_b, lo=30)
                        bv, bbv = self.fm_job(wv, wvb, self.hT, self.hT_b, lo=HALO)
                        gs, gsb = t_ring.next()
                        act.op(lambda bg=bg: nc.scalar.activation(out=gs[:, 30:TB], in_=self.ps[:, bg * 512 + 30: bg * 512 + TB], func=AF.Copy),
                               reads=bbg, writes=[gsb])
                        c, cb = t_ring.next()
                        fw = lambda k, j=j: self.par[:, P_FW + j * 3 + k: P_FW + j * 3 + k + 1]
                        dve.op(lambda: nc.vector.tensor_scalar(c[:, 0:T], gs[:, 32:TB], fw(2), self.pcol(P_FB, j), ALU.mult, ALU.add),
                               reads=[gsb, self.par_b], writes=[cb])
                        dve.op(lambda: nc.vector.scalar_tensor_tensor(c[:, 0:T], gs[:, 31:TB - 1], fw(1), c[:, 0:T], ALU.mult, ALU.add),
                               reads=[gsb, cb, self.par_b], writes=[cb])
                        dve.op(lambda: nc.vector.scalar_tensor_tensor(c[:, 0:T], gs[:, 30:TB - 2], fw(0), c[:, 0:T], ALU.mult, ALU.add),
                               reads=[gsb, cb, self.par_b], writes=[cb])
                        act.op(lambda: nc.scalar.activation(out=c[:, 0:T], in_=c[:, 0:T], func=AF.Silu), reads=[cb], writes=[cb])
                        dve.op(lambda jj=jj, bv=bv: nc.vector.tensor_tensor(actb[:, jj * T:(jj + 1) * T], c[:, 0:T],
                                                                            self.ps[:, bv * 512 + HALO: bv * 512 + TB], ALU.mult),
                               reads=bbv + [cb], writes=[act_bufs[jj]])
                    for m in range(KC):
                        ensure_dn(g, m + 2)
                        w_t, w_b = dnl.pop((g, m))
                        b0, bb = self.banks(2)
                        for half in range(2):
                            for jj in range(GC):
                                pe.op(lambda half=half, jj=jj, m=m, w_t=w_t, b0=b0: nc.tensor.matmul(
                                    self.ps[:, (b0 + half) * 512:(b0 + half + 1) * 512], lhsT=w_t[:, jj, :],
                                    rhs=actb[:, jj * T + half * 512: jj * T + (half + 1) * 512],
                                    start=(jj == 0), stop=(jj == GC - 1)),
                                    reads=[act_bufs[jj], w_b], writes=bb, signal=(half == 1 and jj == GC - 1))
                        if g == 0:
                            dve.op(lambda m=m, b0=b0: nc.vector.scalar_tensor_tensor(
                                Y[:, m, HALO:TB], Y[:, m, HALO:TB], ALPHA, self.ps[:, b0 * 512:(b0 + 2) * 512], ALU.mult, ALU.add),
                                reads=bb + [Yb[m]], writes=[Yb[m]])
                        else:
                            dve.op(lambda m=m, b0=b0: nc.vector.tensor_tensor(
                                Y[:, m, HALO:TB], Y[:, m, HALO:TB], self.ps[:, b0 * 512:(b0 + 2) * 512], ALU.add),
                                reads=bb + [Yb[m]], writes=[Yb[m]])
                self.barrier()
            self.es_cur = esO
            with ExitStack() as e8:
                self.es_cur = e8
                self.ln_fm(Y, Yb, HALO, TB, P_G2, P_B2)
                self.barrier()
            self.es_cur = esO
            with ExitStack() as e9:
                self.es_cur = e9
                o_ring = self.ring("oo", [128, D], F32, 2, dma=True)
                for ti in range(8):
                    ot, otb, osem = o_ring.next()
                    for q in range(4):
                        b0, bb = self.banks(1)
                        for kk in range(4):
                            kc = q * 4 + kk
                            pe.op(lambda kc=kc, kk=kk, b0=b0, ti=ti: nc.tensor.transpose(
                                self.ps[:, b0 * 512 + kk * 128: b0 * 512 + (kk + 1) * 128],
                                Y[:, kc, HALO + ti * 128: HALO + (ti + 1) * 128], self.ident),
                                reads=[Yb[kc], self.cst_b], writes=bb, signal=(kk == 3))
                        if q % 2 == 0:
                            act.op(lambda q=q, b0=b0, ot=ot: nc.scalar.activation(out=ot[:, q * 512:(q + 1) * 512],
                                                                                  in_=self.ps[:, b0 * 512:(b0 + 1) * 512], func=AF.Copy),
                                   reads=bb, writes=[otb])
                        else:
                            dve.op(lambda q=q, b0=b0, ot=ot: nc.vector.tensor_copy(ot[:, q * 512:(q + 1) * 512], self.ps[:, b0 * 512:(b0 + 1) * 512]),
                                   reads=bb, writes=[otb])
                    self.sp.dma(self.out_d[ti * 128:(ti + 1) * 128, :], ot[:], osem, reads=[otb])
                self.final_waits.extend(o_ring.sems)
                self.barrier()
            self.es_cur = esO
        self.es_cur = self.es


def _pack_params(inp, hmask):
    P = np.zeros((128, NPP), np.float32)
    fm = lambda v, n: np.ascontiguousarray(np.asarray(v, np.float32).reshape(n, 128).T)
    P[:, P_G0:P_G0 + 16] = fm(inp["emb_ln_g"], 16)
    P[:, P_B0:P_B0 + 16] = fm(inp["emb_ln_b"], 16)
    cw = np.asarray(inp["conv_w"], np.float32)[0]
    P[:, P_CW:P_CW + 248] = cw.reshape(CK, 8, 128).transpose(2, 1, 0).reshape(128, 248)
    P[:, P_CB:P_CB + 8] = fm(inp["conv_b"][0], 8)
    P[:, P_CNG:P_CNG + 8] = fm(inp["conv_norm_g"][0], 8)
    P[:, P_CNB:P_CNB + 8] = fm(inp["conv_norm_b"][0], 8)
    lbl = np.asarray(inp["lb_logits"], np.float32)
    P[:, P_LBL:P_LBL + 16] = lbl.reshape(2, 8, 128).transpose(2, 1, 0).reshape(128, 16)
    P[:, P_HG:P_HG + 8] = fm(inp["hgrn_norm_g"][0], 8)
    P[:, P_G1:P_G1 + 16] = fm(inp["ln1_g"][0], 16)
    P[:, P_B1:P_B1 + 16] = fm(inp["ln1_b"][0], 16)
    fw = np.asarray(inp["ffn_conv_w"], np.float32)[0]
    P[:, P_FW:P_FW + 132] = fw.reshape(3, FC, 128).transpose(2, 1, 0).reshape(128, 132)
    P[:, P_FB:P_FB + FC] = fm(inp["ffn_conv_b"][0], FC)
    P[:, P_G2:P_G2 + 16] = fm(inp["ln2_g"][0], 16)
    P[:, P_B2:P_B2 + 16] = fm(inp["ln2_b"][0], 16)
    P[:, P_HM] = hmask
    return P


def _consts():
    C = np.zeros((128, NCC), np.float32)
    C[:, C_ID:C_ID + 128] = np.eye(128, dtype=np.float32)
    s = np.arange(128)[:, None]
    t = np.arange(128)[None, :]
    C[:, C_CM:C_CM + 128] = ((s // 64 == t // 64) & (s <= t)).astype(np.float32)
    rm = np.ones(TB, np.float32)
    for (_, _, _, tlo) in CHUNKS:
        rm[tlo] = 0.0
    C[:, C_RM:C_RM + TB] = rm[None, :]
    C[0:64, C_MA] = 1.0
    C[64:128, C_MB] = 1.0
    return C


_CACHE = {}


def make_in_maps(inputs, cores=range(8)):
    x = np.asarray(inputs["x"], np.float32)
    w_in = np.ascontiguousarray(np.asarray(inputs["w_in"], np.float32)[0])
    w_out = np.ascontiguousarray(np.asarray(inputs["w_out"], np.float32)[0])
    w_up = np.ascontiguousarray(np.asarray(inputs["w_ffn_up"], np.float32)[0])
    w_down = np.ascontiguousarray(np.asarray(inputs["w_ffn_down"], np.float32)[0])
    consts = _consts()
    maps = []
    for c in cores:
        b, p = c // 4, c % 4
        t0 = p * T
        xo = np.zeros((TB, D), np.float32)
        lo = t0 - HALO
        if lo >= 0:
            xo[:] = x[b, lo:t0 + T]
        else:
            xo[HALO:] = x[b, 0:T]
        xp = np.zeros((NPRE, D), np.float32)
        pm = np.zeros(NPRE, np.float32)
        end = t0 - HALO
        nval = max(0, end)
        if nval > 0:
            xp[NPRE - nval:] = x[b, 0:end]
            pm[NPRE - nval:] = 1.0
        pmask = np.ascontiguousarray(pm.reshape(NPRE // 128, 128).T)
        maps.append({
            "x_own": xo, "x_pre": xp, "pmask": pmask,
            "params": _pack_params(inputs, 0.0 if p == 0 else 1.0), "consts": consts,
            "w_in": w_in, "w_out": w_out, "w_up": w_up, "w_down": w_down,
        })
    return maps


def kernel(**inputs):
    if "nc" not in _CACHE:
        _CACHE["nc"] = KB().build()
    nc = _CACHE["nc"]
    maps = make_in_maps(inputs)
    res = run_bass_kernel_spmd(nc, maps, core_ids=list(range(8)))
    out = np.zeros((2, 4 * T, D), np.float32)
    for c in range(8):
        b, p = c // 4, c % 4
        out[b, p * T:(p + 1) * T] = res.results[c]["out"]
    return out
```

```python
import numpy as np
from contextlib import ExitStack
import concourse.bass as bass
import concourse.mybir as mybir
from concourse.bass_utils import run_bass_kernel_spmd

F32 = mybir.dt.float32
BF16 = mybir.dt.bfloat16
AF = mybir.ActivationFunctionType
ALU = mybir.AluOpType

D = 2048
KC = 16
T = 1024
HALO = 32
TB = T + HALO
NPRE = 3072
DFF = 5632
FC = 44
NG = 4
GC = FC // NG
ALPHA = 2.0 ** 0.25
LN_EPS = 1e-5
RMS_EPS = 1e-6
CK = 31

P_G0, P_B0, P_CW, P_CB, P_CNG, P_CNB, P_LBL, P_HG = 0, 16, 32, 280, 288, 296, 304, 320
P_G1, P_B1, P_FW, P_FB, P_G2, P_B2, P_HM = 328, 344, 360, 492, 536, 552, 568
NPP = 576
C_ID, C_CM, C_RM = 0, 128, 256
C_MA = 256 + TB
C_MB = C_MA + 1
NCC = 256 + TB + 2

BLOCKS = [(0, 512), (512, 1024), (1024, TB)]
TILES = [(0, 32)] + [(32 + 128 * i, 128) for i in range(8)]
CHUNKS = [(0, 0, 32, 0)]
for _i in range(8):
    CHUNKS.append((_i + 1, 0, 64, 32 + 128 * _i))
    CHUNKS.append((_i + 1, 64, 64, 32 + 128 * _i + 64))


import os
SAME_ENGINE_WAITS = os.environ.get("SEW", "1") == "1"


class Sem:
    def __init__(self, nc, es, name):
        self.h = es.enter_context(nc.semaphore(name))
        self.count = 0
        self.name = name


class Buf:
    __slots__ = ("name", "w", "r", "excl")

    def __init__(self, name, excl=False):
        self.name = name
        self.w = {}
        self.r = {}
        self.excl = excl


def _merge(d, src):
    for k, v in src.items():
        if d.get(k, 0) < v:
            d[k] = v


def split_buf(buf, n):
    parts = []
    for i in range(n):
        p = Buf(f"{buf.name}.{i}", buf.excl)
        p.w = dict(buf.w)
        p.r = dict(buf.r)
        parts.append(p)
    return parts


def join_buf(buf, parts):
    buf.w = {}
    buf.r = {}
    for p in parts:
        _merge(buf.w, p.w)
        _merge(buf.r, p.r)


class Eng:
    def __init__(self, kb, name, beng, is_pe=False):
        self.kb = kb
        self.name = name
        self.e = beng
        self.sem = Sem(kb.nc, kb.es, "s_" + name)
        self.known = {}
        self.is_pe = is_pe
        self.pending = False

    def wait(self, sem, val):
        if val <= 0 or self.known.get(sem, 0) >= val:
            return
        self.e.wait_ge(sem.h, val)
        self.known[sem] = val

    def _deps(self, reads, writes):
        deps = {}
        for b in reads:
            _merge(deps, b.w)
            if b.excl:
                for k, v in b.r.items():
                    if k is not self.sem and deps.get(k, 0) < v:
                        deps[k] = v
        for b in writes:
            _merge(deps, b.w)
            _merge(deps, b.r)
        for sem, val in deps.items():
            if sem is self.sem:
                if self.is_pe or not SAME_ENGINE_WAITS:
                    continue
                if val > sem.count:
                    continue
            self.wait(sem, val)

    def op(self, fn, reads=(), writes=(), signal=True):
        self._deps(reads, writes)
        inst = fn()
        val = self.sem.count + 1
        if signal:
            self.sem.count = val
            inst.then_inc(self.sem.h, 1)
            self.pending = False
        else:
            self.pending = True
        for b in reads:
            if b.r.get(self.sem, 0) < val:
                b.r[self.sem] = val
        for b in writes:
            b.w = {self.sem: val}
            b.r = {}
        return inst

    def dma(self, out, in_, sem, reads=(), writes=()):
        self._deps(reads, writes)
        self.wait(sem, sem.count)
        inst = self.e.dma_start(out=out, in_=in_)
        sem.count += 16
        inst.then_inc(sem.h, 16)
        for b in reads:
            b.r[sem] = sem.count
        for b in writes:
            b.w = {sem: sem.count}
            b.r = {}
        return inst


class Ring:
    def __init__(self, kb, name, shape, dtype, n, dma=False):
        self.tiles = [kb.sb(f"{name}{i}", shape, dtype) for i in range(n)]
        self.bufs = [Buf(f"{name}{i}") for i in range(n)]
        self.sems = [Sem(kb.nc, kb.es, f"d_{name}{i}") for i in range(n)] if dma else None
        self.i = 0
        self.n = n

    def next(self):
        k = self.i % self.n
        self.i += 1
        if self.sems:
            return self.tiles[k], self.bufs[k], self.sems[k]
        return self.tiles[k], self.bufs[k]


class KB:
    def __init__(self, dbg=()):
        self.dbg = list(dbg)
        self.dbg_out = {}

    def sb(self, name, shape, dtype):
        self._uid = getattr(self, "_uid", 0) + 1
        return self.es_cur.enter_context(self.nc.sbuf_tensor(f"{name}_{self._uid}", list(shape), dtype))

    def banks(self, n):
        if self.bank_ptr + n > 8:
            self.bank_ptr = 0
        b0 = self.bank_ptr
        self.bank_ptr = (self.bank_ptr + n) % 8
        return b0, self.bank_bufs[b0:b0 + n]

    def barrier(self):
        sems = [e.sem for e in self.engs] + self.all_dma_sems
        for e in self.engs:
            assert not e.pending, e.name
            for s in sems:
                if s is e.sem:
                    continue
                e.wait(s, s.count)

    def dsem(self, name):
        s = Sem(self.nc, self.es, name)
        self.all_dma_sems.append(s)
        return s

    def ring(self, name, shape, dtype, n, dma=False):
        r = Ring(self, name, shape, dtype, n, dma)
        if dma:
            self.all_dma_sems.extend(r.sems)
        return r

    def dump(self, name, tile_ap, shape, buf):
        if name not in self.dbg:
            return
        dt = tile_ap.dtype
        o = self.nc.dram_tensor("dbg_" + name, list(shape), dt, kind="ExternalOutput").ap()
        self.dbg_out[name] = (list(shape), dt)
        s = self.dsem("dbg_" + name)
        idx = tuple(slice(None) for _ in shape)
        self.sp.dma(o[idx], tile_ap, s, reads=[buf])
        self.final_waits.append(s)

    def build(self, stop_after=None):
        nc = bass.Bass("TRN2", target_bir_lowering=False)
        self.nc = nc
        self.stop_after = stop_after
        dr = lambda name, shape, kind="ExternalInput", dt=F32: nc.dram_tensor(name, list(shape), dt, kind=kind).ap()
        self.x_own = dr("x_own", [TB, D])
        self.x_pre = dr("x_pre", [NPRE, D])
        self.pmask_d = dr("pmask", [128, NPRE // 128])
        self.params_d = dr("params", [128, NPP])
        self.consts_d = dr("consts", [128, NCC])
        self.w_in = dr("w_in", [D, 6144])
        self.w_out = dr("w_out", [D, D])
        self.w_up = dr("w_up", [D, 2 * DFF])
        self.w_down = dr("w_down", [DFF, D])
        self.out_d = dr("out", [T, D], kind="ExternalOutput")
        self.h0s = nc.dram_tensor("h0s", [128, KC, TB], F32).ap()
        self.final_waits = []
        self.all_dma_sems = []
        with ExitStack() as es:
            self.es = es
            self.es_cur = es
            self.pe = Eng(self, "pe", nc.tensor, is_pe=True)
            self.act = Eng(self, "act", nc.scalar)
            self.dve = Eng(self, "dve", nc.vector)
            self.gp = Eng(self, "gp", nc.gpsimd)
            self.sp = Eng(self, "sp", nc.sync)
            self.engs = [self.pe, self.act, self.dve, self.gp, self.sp]
            self.ps = es.enter_context(nc.psum_tensor("ps", [128, 8 * 512], F32))
            self.psb = self.ps[:, :].bitcast(BF16)
            self.bank_bufs = [Buf(f"bank{i}", excl=True) for i in range(8)]
            self.bank_ptr = 0
            self.setup()
            if stop_after != "setup":
                self.run_phases()
            for s in self.final_waits:
                self.sp.wait(s, s.count)
            for e in self.engs:
                assert not e.pending, e.name
        return nc

    def setup(self):
        nc = self.nc
        self.par = self.sb("par", [128, NPP], F32)
        self.par_b = Buf("par")
        self.cst = self.sb("cst", [128, NCC], F32)
        self.cst_b = Buf("cst")
        self.pm = self.sb("pm", [128, NPRE // 128], F32)
        s = self.dsem("ld_setup")
        self.sp.dma(self.par[:], self.params_d[:, :], s, writes=[self.par_b])
        self.sp.dma(self.cst[:], self.consts_d[:, :], s, writes=[self.cst_b])
        self.sp.dma(self.pm[:], self.pmask_d[:, :], s, writes=[self.par_b])
        self.ident = self.cst[:, C_ID:C_ID + 128]
        self.cmask = self.cst[:, C_CM:C_CM + 128]
        self.rmask = self.cst[:, C_RM:C_RM + TB]
        self.ident_bf = self.sb("ident_bf", [128, 128], BF16)
        self.ones128 = self.sb("ones128", [128, 128], F32)
        self.onesD = self.sb("onesD", [128, 128], F32)
        self.ones512 = self.sb("ones512", [128, 512], F32)
        self.lbv = self.sb("lbv", [128, 8], F32)
        self.oml = self.sb("oml", [128, 8], F32)
        self.noml = self.sb("noml", [128, 8], F32)
        self.misc_b = Buf("misc")
        self.Sin = self.sb("Sin", [128, 8, 128], F32)
        self.Sin_b = Buf("Sin")
        self.Sin_bf = self.sb("Sin_bf", [128, 8, 128], BF16)
        self.Sinbf_b = Buf("Sin_bf")
        d = self.dve
        d.op(lambda: nc.vector.tensor_copy(self.ident_bf[:], self.ident), reads=[self.cst_b], writes=[self.misc_b])
        d.op(lambda: nc.vector.memset(self.ones128[:], 1.0 / 128.0), writes=[self.misc_b])
        d.op(lambda: nc.vector.memset(self.onesD[:], 1.0 / D), writes=[self.misc_b])
        d.op(lambda: nc.vector.memset(self.ones512[:], 1.0), writes=[self.misc_b])
        d.op(lambda: nc.vector.memset(self.Sin[:], 0.0), writes=[self.Sin_b])
        lbl = self.par[:, P_LBL:P_LBL + 16].rearrange("p (j two) -> p j two", two=2)
        d.op(lambda: nc.vector.tensor_tensor(self.noml[:], lbl[:, :, 0], lbl[:, :, 1], ALU.subtract),
             reads=[self.par_b], writes=[self.misc_b])
        self.act.op(lambda: nc.scalar.activation(out=self.lbv[:], in_=self.noml[:], func=AF.Sigmoid),
                    reads=[self.misc_b], writes=[self.misc_b])
        d.op(lambda: nc.vector.tensor_scalar(self.oml[:], self.lbv[:], -1.0, 1.0, ALU.mult, ALU.add),
             reads=[self.misc_b], writes=[self.misc_b])
        d.op(lambda: nc.vector.tensor_scalar(self.noml[:], self.oml[:], -1.0, None, ALU.mult),
             reads=[self.misc_b], writes=[self.misc_b])

    def pcol(self, off, j=0):
        return self.par[:, off + j:off + j + 1]

    def load_w(self, ring, dram_ap_2d, ncols, nk=KC):
        t, b, s = ring.next()
        self.gp.dma(t[:, 0:nk, 0:ncols], dram_ap_2d.rearrange("(kc p) c -> p kc c", p=128), s, writes=[b])
        return t, b

    def ln0_stats(self, tiles):
        nc = self.nc
        for (xr, ntok, xt, xb, xs, st, stb, mv, mvb) in tiles:
            self.sp.dma(xt[0:ntok, :], xr, xs, writes=[xb])
        for (xr, ntok, xt, xb, xs, st, stb, mv, mvb) in tiles:
            for q in range(4):
                self.dve.op(lambda q=q, st=st, xt=xt, ntok=ntok: nc.vector.bn_stats(st[0:ntok, q, :], xt[0:ntok, q * 512:(q + 1) * 512]),
                            reads=[xb], writes=[stb])
        for (xr, ntok, xt, xb, xs, st, stb, mv, mvb) in tiles:
            self.dve.op(lambda st=st, mv=mv, ntok=ntok: nc.vector.bn_aggr(mv[0:ntok, 0:2], st[0:ntok, :, :].rearrange("p a b -> p (a b)")),
                        reads=[stb], writes=[mvb])
        for (xr, ntok, xt, xb, xs, st, stb, mv, mvb) in tiles:
            self.dve.op(lambda mv=mv, ntok=ntok: nc.vector.tensor_scalar(mv[0:ntok, 2:3], mv[0:ntok, 1:2], LN_EPS, None, ALU.add),
                        reads=[mvb], writes=[mvb])
        for (xr, ntok, xt, xb, xs, st, stb, mv, mvb) in tiles:
            self.act.op(lambda mv=mv, ntok=ntok: nc.scalar.activation(out=mv[0:ntok, 3:4], in_=mv[0:ntok, 2:3], func=AF.Sqrt),
                        reads=[mvb], writes=[mvb])
        for (xr, ntok, xt, xb, xs, st, stb, mv, mvb) in tiles:
            self.dve.op(lambda mv=mv, ntok=ntok: nc.vector.reciprocal(mv[0:ntok, 4:5], mv[0:ntok, 3:4]), reads=[mvb], writes=[mvb])
        for (xr, ntok, xt, xb, xs, st, stb, mv, mvb) in tiles:
            self.dve.op(lambda mv=mv, ntok=ntok: nc.vector.tensor_scalar(mv[0:ntok, 5:6], mv[0:ntok, 0:1], -1.0, mv[0:ntok, 4:5],
                                                                         ALU.mult, ALU.mult), reads=[mvb], writes=[mvb])

    def run_phases(self):
        self.phase_prefix()
        if self.stop_after == "prefix":
            return
        self.phase_mixer()

    def phase_prefix(self):
        nc = self.nc
        pe, act, dve = self.pe, self.act, self.dve
        with ExitStack() as es:
            self.es_cur = es
            wring = self.ring("pw", [128, KC, 512], BF16, 4, dma=True)
            wts = []
            for c0 in (3072, 3072 + 512, 4096, 4096 + 512):
                wts.append(self.load_w(wring, self.w_in[:, c0:c0 + 512], 512))
            xt_ring = self.ring("pxt", [128, D], F32, 4, dma=True)
            st_ring = self.ring("pst", [128, 4, 6], F32, 4)
            mv_ring = self.ring("pmv", [128, 8], F32, 4)
            xn = self.sb("pxn", [128, 4, D], BF16)
            xnb = Buf("pxn")
            h0p = self.sb("ph0", [128, KC, 512], BF16)
            h0b = [Buf(f"ph0_{k}") for k in range(KC)]
            NF = 4
            tsig = self.ring("psg", [128, 512], F32, NF)
            ts2 = self.ring("ps2", [128, 512], F32, NF)
            tlf = self.ring("plf", [128, 512], F32, NF)
            tcb = self.ring("pcb", [128, 512], F32, NF)
            wT_ring = self.ring("pwT", [128, 512], BF16, 2 * NF)
            wk = self.sb("pwk", [128, 4, 1024], BF16)
            wkb = Buf("pwk")
            vt = self.sb("pv", [128, 4, 1024], BF16)
            vb = Buf("pv")
            carry = self.sb("pcarry", [128, 8], F32)
            carry_b = Buf("carry")
            dve.op(lambda: nc.vector.memset(carry[:], 0.0), writes=[carry_b])
            order = list(range(NPRE // 512 - 1, -1, -1))

            def ln0_block(pb):
                tiles = []
                for tt in range(4):
                    g = pb * 4 + tt
                    xt, xb, xs = xt_ring.next()
                    st, stb = st_ring.next()
                    mv, mvb = mv_ring.next()
                    tiles.append((self.x_pre[g * 128:(g + 1) * 128, :], 128, xt, xb, xs, st, stb, mv, mvb))
                self.ln0_stats(tiles)
                for tt, (xr, ntok, xt, xb, xs, st, stb, mv, mvb) in enumerate(tiles):
                    act.op(lambda tt=tt, xt=xt, mv=mv: nc.scalar.activation(out=xn[:, tt, :], in_=xt[:, :], func=AF.Identity,
                                                                            scale=mv[:, 4:5], bias=mv[:, 5:6]),
                           reads=[xb, mvb], writes=[xnb])

            ln0_block(order[0])
            for bi, pb in enumerate(order):
                for kg in range(4):
                    b0, bb = self.banks(2)
                    for kk in range(4):
                        kc = kg * 4 + kk
                        for tt in range(4):
                            last = (kk == 3 and tt == 3)
                            o = self.psb[:, b0 * 1024 + kk * 512 + tt * 128: b0 * 1024 + kk * 512 + (tt + 1) * 128]
                            pe.op(lambda o=o, kc=kc, tt=tt: nc.tensor.transpose(o, xn[:, tt, kc * 128:(kc + 1) * 128],
                                                                                 self.ident_bf[:]),
                                  reads=[xnb, self.misc_b], writes=bb, signal=last)
                    for kk in range(4):
                        kc = kg * 4 + kk
                        src = self.psb[:, b0 * 1024 + kk * 512: b0 * 1024 + (kk + 1) * 512]
                        act.op(lambda src=src, kc=kc: nc.scalar.activation(
                            out=h0p[:, kc, :], in_=src, func=AF.Identity,
                            scale=self.pcol(P_G0, kc), bias=self.pcol(P_B0, kc)),
                            reads=bb + [self.par_b], writes=[h0b[kc]])
                if bi + 1 < len(order):
                    ln0_block(order[bi + 1])

                def v_tiles(tts):
                    for tt in tts:
                        b0, bb = self.banks(2)
                        for half in range(2):
                            w_t, w_b = wts[2 + half]
                            for kc in range(KC):
                                pe.op(lambda half=half, kc=kc, w_t=w_t, tt=tt, b0=b0: nc.tensor.matmul(
                                    self.ps[:, (b0 + half) * 512:(b0 + half + 1) * 512],
                                    lhsT=h0p[:, kc, tt * 128:(tt + 1) * 128], rhs=w_t[:, kc, :],
                                    start=(kc == 0), stop=(kc == KC - 1)),
                                    reads=[h0b[kc], w_b], writes=bb, signal=(half == 1 and kc == KC - 1))
                        dve.op(lambda tt=tt, b0=b0: nc.vector.tensor_copy(vt[:, tt, :], self.ps[:, b0 * 512:(b0 + 2) * 512]),
                               reads=bb, writes=[vb])

                pend_tr = []
                for fg in range(2):
                    js = list(range(fg * NF, (fg + 1) * NF))
                    fb = {}
                    for j in js:
                        w_t, w_b = wts[j // 4]
                        b0, bb = self.banks(1)
                        fb[j] = (b0, bb)
                        for kc in range(KC):
                            pe.op(lambda kc=kc, w_t=w_t, j=j, b0=b0: nc.tensor.matmul(
                                self.ps[:, b0 * 512:(b0 + 1) * 512],
                                lhsT=w_t[:, kc, (j % 4) * 128:(j % 4 + 1) * 128], rhs=h0p[:, kc, :],
                                start=(kc == 0), stop=(kc == KC - 1)),
                                reads=[h0b[kc], w_b], writes=bb, signal=(kc == KC - 1))
                    T_ = {}
                    for j in js:
                        T_[j] = (tsig.next(), ts2.next(), tlf.next(), tcb.next(), wT_ring.next())
                    for j in js:
                        (sig, sigb) = T_[j][0]
                        b0, bb = fb[j]
                        act.op(lambda b0=b0, sig=sig: nc.scalar.activation(out=sig[:], in_=self.ps[:, b0 * 512:(b0 + 1) * 512],
                                                                           func=AF.Sigmoid), reads=bb, writes=[sigb])
                    for j in js:
                        (s2, s2b) = T_[j][1]
                        b0, bb = fb[j]
                        act.op(lambda b0=b0, s2=s2: nc.scalar.activation(out=s2[:], in_=self.ps[:, b0 * 512:(b0 + 1) * 512],
                                                                         func=AF.Sigmoid, scale=-1.0), reads=bb, writes=[s2b])
                    v_tiles([2 * fg, 2 * fg + 1])
                    for j in js:
                        (sig, sigb), _, (lf, lfb) = T_[j][0], None, T_[j][2]
                        act.op(lambda j=j, sig=sig, lf=lf: nc.scalar.activation(out=lf[:], in_=sig[:], func=AF.Ln,
                                                                                scale=self.oml[:, j:j + 1], bias=self.lbv[:, j:j + 1]),
                               reads=[sigb, self.misc_b], writes=[lfb])
                    for j in js:
                        (lf, lfb), (cb_, cbb) = T_[j][2], T_[j][3]
                        dve.op(lambda lf=lf, cb_=cb_: nc.vector.tensor_tensor_scan(cb_[:], self.ones512[:], lf[:], 0.0, ALU.mult, ALU.add),
                               reads=[lfb, self.misc_b], writes=[cbb])
                    for j in js:
                        (cb_, cbb) = T_[j][3]
                        dve.op(lambda j=j, cb_=cb_: nc.vector.tensor_tensor(carry[:, j:j + 1], carry[:, j:j + 1], cb_[:, 511:512], ALU.add),
                               reads=[cbb, carry_b], writes=[carry_b])
                    for j in js:
                        (lf, lfb), (cb_, cbb) = T_[j][2], T_[j][3]
                        act.op(lambda j=j, lf=lf, cb_=cb_: nc.scalar.activation(out=lf[:], in_=cb_[:], func=AF.Exp, scale=-1.0,
                                                                                bias=carry[:, j:j + 1]),
                               reads=[cbb, carry_b], writes=[lfb])
                    for j in js:
                        (s2, s2b), (lf, lfb), (wT, wTb) = T_[j][1], T_[j][2], T_[j][4]
                        dve.op(lambda j=j, s2=s2, lf=lf, wT=wT: nc.vector.scalar_tensor_tensor(wT[:], s2[:], self.oml[:, j:j + 1], lf[:],
                                                                                               ALU.mult, ALU.mult),
                               reads=[s2b, lfb, self.misc_b], writes=[wTb])
                    for j in js:
                        pend_tr.append((j, T_[j][4]))
                for (j, (wT, wTb)) in pend_tr:
                    b1, bb1 = self.banks(1)
                    for tt in range(4):
                        o = self.psb[:, b1 * 1024 + tt * 128: b1 * 1024 + (tt + 1) * 128]
                        pe.op(lambda o=o, tt=tt, wT=wT: nc.tensor.transpose(o, wT[:, tt * 128:(tt + 1) * 128], self.ident_bf[:]),
                              reads=[wTb, self.misc_b], writes=bb1, signal=(tt == 3))
                    src = self.psb[:, b1 * 1024: b1 * 1024 + 512].rearrange("p (a b) -> p a b", a=4)
                    msk = self.pm[:, pb * 4:pb * 4 + 4].unsqueeze(2).to_broadcast([128, 4, 128])
                    dve.op(lambda src=src, msk=msk, j=j: nc.vector.tensor_tensor(wk[:, :, j * 128:(j + 1) * 128], src, msk, ALU.mult),
                           reads=bb1 + [self.par_b], writes=[wkb])
                for hb in range(2):
                    b0, bb = self.banks(1)
                    for hh in range(4):
                        h = hb * 4 + hh
                        for tt in range(4):
                            pe.op(lambda h=h, hh=hh, tt=tt, b0=b0: nc.tensor.matmul(
                                self.ps[:, b0 * 512 + hh * 128: b0 * 512 + (hh + 1) * 128],
                                lhsT=wk[:, tt, h * 128:(h + 1) * 128], rhs=vt[:, tt, h * 128:(h + 1) * 128],
                                start=(tt == 0), stop=(tt == 3)),
                                reads=[wkb, vb], writes=bb, signal=(hh == 3 and tt == 3))
                    sl = self.Sin[:, hb * 4:(hb + 1) * 4, :].rearrange("p a b -> p (a b)")
                    dve.op(lambda sl=sl, b0=b0: nc.vector.tensor_tensor(sl, sl, self.ps[:, b0 * 512:(b0 + 1) * 512], ALU.add),
                           reads=bb + [self.Sin_b], writes=[self.Sin_b])
            dve.op(lambda: nc.vector.tensor_copy(self.Sin_bf[:], self.Sin[:]), reads=[self.Sin_b], writes=[self.Sinbf_b])
            self.dump("Sin", self.Sin[:], [128, 8, 128], self.Sin_b)
            self.barrier()
        self.es_cur = self.es

    def fm_job(self, w_t, w_b, src, src_bufs, lo=0, hi=TB, ncol=128, wcol0=0):
        nc = self.nc
        b0, bb = self.banks(3)
        blks = [(max(a, lo), min(b, hi)) for (a, b) in BLOCKS if max(a, lo) < min(b, hi)]
        for bi, (a, b) in enumerate(blks):
            for kc in range(KC):
                self.pe.op(lambda a=a, b=b, kc=kc: nc.tensor.matmul(
                    self.ps[0:ncol, b0 * 512 + a: b0 * 512 + b], lhsT=w_t[:, kc, wcol0:wcol0 + ncol], rhs=src[:, kc, a:b],
                    start=(kc == 0), stop=(kc == KC - 1)),
                    reads=src_bufs + [w_b], writes=bb, signal=(bi == len(blks) - 1 and kc == KC - 1))
        return b0, bb

    def phase_mixer(self):
        nc = self.nc
        pe, act, dve = self.pe, self.act, self.dve
        with ExitStack() as esB:
            self.es_cur = esB
            self.hT = self.sb("hT", [128, KC, TB], BF16)
            self.hT_b = [Buf(f"hT{k}") for k in range(KC)]
            self.cat = self.sb("cat", [128, KC, TB], BF16)
            self.cat_b = [Buf(f"cat{k}") for k in range(KC)]
            with ExitStack() as esA:
                self.es_cur = esA
                wring = self.ring("mw", [128, KC, 128], BF16, 4, dma=True)
                vbf = self.sb("vbf", [128, 9, 1024], BF16)
                vbf_b = [Buf(f"vbf{i}") for i in range(9)]
                dve.op(lambda: nc.vector.memset(vbf[:, 0, :], 0.0), writes=[vbf_b[0]])
                with ExitStack() as es1:
                    self.es_cur = es1
                    xt_ring = self.ring("mxt", [128, D], F32, 4, dma=True)
                    st_ring = self.ring("mst", [128, 4, 6], F32, 4)
                    mv_ring = self.ring("mmv", [128, 8], F32, 4)
                    stg = self.sb("mstg", [128, KC, 512], F32)
                    stgb = [Buf(f"mstg{k}") for k in range(KC)]
                    stgs = self.dsem("d_mstg")
                    groups = [[0], [1, 2, 3, 4], [5, 6, 7, 8]]
                    for grp in groups:
                        tiles = []
                        for ti in grp:
                            tlo, ntok = TILES[ti]
                            xt, xb, xs = xt_ring.next()
                            st, stb = st_ring.next()
                            mv, mvb = mv_ring.next()
                            tiles.append((self.x_own[tlo:tlo + ntok, :], ntok, xt, xb, xs, st, stb, mv, mvb))
                        self.ln0_stats(tiles)
                        for (xr, ntok, xt, xb, xs, st, stb, mv, mvb) in tiles:
                            act.op(lambda xt=xt, mv=mv, ntok=ntok: nc.scalar.activation(
                                out=xt[0:ntok, :], in_=xt[0:ntok, :], func=AF.Identity, scale=mv[0:ntok, 4:5], bias=mv[0:ntok, 5:6]),
                                reads=[xb, mvb], writes=[xb])
                        glo = TILES[grp[0]][0]
                        ncols = sum(TILES[ti][1] for ti in grp)
                        for kc in range(KC):
                            b0, bb = self.banks(1)
                            col = 0
                            for gi, (xr, ntok, xt, xb, xs, st, stb, mv, mvb) in enumerate(tiles):
                                pe.op(lambda kc=kc, b0=b0, col=col, xt=xt, ntok=ntok: nc.tensor.transpose(
                                    self.ps[:, b0 * 512 + col: b0 * 512 + col + ntok],
                                    xt[0:ntok, kc * 128:(kc + 1) * 128], self.ident[0:ntok, 0:ntok]),
                                    reads=[xb, self.cst_b], writes=bb, signal=(gi == len(tiles) - 1))
                                col += ntok
                            act.op(lambda kc=kc, b0=b0: nc.scalar.activation(
                                out=stg[:, kc, 0:ncols], in_=self.ps[:, b0 * 512: b0 * 512 + ncols], func=AF.Identity,
                                scale=self.pcol(P_G0, kc), bias=self.pcol(P_B0, kc)),
                                reads=bb + [self.par_b], writes=[stgb[kc]])
                            dve.op(lambda kc=kc: nc.vector.tensor_copy(self.hT[:, kc, glo:glo + ncols], stg[:, kc, 0:ncols]),
                                   reads=[stgb[kc]], writes=[self.hT_b[kc]])
                        self.sp.dma(self.h0s[:, :, glo:glo + ncols], stg[:, :, 0:ncols], stgs, reads=stgb)
                    self.h0s_sems = [stgs]
                    self.dump("hT0", self.hT[:], [128, KC, TB], self.hT_b[KC - 1])
                    self.barrier()
                self.es_cur = esA
                if self.stop_after == "ln0":
                    self.barrier()
                    self.es_cur = self.es
                    return
                with ExitStack() as es2:
                    self.es_cur = es2
                    iring = self.ring("mwi", [128, KC, 512], BF16, 2, dma=True)
                    wi = [self.load_w(iring, self.w_in[:, 4096 + hf * 512:4096 + (hf + 1) * 512], 512) for hf in range(2)]
                    for ti, (tlo, ntok) in enumerate(TILES):
                        b0, bb = self.banks(2)
                        for half in range(2):
                            w_t, w_b = wi[half]
                            for kc in range(KC):
                                pe.op(lambda half=half, kc=kc, w_t=w_t, b0=b0, tlo=tlo, ntok=ntok: nc.tensor.matmul(
                                    self.ps[0:ntok, (b0 + half) * 512:(b0 + half + 1) * 512],
                                    lhsT=self.hT[:, kc, tlo:tlo + ntok], rhs=w_t[:, kc, :],
                                    start=(kc == 0), stop=(kc == KC - 1)),
                                    reads=self.hT_b + [w_b], writes=bb, signal=(half == 1 and kc == KC - 1))
                        act.op(lambda ti=ti, b0=b0, ntok=ntok: nc.scalar.activation(
                            out=vbf[0:ntok, ti, :], in_=self.ps[0:ntok, b0 * 512:(b0 + 2) * 512], func=AF.Copy),
                            reads=bb, writes=[vbf_b[ti]])
                    self.barrier()
                self.es_cur = esA
                with ExitStack() as es3:
                    self.es_cur = es3
                    self.conv_branch(wring)
                    self.barrier()
                self.es_cur = esA
                if self.stop_after == "conv":
                    self.es_cur = self.es
                    return
                with ExitStack() as es4:
                    self.es_cur = es4
                    self.hgrn_branch(wring, vbf, vbf_b)
                    self.barrier()
                self.es_cur = esA
            self.es_cur = esB
            self.dump("cat", self.cat[:], [128, KC, TB], self.cat_b[KC - 1])
            if self.stop_after == "hgrn":
                self.barrier()
                self.es_cur = self.es
                return
            self.phase_out_ffn()
        self.es_cur = self.es

    def conv_branch(self, wring):
        nc = self.nc
        pe, act, dve = self.pe, self.act, self.dve
        t_ring = self.ring("ct", [128, TB], F32, 6)
        pend = None
        jobs = []
        for j in range(8):
            jobs.append((j, 0))
            jobs.append((j, 1))
        loaded = {}
        def ensure(idx):
            if idx < len(jobs) and idx not in loaded:
                jj, which = jobs[idx]
                c0 = which * 1024 + jj * 128
                loaded[idx] = self.load_w(wring, self.w_in[:, c0:c0 + 128], 128)
        ensure(0)
        ensure(1)
        for j in range(8):
            ensure(2 * j + 2)
            ensure(2 * j + 3)
            wa, wab = loaded[2 * j]
            wg, wgb = loaded[2 * j + 1]
            ba, bba = self.fm_job(wa, wab, self.hT, self.hT_b)
            bg, bbg = self.fm_job(wg, wgb, self.hT, self.hT_b)
            sg, sgb = t_ring.next()
            act.op(lambda: nc.scalar.activation(out=sg[:], in_=self.ps[:, bg * 512: bg * 512 + TB], func=AF.Sigmoid),
                   reads=bbg, writes=[sgb])
            u, ub = t_ring.next()
            dve.op(lambda: nc.vector.tensor_tensor(u[:], self.ps[:, ba * 512: ba * 512 + TB], sg[:], ALU.mult),
                   reads=bba + [sgb], writes=[ub])
            dve.op(lambda: nc.vector.tensor_scalar(u[:, 0:HALO], u[:, 0:HALO], self.pcol(P_HM), None, ALU.mult),
                   reads=[ub, self.par_b], writes=[ub])
            NO = TB - 30
            acc, accb = t_ring.next()
            cw = lambda k: self.par[:, P_CW + j * CK + k: P_CW + j * CK + k + 1]
            hparts = split_buf(accb, 2)
            HB = [(30, 543, hparts[0]), (543, TB, hparts[1])]
            for (ha, hb_, hbuf) in HB:
                act.op(lambda ha=ha, hb_=hb_: nc.scalar.activation(out=acc[:, ha:hb_], in_=u[:, ha - 30:hb_ - 30], func=AF.Identity,
                                                                   scale=cw(0), bias=self.pcol(P_CB, j)),
                       reads=[ub, self.par_b], writes=[hbuf])
            for k in range(1, CK):
                for (ha, hb_, hbuf) in HB:
                    dve.op(lambda k=k, ha=ha, hb_=hb_: nc.vector.scalar_tensor_tensor(
                        acc[:, ha:hb_], u[:, ha - 30 + k:hb_ - 30 + k], cw(k), acc[:, ha:hb_], ALU.mult, ALU.add),
                        reads=[ub, hbuf, self.par_b], writes=[hbuf])
            join_buf(accb, hparts)
            sq, sqb = t_ring.next()
            act.op(lambda: nc.scalar.activation(out=sq[:, 30:TB], in_=acc[:, 30:TB], func=AF.Square), reads=[accb], writes=[sqb])
            bm, bbm = self.stat_mm(self.ones128, acc, accb, 30, TB)
            bq, bbq = self.stat_mm(self.ones128, sq, sqb, 30, TB)
            mean = self.ps[:, bm * 512 + 30: bm * 512 + TB]
            ex2 = self.ps[:, bq * 512 + 30: bq * 512 + TB]
            rstd, rstdb = self.rstd_from(mean, bbm, ex2, bbq, t_ring, NO, LN_EPS)
            dve.op(lambda: nc.vector.tensor_tensor(acc[:, 30:TB], acc[:, 30:TB], mean, ALU.subtract),
                   reads=bbm + [accb], writes=[accb])
            dve.op(lambda: nc.vector.scalar_tensor_tensor(acc[:, 30:TB], acc[:, 30:TB], self.pcol(P_CNG, j), rstd[:, 0:NO],
                                                          ALU.mult, ALU.mult),
                   reads=[accb, rstdb, self.par_b], writes=[accb])
            act.op(lambda: nc.scalar.activation(out=self.cat[:, j, 30:TB], in_=acc[:, 30:TB], func=AF.Silu,
                                                bias=self.pcol(P_CNB, j)),
                   reads=[accb, self.par_b], writes=[self.cat_b[j]])
            if j == 0:
                self.dump("u0", u[:], [128, TB], ub)
        for j in range(8):
            dve.op(lambda j=j: nc.vector.memset(self.cat[:, j, 0:30], 0.0), reads=[self.cat_b[j]], writes=[self.cat_b[j]])

    def stat_mm(self, ones, src, srcb, lo, hi):
        nc = self.nc
        b0, bb = self.banks(3)
        blks = [(max(a, lo), min(b, hi)) for (a, b) in BLOCKS if max(a, lo) < min(b, hi)]
        for bi, (a, b) in enumerate(blks):
            self.pe.op(lambda a=a, b=b: nc.tensor.matmul(self.ps[:, b0 * 512 + a: b0 * 512 + b], lhsT=ones[:], rhs=src[:, a:b],
                                                         start=True, stop=True),
                       reads=[srcb, self.misc_b], writes=bb, signal=(bi == len(blks) - 1))
        return b0, bb

    def rstd_from(self, mean, bbm, ex2, bbq, t_ring, n, eps):
        nc = self.nc
        m2, m2b = t_ring.next()
        self.act.op(lambda: nc.scalar.activation(out=m2[:, 0:n], in_=mean, func=AF.Square), reads=bbm, writes=[m2b])
        self.dve.op(lambda: nc.vector.scalar_tensor_tensor(m2[:, 0:n], ex2, eps, m2[:, 0:n], ALU.add, ALU.subtract),
                    reads=bbq + [m2b], writes=[m2b])
        self.act.op(lambda: nc.scalar.activation(out=m2[:, 0:n], in_=m2[:, 0:n], func=AF.Sqrt), reads=[m2b], writes=[m2b])
        self.dve.op(lambda: nc.vector.reciprocal(m2[:, 0:n], m2[:, 0:n]), reads=[m2b], writes=[m2b])
        return m2, m2b

    def hgrn_branch(self, wring, vbf, vbf_b):
        nc = self.nc
        pe, act, dve = self.pe, self.act, self.dve
        t_ring = self.ring("ht", [128, TB], F32, 7)
        q_ring = self.ring("hq", [128, TB], BF16, 2)
        k_ring = self.ring("hk", [128, TB], BF16, 2)
        ktok_ring = self.ring("hkt", [128, 2, 9, 128], BF16, 2)
        at_ring = self.ring("hat", [128, 9, 128], BF16, 2)
        for i_ in range(2):
            dve.op(lambda i_=i_: nc.vector.memset(ktok_ring.tiles[i_][:, 0, 0, :], 0.0), writes=[ktok_ring.bufs[i_]])
            dve.op(lambda i_=i_: nc.vector.memset(at_ring.tiles[i_][:, 0, :], 0.0), writes=[at_ring.bufs[i_]])
        sbf_ring = self.ring("hsb", [128, 17, 128], BF16, 2)
        sf_ring = self.ring("hsf", [128, 16, 128], F32, 1)
        kvs_ring = self.ring("hkv", [128, 16, 128], F32, 1)
        cols = {"q": 2048, "f": 3072, "og": 5120}
        order = []
        for h in range(8):
            order += [(h, "f"), (h, "q"), (h, "og")]
        loaded = {}
        def ensure(idx):
            if idx < len(order) and idx not in loaded:
                hh, nm = order[idx]
                c0 = cols[nm] + hh * 128
                loaded[idx] = self.load_w(wring, self.w_in[:, c0:c0 + 128], 128)
        for idx in range(3):
            ensure(idx)
        for h in range(8):
            wf_, wfb = loaded[3 * h]
            bf_, bbf = self.fm_job(wf_, wfb, self.hT, self.hT_b)
            ensure(3 * h + 3)
            sig, sigb = t_ring.next()
            act.op(lambda: nc.scalar.activation(out=sig[:], in_=self.ps[:, bf_ * 512: bf_ * 512 + TB], func=AF.Sigmoid),
                   reads=bbf, writes=[sigb])
            kk, kkb = t_ring.next()
            dve.op(lambda: nc.vector.tensor_scalar(kk[:], sig[:], self.noml[:, h:h + 1], self.oml[:, h:h + 1], ALU.mult, ALU.add),
                   reads=[sigb, self.misc_b], writes=[kkb])
            lf, lfb = t_ring.next()
            act.op(lambda: nc.scalar.activation(out=lf[:], in_=sig[:], func=AF.Ln, scale=self.oml[:, h:h + 1],
                                                bias=self.lbv[:, h:h + 1]), reads=[sigb, self.misc_b], writes=[lfb])
            cum, cumb = t_ring.next()
            dve.op(lambda: nc.vector.tensor_tensor_scan(cum[:], self.rmask, lf[:], 0.0, ALU.mult, ALU.add),
                   reads=[lfb, self.cst_b], writes=[cumb])
            eq, eqb = t_ring.next()
            act.op(lambda: nc.scalar.activation(out=eq[:], in_=cum[:], func=AF.Exp), reads=[cumb], writes=[eqb])
            act.op(lambda: nc.scalar.activation(out=lf[:], in_=cum[:], func=AF.Exp, scale=-1.0), reads=[cumb], writes=[lfb])
            kT, kTb = k_ring.next()
            dve.op(lambda: nc.vector.tensor_tensor(kT[:], kk[:], lf[:], ALU.mult), reads=[kkb, lfb], writes=[kTb])
            dve.op(lambda: nc.vector.tensor_scalar(kT[:, 0:HALO], kT[:, 0:HALO], self.pcol(P_HM), None, ALU.mult),
                   reads=[kTb, self.par_b], writes=[kTb])
            wq_, wqb = loaded[3 * h + 1]
            bq_, bbq = self.fm_job(wq_, wqb, self.hT, self.hT_b)
            ensure(3 * h + 4)
            qs, qsb = t_ring.next()
            act.op(lambda: nc.scalar.activation(out=qs[:], in_=self.ps[:, bq_ * 512: bq_ * 512 + TB], func=AF.Silu),
                   reads=bbq, writes=[qsb])
            qT, qTb = q_ring.next()
            dve.op(lambda: nc.vector.tensor_tensor(qT[:], qs[:], eq[:], ALU.mult), reads=[qsb, eqb], writes=[qTb])
            wo_, wob = loaded[3 * h + 2]
            bo_, bbo = self.fm_job(wo_, wob, self.hT, self.hT_b)
            ensure(3 * h + 5)
            ogs, ogsb = t_ring.next()
            act.op(lambda: nc.scalar.activation(out=ogs[:], in_=self.ps[:, bo_ * 512: bo_ * 512 + TB], func=AF.Silu),
                   reads=bbo, writes=[ogsb])
            ktok, ktokb = ktok_ring.next()
            b0, bb = self.banks(2)
            for ti, (tlo, ntok) in enumerate(TILES):
                o = self.psb[0:ntok, b0 * 1024 + ti * 128: b0 * 1024 + (ti + 1) * 128]
                pe.op(lambda o=o, tlo=tlo, ntok=ntok: nc.tensor.transpose(o, kT[:, tlo:tlo + ntok], self.ident_bf[:]),
                      reads=[kTb, self.misc_b], writes=bb, signal=(ti == 8))
            act.op(lambda: nc.scalar.activation(out=ktok[0:32, 0, 0, :], in_=self.psb[0:32, b0 * 1024: b0 * 1024 + 128], func=AF.Copy),
                   reads=bb, writes=[ktokb])
            for ab, mcol in ((0, C_MA), (1, C_MB)):
                act.op(lambda ab=ab, mcol=mcol: nc.scalar.activation(
                    out=ktok[:, ab, 1:9, :].rearrange("p a b -> p (a b)"),
                    in_=self.psb[:, b0 * 1024 + 128: b0 * 1024 + 9 * 128], func=AF.Identity,
                    scale=self.cst[:, mcol:mcol + 1]),
                    reads=bb + [self.cst_b], writes=[ktokb])
            at, atb = at_ring.next()
            b0, bba = self.banks(3)
            for ti, (tlo, ntok) in enumerate(TILES):
                pe.op(lambda ti=ti, tlo=tlo, ntok=ntok: nc.tensor.matmul(
                    self.ps[0:ntok, b0 * 512 + ti * 128: b0 * 512 + ti * 128 + ntok],
                    lhsT=kT[:, tlo:tlo + ntok], rhs=qT[:, tlo:tlo + ntok], start=True, stop=True),
                    reads=[kTb, qTb], writes=bba, signal=(ti == 8))
            dve.op(lambda: nc.vector.tensor_tensor(at[0:32, 0, 0:32], self.ps[0:32, b0 * 512: b0 * 512 + 32], self.cmask[0:32, 0:32],
                                                   ALU.mult), reads=bba + [self.cst_b], writes=[atb])
            dve.op(lambda: nc.vector.tensor_tensor(
                at[:, 1:9, :], self.ps[:, b0 * 512 + 128: b0 * 512 + 9 * 128].rearrange("p (a b) -> p a b", a=8),
                self.cmask.unsqueeze(1).to_broadcast([128, 8, 128]), ALU.mult), reads=bba + [self.cst_b], writes=[atb])
            sbf, sbfb = sbf_ring.next()
            sf, sfb = sf_ring.next()
            kvbanks = []
            for g4 in range(4):
                b0k, bbk = self.banks(1)
                kvbanks.append((b0k, bbk))
                for cc in range(4):
                    c = g4 * 4 + cc
                    ti, plo, n, tlo = CHUNKS[c]
                    pe.op(lambda ti=ti, plo=plo, n=n, cc=cc, b0k=b0k: nc.tensor.matmul(
                        self.ps[:, b0k * 512 + cc * 128: b0k * 512 + (cc + 1) * 128],
                        lhsT=ktok[:, (0 if plo == 0 else 1), ti, :], rhs=vbf[:, ti, h * 128:(h + 1) * 128], start=True, stop=True),
                        reads=[ktokb, vbf_b[ti]], writes=bbk, signal=(cc == 3))
            kvs, kvsb = kvs_ring.next()
            kvs_bufs = split_buf(kvsb, 16)
            for c in range(16):
                b0k, bbk = kvbanks[c // 4]
                kv = self.ps[:, b0k * 512 + (c % 4) * 128: b0k * 512 + (c % 4 + 1) * 128]
                tlo_end = CHUNKS[c][3] + CHUNKS[c][2] - 1
                act.op(lambda c=c, kv=kv, tlo_end=tlo_end: nc.scalar.activation(
                    out=kvs[:, c, :], in_=kv, func=AF.Identity, scale=eq[:, tlo_end:tlo_end + 1]),
                    reads=bbk + [eqb], writes=[kvs_bufs[c]])
            prev = self.Sin[:, h, :]
            prevb = self.Sin_b
            for c in range(16):
                tlo_end = CHUNKS[c][3] + CHUNKS[c][2] - 1
                dve.op(lambda c=c, prev=prev, tlo_end=tlo_end: nc.vector.scalar_tensor_tensor(
                    sf[:, c, :], prev, eq[:, tlo_end:tlo_end + 1], kvs[:, c, :], ALU.mult, ALU.add),
                    reads=[prevb, kvs_bufs[c], eqb], writes=[sfb])
                prev = sf[:, c, :]
                prevb = sfb
            join_buf(kvsb, kvs_bufs)
            act.op(lambda: nc.scalar.activation(out=sbf[:, 1:17, :].rearrange("p a b -> p (a b)"),
                                                in_=sf[:, :, :].rearrange("p a b -> p (a b)"), func=AF.Copy),
                   reads=[sfb], writes=[sbfb])
            b0o, bbo2 = self.banks(3)
            for ti, (tlo, ntok) in enumerate(TILES):
                oc = b0o * 512 + ti * 128
                pe.op(lambda ti=ti, ntok=ntok, oc=oc: nc.tensor.matmul(
                    self.ps[:, oc: oc + ntok], lhsT=vbf[:, ti, h * 128:(h + 1) * 128], rhs=at[:, ti, 0:ntok],
                    start=True, stop=False, skip_group_check=True), reads=[vbf_b[ti], atb], writes=bbo2, signal=False)
                cs = [c for c in range(17) if CHUNKS[c][0] == ti]
                for ci, c in enumerate(cs):
                    _, plo, n, ctlo = CHUNKS[c]
                    lhs = self.Sin_bf[:, h, :] if c == 0 else sbf[:, c, :]
                    rb = [self.Sinbf_b] if c == 0 else [sbfb]
                    pe.op(lambda lhs=lhs, oc=oc, plo=plo, n=n, ctlo=ctlo: nc.tensor.matmul(
                        self.ps[:, oc + plo: oc + plo + n], lhsT=lhs, rhs=qT[:, ctlo:ctlo + n], start=False,
                        stop=True, skip_group_check=True), reads=rb + [qTb], writes=bbo2,
                        signal=(ti == 8 and ci == len(cs) - 1))
            o_sb, osb = t_ring.next()
            act.op(lambda: nc.scalar.activation(out=o_sb[:, 0:32], in_=self.ps[:, b0o * 512: b0o * 512 + 32], func=AF.Copy),
                   reads=bbo2, writes=[osb])
            act.op(lambda: nc.scalar.activation(out=o_sb[:, 32:TB], in_=self.ps[:, b0o * 512 + 128: b0o * 512 + 9 * 128], func=AF.Copy),
                   reads=bbo2, writes=[osb])
            if h == 0:
                self.dump("o0", o_sb[:], [128, TB], osb)
            act.op(lambda: nc.scalar.activation(out=cum[:], in_=o_sb[:], func=AF.Square), reads=[osb], writes=[cumb])
            bm, bbm = self.stat_mm(self.ones128, cum, cumb, 0, TB)
            dve.op(lambda: nc.vector.tensor_scalar(cum[:], self.ps[:, bm * 512: bm * 512 + TB], RMS_EPS, None, ALU.add),
                   reads=bbm, writes=[cumb])
            act.op(lambda: nc.scalar.activation(out=cum[:], in_=cum[:], func=AF.Sqrt), reads=[cumb], writes=[cumb])
            dve.op(lambda: nc.vector.reciprocal(cum[:], cum[:]), reads=[cumb], writes=[cumb])
            dve.op(lambda: nc.vector.scalar_tensor_tensor(o_sb[:], o_sb[:], self.pcol(P_HG, h), cum[:], ALU.mult, ALU.mult),
                   reads=[osb, cumb, self.par_b], writes=[osb])
            dve.op(lambda: nc.vector.tensor_tensor(self.cat[:, 8 + h, :], o_sb[:], ogs[:], ALU.mult),
                   reads=[osb, ogsb], writes=[self.cat_b[8 + h]])

    def ln_fm(self, Y, Yb, lo, hi, g_off, b_off, write_bf=None, mask_halo=False):
        nc = self.nc
        pe, act, dve = self.pe, self.act, self.dve
        n = hi - lo
        t_ring = self.ring("lnt", [128, TB], F32, 3)
        sq_ring = self.ring("lnq", [128, TB], F32, 2)
        sy, syb = t_ring.next()
        ss, ssb = t_ring.next()
        for m in range(KC):
            sq, sqb = sq_ring.next()
            act.op(lambda m=m: nc.scalar.activation(out=sq[:, lo:hi], in_=Y[:, m, lo:hi], func=AF.Square),
                   reads=[Yb[m]], writes=[sqb])
            if m == 0:
                dve.op(lambda: nc.vector.tensor_copy(sy[:, lo:hi], Y[:, 0, lo:hi]), reads=[Yb[0]], writes=[syb])
                dve.op(lambda: nc.vector.tensor_copy(ss[:, lo:hi], sq[:, lo:hi]), reads=[sqb], writes=[ssb])
            else:
                dve.op(lambda m=m: nc.vector.tensor_tensor(sy[:, lo:hi], sy[:, lo:hi], Y[:, m, lo:hi], ALU.add),
                       reads=[Yb[m], syb], writes=[syb])
                dve.op(lambda: nc.vector.tensor_tensor(ss[:, lo:hi], ss[:, lo:hi], sq[:, lo:hi], ALU.add),
                       reads=[sqb, ssb], writes=[ssb])
        bm, bbm = self.stat_mm(self.onesD, sy, syb, lo, hi)
        bq, bbq = self.stat_mm(self.onesD, ss, ssb, lo, hi)
        mean = self.ps[:, bm * 512 + lo: bm * 512 + hi]
        ex2 = self.ps[:, bq * 512 + lo: bq * 512 + hi]
        rstd, rstdb = self.rstd_from(mean, bbm, ex2, bbq, t_ring, n, LN_EPS)
        dve.op(lambda: nc.vector.scalar_tensor_tensor(sy[:, lo:hi], mean, -1.0, rstd[:, 0:n], ALU.mult, ALU.mult),
               reads=bbm + [rstdb, syb], writes=[syb])
        for m0 in range(0, KC, 4):
            ms = range(m0, m0 + 4)
            for m in ms:
                dve.op(lambda m=m: nc.vector.tensor_tensor(Y[:, m, lo:hi], Y[:, m, lo:hi], rstd[:, 0:n], ALU.mult),
                       reads=[Yb[m], rstdb], writes=[Yb[m]])
            for m in ms:
                dve.op(lambda m=m: nc.vector.tensor_tensor(Y[:, m, lo:hi], Y[:, m, lo:hi], sy[:, lo:hi], ALU.add),
                       reads=[Yb[m], syb], writes=[Yb[m]])
            for m in ms:
                act.op(lambda m=m: nc.scalar.activation(out=Y[:, m, lo:hi], in_=Y[:, m, lo:hi], func=AF.Identity,
                                                        scale=self.pcol(g_off, m), bias=self.pcol(b_off, m)),
                       reads=[Yb[m], self.par_b], writes=[Yb[m]])
            if write_bf is not None:
                wt, wb = write_bf
                for m in ms:
                    dve.op(lambda m=m: nc.vector.tensor_copy(wt[:, m, lo:hi], Y[:, m, lo:hi]), reads=[Yb[m]], writes=[wb[m]])
                if mask_halo:
                    for m in ms:
                        dve.op(lambda m=m: nc.vector.tensor_scalar(wt[:, m, lo:HALO], wt[:, m, lo:HALO], self.pcol(P_HM), None, ALU.mult),
                               reads=[wb[m], self.par_b], writes=[wb[m]])

    def phase_out_ffn(self):
        nc = self.nc
        pe, act, dve = self.pe, self.act, self.dve
        with ExitStack() as esO:
            self.es_cur = esO
            Y = self.sb("Y", [128, KC, TB], F32)
            Yb = [Buf(f"Y{m}") for m in range(KC)]
            wring = self.ring("ow", [128, KC, 128], BF16, 4, dma=True)
            dring = self.ring("od", [128, GC, 128], BF16, 3, dma=True)
            with ExitStack() as e5:
                self.es_cur = e5
                h0_ring = self.ring("oh0", [128, TB], F32, 2, dma=True)
                loaded = {}
                def ensure(m):
                    if m < KC and m not in loaded:
                        loaded[m] = self.load_w(wring, self.w_out[:, m * 128:(m + 1) * 128], 128)
                for m in range(3):
                    ensure(m)
                for s_ in self.h0s_sems:
                    self.sp.wait(s_, s_.count)
                for m in range(KC):
                    ensure(m + 3)
                    h0t, h0b, h0sem = h0_ring.next()
                    self.sp.dma(h0t[:], self.h0s[:, m, :], h0sem, writes=[h0b])
                    w_t, w_b = loaded[m]
                    b0, bb = self.fm_job(w_t, w_b, self.cat, self.cat_b, lo=30)
                    dve.op(lambda m=m, b0=b0, h0t=h0t: nc.vector.scalar_tensor_tensor(
                        Y[:, m, 30:TB], h0t[:, 30:TB], ALPHA, self.ps[:, b0 * 512 + 30: b0 * 512 + TB], ALU.mult, ALU.add),
                        reads=bb + [h0b], writes=[Yb[m]])
                self.dump("y1", Y[:], [128, KC, TB], Yb[KC - 1])
                self.barrier()
            self.es_cur = esO
            with ExitStack() as e6:
                self.es_cur = e6
                self.ln_fm(Y, Yb, 30, TB, P_G1, P_B1, write_bf=(self.hT, self.hT_b), mask_halo=True)
                self.dump("h1", Y[:], [128, KC, TB], Yb[KC - 1])
                self.barrier()
            self.es_cur = esO
            if self.stop_after == "mixer":
                self.es_cur = self.es
                return
            actb = self.cat[:, :, :].rearrange("p a b -> p (a b)")
            act_bufs = [Buf(f"act{j}") for j in range(GC)]
            with ExitStack() as e7:
                self.es_cur = e7
                t_ring = self.ring("ft", [128, TB], F32, 4)
                jobs = [(g, jj) for g in range(NG) for jj in range(GC)]
                upl = {}
                def ensure_up(idx):
                    if idx < len(jobs) and idx not in upl:
                        j = idx
                        wg = self.load_w(wring, self.w_up[:, j * 128:(j + 1) * 128], 128)
                        wv = self.load_w(wring, self.w_up[:, DFF + j * 128: DFF + (j + 1) * 128], 128)
                        upl[idx] = (wg, wv)
                dnl = {}
                def ensure_dn(g, m):
                    if g < NG and m < KC and (g, m) not in dnl:
                        dnl[(g, m)] = self.load_w(dring, self.w_down[g * GC * 128:(g + 1) * GC * 128, m * 128:(m + 1) * 128], 128, nk=GC)
                ensure_up(0)
                for g in range(NG):
                    for jj in range(GC):
                        idx = g * GC + jj
                        j = idx
                        ensure_up(idx + 1)
                        if jj == GC - 1:
                            ensure_dn(g, 0)
                            ensure_dn(g, 1)
                        (wg, wgb), (wv, wvb) = upl.pop(idx)
                        bg, bbg = self.fm_job(wg, wgb, self.hT, self.hT_b, lo=30)
                        bv, bbv = self.fm_job(wv, wvb, self.hT, self.hT_b, lo=HALO)
                        gs, gsb = t_ring.next()
                        act.op(lambda bg=bg: nc.scalar.activation(out=gs[:, 30:TB], in_=self.ps[:, bg * 512 + 30: bg * 512 + TB], func=AF.Copy),
                               reads=bbg, writes=[gsb])
                        c, cb = t_ring.next()
                        fw = lambda k, j=j: self.par[:, P_FW + j * 3 + k: P_FW + j * 3 + k + 1]
                        dve.op(lambda: nc.vector.tensor_scalar(c[:, 0:T], gs[:, 32:TB], fw(2), self.pcol(P_FB, j), ALU.mult, ALU.add),
                               reads=[gsb, self.par_b], writes=[cb])
                        dve.op(lambda: nc.vector.scalar_tensor_tensor(c[:, 0:T], gs[:, 31:TB - 1], fw(1), c[:, 0:T], ALU.mult, ALU.add),
                               reads=[gsb, cb, self.par_b], writes=[cb])
                        dve.op(lambda: nc.vector.scalar_tensor_tensor(c[:, 0:T], gs[:, 30:TB - 2], fw(0), c[:, 0:T], ALU.mult, ALU.add),
                               reads=[gsb, cb, self.par_b], writes=[cb])
                        act.op(lambda: nc.scalar.activation(out=c[:, 0:T], in_=c[:, 0:T], func=AF.Silu), reads=[cb], writes=[cb])
                        dve.op(lambda jj=jj, bv=bv: nc.vector.tensor_tensor(actb[:, jj * T:(jj + 1) * T], c[:, 0:T],
                                                                            self.ps[:, bv * 512 + HALO: bv * 512 + TB], ALU.mult),
                               reads=bbv + [cb], writes=[act_bufs[jj]])
                    for m in range(KC):
                        ensure_dn(g, m + 2)
                        w_t, w_b = dnl.pop((g, m))
                        b0, bb = self.banks(2)
                        for half in range(2):
                            for jj in range(GC):
                                pe.op(lambda half=half, jj=jj, m=m, w_t=w_t, b0=b0: nc.tensor.matmul(
                                    self.ps[:, (b0 + half) * 512:(b0 + half + 1) * 512], lhsT=w_t[:, jj, :],
                                    rhs=actb[:, jj * T + half * 512: jj * T + (half + 1) * 512],
                                    start=(jj == 0), stop=(jj == GC - 1)),
                                    reads=[act_bufs[jj], w_b], writes=bb, signal=(half == 1 and jj == GC - 1))
                        if g == 0:
                            dve.op(lambda m=m, b0=b0: nc.vector.scalar_tensor_tensor(
                                Y[:, m, HALO:TB], Y[:, m, HALO:TB], ALPHA, self.ps[:, b0 * 512:(b0 + 2) * 512], ALU.mult, ALU.add),
                                reads=bb + [Yb[m]], writes=[Yb[m]])
                        else:
                            dve.op(lambda m=m, b0=b0: nc.vector.tensor_tensor(
                                Y[:, m, HALO:TB], Y[:, m, HALO:TB], self.ps[:, b0 * 512:(b0 + 2) * 512], ALU.add),
                                reads=bb + [Yb[m]], writes=[Yb[m]])
                self.barrier()
            self.es_cur = esO
            with ExitStack() as e8:
                self.es_cur = e8
                self.ln_fm(Y, Yb, HALO, TB, P_G2, P_B2)
                self.barrier()
            self.es_cur = esO
            with ExitStack() as e9:
                self.es_cur = e9
                o_ring = self.ring("oo", [128, D], F32, 2, dma=True)
                for ti in range(8):
                    ot, otb, osem = o_ring.next()
                    for q in range(4):
                        b0, bb = self.banks(1)
                        for kk in range(4):
                            kc = q * 4 + kk
                            pe.op(lambda kc=kc, kk=kk, b0=b0, ti=ti: nc.tensor.transpose(
                                self.ps[:, b0 * 512 + kk * 128: b0 * 512 + (kk + 1) * 128],
                                Y[:, kc, HALO + ti * 128: HALO + (ti + 1) * 128], self.ident),
                                reads=[Yb[kc], self.cst_b], writes=bb, signal=(kk == 3))
                        if q % 2 == 0:
                            act.op(lambda q=q, b0=b0, ot=ot: nc.scalar.activation(out=ot[:, q * 512:(q + 1) * 512],
                                                                                  in_=self.ps[:, b0 * 512:(b0 + 1) * 512], func=AF.Copy),
                                   reads=bb, writes=[otb])
                        else:
                            dve.op(lambda q=q, b0=b0, ot=ot: nc.vector.tensor_copy(ot[:, q * 512:(q + 1) * 512], self.ps[:, b0 * 512:(b0 + 1) * 512]),
                                   reads=bb, writes=[otb])
                    self.sp.dma(self.out_d[ti * 128:(ti + 1) * 128, :], ot[:], osem, reads=[otb])
                self.final_waits.extend(o_ring.sems)
                self.barrier()
            self.es_cur = esO
        self.es_cur = self.es


def _pack_params(inp, hmask):
    P = np.zeros((128, NPP), np.float32)
    fm = lambda v, n: np.ascontiguousarray(np.asarray(v, np.float32).reshape(n, 128).T)
    P[:, P_G0:P_G0 + 16] = fm(inp["emb_ln_g"], 16)
    P[:, P_B0:P_B0 + 16] = fm(inp["emb_ln_b"], 16)
    cw = np.asarray(inp["conv_w"], np.float32)[0]
    P[:, P_CW:P_CW + 248] = cw.reshape(CK, 8, 128).transpose(2, 1, 0).reshape(128, 248)
    P[:, P_CB:P_CB + 8] = fm(inp["conv_b"][0], 8)
    P[:, P_CNG:P_CNG + 8] = fm(inp["conv_norm_g"][0], 8)
    P[:, P_CNB:P_CNB + 8] = fm(inp["conv_norm_b"][0], 8)
    lbl = np.asarray(inp["lb_logits"], np.float32)
    P[:, P_LBL:P_LBL + 16] = lbl.reshape(2, 8, 128).transpose(2, 1, 0).reshape(128, 16)
    P[:, P_HG:P_HG + 8] = fm(inp["hgrn_norm_g"][0], 8)
    P[:, P_G1:P_G1 + 16] = fm(inp["ln1_g"][0], 16)
    P[:, P_B1:P_B1 + 16] = fm(inp["ln1_b"][0], 16)
    fw = np.asarray(inp["ffn_conv_w"], np.float32)[0]
    P[:, P_FW:P_FW + 132] = fw.reshape(3, FC, 128).transpose(2, 1, 0).reshape(128, 132)
    P[:, P_FB:P_FB + FC] = fm(inp["ffn_conv_b"][0], FC)
    P[:, P_G2:P_G2 + 16] = fm(inp["ln2_g"][0], 16)
    P[:, P_B2:P_B2 + 16] = fm(inp["ln2_b"][0], 16)
    P[:, P_HM] = hmask
    return P


def _consts():
    C = np.zeros((128, NCC), np.float32)
    C[:, C_ID:C_ID + 128] = np.eye(128, dtype=np.float32)
    s = np.arange(128)[:, None]
    t = np.arange(128)[None, :]
    C[:, C_CM:C_CM + 128] = ((s // 64 == t // 64) & (s <= t)).astype(np.float32)
    rm = np.ones(TB, np.float32)
    for (_, _, _, tlo) in CHUNKS:
        rm[tlo] = 0.0
    C[:, C_RM:C_RM + TB] = rm[None, :]
    C[0:64, C_MA] = 1.0
    C[64:128, C_MB] = 1.0
    return C


_CACHE = {}


def make_in_maps(inputs, cores=range(8)):
    x = np.asarray(inputs["x"], np.float32)
    w_in = np.ascontiguousarray(np.asarray(inputs["w_in"], np.float32)[0])
    w_out = np.ascontiguousarray(np.asarray(inputs["w_out"], np.float32)[0])
    w_up = np.ascontiguousarray(np.asarray(inputs["w_ffn_up"], np.float32)[0])
    w_down = np.ascontiguousarray(np.asarray(inputs["w_ffn_down"], np.float32)[0])
    consts = _consts()
    maps = []
    for c in cores:
        b, p = c // 4, c % 4
        t0 = p * T
        xo = np.zeros((TB, D), np.float32)
        lo = t0 - HALO
        if lo >= 0:
            xo[:] = x[b, lo:t0 + T]
        else:
            xo[HALO:] = x[b, 0:T]
        xp = np.zeros((NPRE, D), np.float32)
        pm = np.zeros(NPRE, np.float32)
        end = t0 - HALO
        nval = max(0, end)
        if nval > 0:
            xp[NPRE - nval:] = x[b, 0:end]
            pm[NPRE - nval:] = 1.0
        pmask = np.ascontiguousarray(pm.reshape(NPRE // 128, 128).T)
        maps.append({
            "x_own": xo, "x_pre": xp, "pmask": pmask,
            "params": _pack_params(inputs, 0.0 if p == 0 else 1.0), "consts": consts,
            "w_in": w_in, "w_out": w_out, "w_up": w_up, "w_down": w_down,
        })
    return maps


def kernel(**inputs):
    if "nc" not in _CACHE:
        _CACHE["nc"] = KB().build()
    nc = _CACHE["nc"]
    maps = make_in_maps(inputs)
    res = run_bass_kernel_spmd(nc, maps, core_ids=list(range(8)))
    out = np.zeros((2, 4 * T, D), np.float32)
    for c in range(8):
        b, p = c // 4, c % 4
        out[b, p * T:(p + 1) * T] = res.results[c]["out"]
    return out
```

```python
import numpy as np
from contextlib import ExitStack
import concourse.bass as bass
import concourse.mybir as mybir
from concourse.bass_utils import run_bass_kernel_spmd

F32 = mybir.dt.float32
BF16 = mybir.dt.bfloat16
AF = mybir.ActivationFunctionType
ALU = mybir.AluOpType

D = 2048
KC = 16
T = 1024
HALO = 32
TB = T + HALO
NPRE = 3072
DFF = 5632
FC = 44
NG = 4
GC = FC // NG
ALPHA = 2.0 ** 0.25
LN_EPS = 1e-5
RMS_EPS = 1e-6
CK = 31

P_G0, P_B0, P_CW, P_CB, P_CNG, P_CNB, P_LBL, P_HG = 0, 16, 32, 280, 288, 296, 304, 320
P_G1, P_B1, P_FW, P_FB, P_G2, P_B2, P_HM = 328, 344, 360, 492, 536, 552, 568
NPP = 576
C_ID, C_CM, C_RM = 0, 128, 256
C_MA = 256 + TB
C_MB = C_MA + 1
NCC = 256 + TB + 2

BLOCKS = [(0, 512), (512, 1024), (1024, TB)]
TILES = [(0, 32)] + [(32 + 128 * i, 128) for i in range(8)]
CHUNKS = [(0, 0, 32, 0)]
for _i in range(8):
    CHUNKS.append((_i + 1, 0, 64, 32 + 128 * _i))
    CHUNKS.append((_i + 1, 64, 64, 32 + 128 * _i + 64))


import os
SAME_ENGINE_WAITS = True
POOL_TAPS = int(os.environ.get("POOL_TAPS", "0"))


class Sem:
    def __init__(self, nc, es, name):
        self.h = es.enter_context(nc.semaphore(name))
        self.count = 0
        self.name = name


class Buf:
    __slots__ = ("name", "w", "r", "excl")

    def __init__(self, name, excl=False):
        self.name = name
        self.w = {}
        self.r = {}
        self.excl = excl


def _merge(d, src):
    for k, v in src.items():
        if d.get(k, 0) < v:
            d[k] = v


def split_buf(buf, n):
    parts = []
    for i in range(n):
        p = Buf(f"{buf.name}.{i}", buf.excl)
        p.w = dict(buf.w)
        p.r = dict(buf.r)
        parts.append(p)
    return parts


def join_buf(buf, parts):
    buf.w = {}
    buf.r = {}
    for p in parts:
        _merge(buf.w, p.w)
        _merge(buf.r, p.r)


class Eng:
    def __init__(self, kb, name, beng, is_pe=False):
        self.kb = kb
        self.name = name
        self.e = beng
        self.sem = Sem(kb.nc, kb.es, "s_" + name)
        self.known = {}
        self.is_pe = is_pe
        self.pending = False

    def wait(self, sem, val):
        if val <= 0 or self.known.get(sem, 0) >= val:
            return
        self.e.wait_ge(sem.h, val)
        self.known[sem] = val

    def _deps(self, reads, writes):
        deps = {}
        for b in reads:
            _merge(deps, b.w)
            if b.excl:
                for k, v in b.r.items():
                    if k is not self.sem and deps.get(k, 0) < v:
                        deps[k] = v
        for b in writes:
            _merge(deps, b.w)
            _merge(deps, b.r)
        for sem, val in deps.items():
            if sem is self.sem:
                if self.is_pe or not SAME_ENGINE_WAITS:
                    continue
                if val > sem.count:
                    continue
            self.wait(sem, val)

    def op(self, fn, reads=(), writes=(), signal=True):
        self._deps(reads, writes)
        inst = fn()
        val = self.sem.count + 1
        if signal:
            self.sem.count = val
            inst.then_inc(self.sem.h, 1)
            self.pending = False
        else:
            self.pending = True
        for b in reads:
            if b.r.get(self.sem, 0) < val:
                b.r[self.sem] = val
        for b in writes:
            b.w = {self.sem: val}
            b.r = {}
        return inst

    def dma(self, out, in_, sem, reads=(), writes=()):
        self._deps(reads, writes)
        self.wait(sem, sem.count)
        inst = self.e.dma_start(out=out, in_=in_)
        sem.count += 16
        inst.then_inc(sem.h, 16)
        for b in reads:
            b.r[sem] = sem.count
        for b in writes:
            b.w = {sem: sem.count}
            b.r = {}
        return inst


class Ring:
    def __init__(self, kb, name, shape, dtype, n, dma=False):
        self.tiles = [kb.sb(f"{name}{i}", shape, dtype) for i in range(n)]
        self.bufs = [Buf(f"{name}{i}") for i in range(n)]
        self.sems = [Sem(kb.nc, kb.es, f"d_{name}{i}") for i in range(n)] if dma else None
        self.i = 0
        self.n = n

    def next(self):
        k = self.i % self.n
        self.i += 1
        if self.sems:
            return self.tiles[k], self.bufs[k], self.sems[k]
        return self.tiles[k], self.bufs[k]


class KB:
    def __init__(self, dbg=()):
        self.dbg = list(dbg)
        self.dbg_out = {}

    def sb(self, name, shape, dtype):
        self._uid = getattr(self, "_uid", 0) + 1
        return self.es_cur.enter_context(self.nc.sbuf_tensor(f"{name}_{self._uid}", list(shape), dtype))

    def banks(self, n):
        if self.bank_ptr + n > 8:
            self.bank_ptr = 0
        b0 = self.bank_ptr
        self.bank_ptr = (self.bank_ptr + n) % 8
        return b0, self.bank_bufs[b0:b0 + n]

    def barrier(self):
        sems = [e.sem for e in self.engs] + self.all_dma_sems
        for e in self.engs:
            assert not e.pending, e.name
            for s in sems:
                if s is e.sem:
                    continue
                e.wait(s, s.count)

    def dsem(self, name):
        s = Sem(self.nc, self.es, name)
        self.all_dma_sems.append(s)
        return s

    def ring(self, name, shape, dtype, n, dma=False):
        r = Ring(self, name, shape, dtype, n, dma)
        if dma:
            self.all_dma_sems.extend(r.sems)
        return r

    def dump(self, name, tile_ap, shape, buf):
        if name not in self.dbg:
            return
        dt = tile_ap.dtype
        o = self.nc.dram_tensor("dbg_" + name, list(shape), dt, kind="ExternalOutput").ap()
        self.dbg_out[name] = (list(shape), dt)
        s = self.dsem("dbg_" + name)
        idx = tuple(slice(None) for _ in shape)
        self.sp.dma(o[idx], tile_ap, s, reads=[buf])
        self.final_waits.append(s)

    def build(self, stop_after=None):
        nc = bass.Bass("TRN2", target_bir_lowering=False)
        self.nc = nc
        self.stop_after = stop_after
        dr = lambda name, shape, kind="ExternalInput", dt=F32: nc.dram_tensor(name, list(shape), dt, kind=kind).ap()
        self.x_own = dr("x_own", [TB, D])
        self.x_pre = dr("x_pre", [NPRE, D])
        self.pmask_d = dr("pmask", [128, NPRE // 128])
        self.params_d = dr("params", [128, NPP])
        self.consts_d = dr("consts", [128, NCC])
        self.w_in = dr("w_in", [D, 6144])
        self.w_out = dr("w_out", [D, D])
        self.w_up = dr("w_up", [D, 2 * DFF])
        self.w_down = dr("w_down", [DFF, D])
        self.out_d = dr("out", [T, D], kind="ExternalOutput")
        self.h0s = nc.dram_tensor("h0s", [128, KC, TB], F32).ap()
        self.final_waits = []
        self.all_dma_sems = []
        with ExitStack() as es:
            self.es = es
            self.es_cur = es
            self.pe = Eng(self, "pe", nc.tensor, is_pe=True)
            self.act = Eng(self, "act", nc.scalar)
            self.dve = Eng(self, "dve", nc.vector)
            self.gp = Eng(self, "gp", nc.gpsimd)
            self.sp = Eng(self, "sp", nc.sync)
            self.engs = [self.pe, self.act, self.dve, self.gp, self.sp]
            self.ps = es.enter_context(nc.psum_tensor("ps", [128, 8 * 512], F32))
            self.psb = self.ps[:, :].bitcast(BF16)
            self.bank_bufs = [Buf(f"bank{i}", excl=True) for i in range(8)]
            self.bank_ptr = 0
            self.setup()
            if stop_after != "setup":
                self.run_phases()
            for s in self.final_waits:
                self.sp.wait(s, s.count)
            for e in self.engs:
                assert not e.pending, e.name
        return nc

    def setup(self):
        nc = self.nc
        self.par = self.sb("par", [128, NPP], F32)
        self.par_b = Buf("par")
        self.cst = self.sb("cst", [128, NCC], F32)
        self.cst_b = Buf("cst")
        self.pm = self.sb("pm", [128, NPRE // 128], F32)
        s = self.dsem("ld_setup")
        self.sp.dma(self.par[:], self.params_d[:, :], s, writes=[self.par_b])
        self.sp.dma(self.cst[:], self.consts_d[:, :], s, writes=[self.cst_b])
        self.sp.dma(self.pm[:], self.pmask_d[:, :], s, writes=[self.par_b])
        self.ident = self.cst[:, C_ID:C_ID + 128]
        self.cmask = self.cst[:, C_CM:C_CM + 128]
        self.rmask = self.cst[:, C_RM:C_RM + TB]
        self.ident_bf = self.sb("ident_bf", [128, 128], BF16)
        self.ones128 = self.sb("ones128", [128, 128], F32)
        self.onesD = self.sb("onesD", [128, 128], F32)
        self.ones512 = self.sb("ones512", [128, 512], F32)
        self.lbv = self.sb("lbv", [128, 8], F32)
        self.oml = self.sb("oml", [128, 8], F32)
        self.noml = self.sb("noml", [128, 8], F32)
        self.misc_b = Buf("misc")
        self.Sin = self.sb("Sin", [128, 8, 128], F32)
        self.Sin_b = Buf("Sin")
        self.Sin_bf = self.sb("Sin_bf", [128, 8, 128], BF16)
        self.Sinbf_b = Buf("Sin_bf")
        d = self.dve
        d.op(lambda: nc.vector.tensor_copy(self.ident_bf[:], self.ident), reads=[self.cst_b], writes=[self.misc_b])
        d.op(lambda: nc.vector.memset(self.ones128[:], 1.0 / 128.0), writes=[self.misc_b])
        d.op(lambda: nc.vector.memset(self.onesD[:], 1.0 / D), writes=[self.misc_b])
        d.op(lambda: nc.vector.memset(self.ones512[:], 1.0), writes=[self.misc_b])
        d.op(lambda: nc.vector.memset(self.Sin[:], 0.0), writes=[self.Sin_b])
        lbl = self.par[:, P_LBL:P_LBL + 16].rearrange("p (j two) -> p j two", two=2)
        d.op(lambda: nc.vector.tensor_tensor(self.noml[:], lbl[:, :, 0], lbl[:, :, 1], ALU.subtract),
             reads=[self.par_b], writes=[self.misc_b])
        self.act.op(lambda: nc.scalar.activation(out=self.lbv[:], in_=self.noml[:], func=AF.Sigmoid),
                    reads=[self.misc_b], writes=[self.misc_b])
        d.op(lambda: nc.vector.tensor_scalar(self.oml[:], self.lbv[:], -1.0, 1.0, ALU.mult, ALU.add),
             reads=[self.misc_b], writes=[self.misc_b])
        d.op(lambda: nc.vector.tensor_scalar(self.noml[:], self.oml[:], -1.0, None, ALU.mult),
             reads=[self.misc_b], writes=[self.misc_b])

    def pcol(self, off, j=0):
        return self.par[:, off + j:off + j + 1]

    def load_w(self, ring, dram_ap_2d, ncols, nk=KC):
        t, b, s = ring.next()
        self.gp.dma(t[:, 0:nk, 0:ncols], dram_ap_2d.rearrange("(kc p) c -> p kc c", p=128), s, writes=[b])
        return t, b

    def ln0_stats(self, tiles):
        nc = self.nc
        for (xr, ntok, xt, xb, xs, st, stb, mv, mvb) in tiles:
            self.sp.dma(xt[0:ntok, :], xr, xs, writes=[xb])
        for (xr, ntok, xt, xb, xs, st, stb, mv, mvb) in tiles:
            for q in range(4):
                self.dve.op(lambda q=q, st=st, xt=xt, ntok=ntok: nc.vector.bn_stats(st[0:ntok, q, :], xt[0:ntok, q * 512:(q + 1) * 512]),
                            reads=[xb], writes=[stb])
        for (xr, ntok, xt, xb, xs, st, stb, mv, mvb) in tiles:
            self.dve.op(lambda st=st, mv=mv, ntok=ntok: nc.vector.bn_aggr(mv[0:ntok, 0:2], st[0:ntok, :, :].rearrange("p a b -> p (a b)")),
                        reads=[stb], writes=[mvb])
        for (xr, ntok, xt, xb, xs, st, stb, mv, mvb) in tiles:
            self.dve.op(lambda mv=mv, ntok=ntok: nc.vector.tensor_scalar(mv[0:ntok, 2:3], mv[0:ntok, 1:2], LN_EPS, None, ALU.add),
                        reads=[mvb], writes=[mvb])
        for (xr, ntok, xt, xb, xs, st, stb, mv, mvb) in tiles:
            self.act.op(lambda mv=mv, ntok=ntok: nc.scalar.activation(out=mv[0:ntok, 3:4], in_=mv[0:ntok, 2:3], func=AF.Sqrt),
                        reads=[mvb], writes=[mvb])
        for (xr, ntok, xt, xb, xs, st, stb, mv, mvb) in tiles:
            self.dve.op(lambda mv=mv, ntok=ntok: nc.vector.reciprocal(mv[0:ntok, 4:5], mv[0:ntok, 3:4]), reads=[mvb], writes=[mvb])
        for (xr, ntok, xt, xb, xs, st, stb, mv, mvb) in tiles:
            self.dve.op(lambda mv=mv, ntok=ntok: nc.vector.tensor_scalar(mv[0:ntok, 5:6], mv[0:ntok, 0:1], -1.0, mv[0:ntok, 4:5],
                                                                         ALU.mult, ALU.mult), reads=[mvb], writes=[mvb])

    def run_phases(self):
        self.phase_prefix()
        if self.stop_after == "prefix":
            return
        self.phase_mixer()

    def phase_prefix(self):
        nc = self.nc
        pe, act, dve = self.pe, self.act, self.dve
        with ExitStack() as es:
            self.es_cur = es
            wring = self.ring("pw", [128, KC, 512], BF16, 4, dma=True)
            wts = []
            for c0 in (3072, 3072 + 512, 4096, 4096 + 512):
                wts.append(self.load_w(wring, self.w_in[:, c0:c0 + 512], 512))
            xt_ring = self.ring("pxt", [128, D], F32, 4, dma=True)
            st_ring = self.ring("pst", [128, 4, 6], F32, 4)
            mv_ring = self.ring("pmv", [128, 8], F32, 4)
            xn = self.sb("pxn", [128, 4, D], BF16)
            xnb = Buf("pxn")
            h0p = self.sb("ph0", [128, KC, 512], BF16)
            h0b = [Buf(f"ph0_{k}") for k in range(KC)]
            NF = 4
            tsig = self.ring("psg", [128, 512], F32, NF)
            ts2 = self.ring("ps2", [128, 512], F32, NF)
            tlf = self.ring("plf", [128, 512], F32, NF)
            tcb = self.ring("pcb", [128, 512], F32, NF)
            wT_ring = self.ring("pwT", [128, 512], BF16, 2 * NF)
            wk = self.sb("pwk", [128, 4, 1024], BF16)
            wkb = Buf("pwk")
            vt = self.sb("pv", [128, 4, 1024], BF16)
            vb = Buf("pv")
            carry = self.sb("pcarry", [128, 8], F32)
            carry_b = Buf("carry")
            dve.op(lambda: nc.vector.memset(carry[:], 0.0), writes=[carry_b])
            order = list(range(NPRE // 512 - 1, -1, -1))

            def ln0_block(pb):
                tiles = []
                for tt in range(4):
                    g = pb * 4 + tt
                    xt, xb, xs = xt_ring.next()
                    st, stb = st_ring.next()
                    mv, mvb = mv_ring.next()
                    tiles.append((self.x_pre[g * 128:(g + 1) * 128, :], 128, xt, xb, xs, st, stb, mv, mvb))
                self.ln0_stats(tiles)
                for tt, (xr, ntok, xt, xb, xs, st, stb, mv, mvb) in enumerate(tiles):
                    act.op(lambda tt=tt, xt=xt, mv=mv: nc.scalar.activation(out=xn[:, tt, :], in_=xt[:, :], func=AF.Identity,
                                                                            scale=mv[:, 4:5], bias=mv[:, 5:6]),
                           reads=[xb, mvb], writes=[xnb])

            ln0_block(order[0])
            for bi, pb in enumerate(order):
                for kg in range(4):
                    b0, bb = self.banks(2)
                    for kk in range(4):
                        kc = kg * 4 + kk
                        for tt in range(4):
                            last = (kk == 3 and tt == 3)
                            o = self.psb[:, b0 * 1024 + kk * 512 + tt * 128: b0 * 1024 + kk * 512 + (tt + 1) * 128]
                            pe.op(lambda o=o, kc=kc, tt=tt: nc.tensor.transpose(o, xn[:, tt, kc * 128:(kc + 1) * 128],
                                                                                 self.ident_bf[:]),
                                  reads=[xnb, self.misc_b], writes=bb, signal=last)
                    for kk in range(4):
                        kc = kg * 4 + kk
                        src = self.psb[:, b0 * 1024 + kk * 512: b0 * 1024 + (kk + 1) * 512]
                        act.op(lambda src=src, kc=kc: nc.scalar.activation(
                            out=h0p[:, kc, :], in_=src, func=AF.Identity,
                            scale=self.pcol(P_G0, kc), bias=self.pcol(P_B0, kc)),
                            reads=bb + [self.par_b], writes=[h0b[kc]])
                if bi + 1 < len(order):
                    ln0_block(order[bi + 1])

                def v_tiles(tts):
                    for tt in tts:
                        b0, bb = self.banks(2)
                        for half in range(2):
                            w_t, w_b = wts[2 + half]
                            for kc in range(KC):
                                pe.op(lambda half=half, kc=kc, w_t=w_t, tt=tt, b0=b0: nc.tensor.matmul(
                                    self.ps[:, (b0 + half) * 512:(b0 + half + 1) * 512],
                                    lhsT=h0p[:, kc, tt * 128:(tt + 1) * 128], rhs=w_t[:, kc, :],
                                    start=(kc == 0), stop=(kc == KC - 1)),
                                    reads=[h0b[kc], w_b], writes=bb, signal=(half == 1 and kc == KC - 1))
                        dve.op(lambda tt=tt, b0=b0: nc.vector.tensor_copy(vt[:, tt, :], self.ps[:, b0 * 512:(b0 + 2) * 512]),
                               reads=bb, writes=[vb])

                pend_tr = []
                for fg in range(2):
                    js = list(range(fg * NF, (fg + 1) * NF))
                    fb = {}
                    for j in js:
                        w_t, w_b = wts[j // 4]
                        b0, bb = self.banks(1)
                        fb[j] = (b0, bb)
                        for kc in range(KC):
                            pe.op(lambda kc=kc, w_t=w_t, j=j, b0=b0: nc.tensor.matmul(
                                self.ps[:, b0 * 512:(b0 + 1) * 512],
                                lhsT=w_t[:, kc, (j % 4) * 128:(j % 4 + 1) * 128], rhs=h0p[:, kc, :],
                                start=(kc == 0), stop=(kc == KC - 1)),
                                reads=[h0b[kc], w_b], writes=bb, signal=(kc == KC - 1))
                    T_ = {}
                    for j in js:
                        T_[j] = (tsig.next(), ts2.next(), tlf.next(), tcb.next(), wT_ring.next())
                    for j in js:
                        (sig, sigb) = T_[j][0]
                        b0, bb = fb[j]
                        act.op(lambda b0=b0, sig=sig: nc.scalar.activation(out=sig[:], in_=self.ps[:, b0 * 512:(b0 + 1) * 512],
                                                                           func=AF.Sigmoid), reads=bb, writes=[sigb])
                    for j in js:
                        (s2, s2b) = T_[j][1]
                        b0, bb = fb[j]
                        act.op(lambda b0=b0, s2=s2: nc.scalar.activation(out=s2[:], in_=self.ps[:, b0 * 512:(b0 + 1) * 512],
                                                                         func=AF.Sigmoid, scale=-1.0), reads=bb, writes=[s2b])
                    v_tiles([2 * fg, 2 * fg + 1])
                    for j in js:
                        (sig, sigb), _, (lf, lfb) = T_[j][0], None, T_[j][2]
                        act.op(lambda j=j, sig=sig, lf=lf: nc.scalar.activation(out=lf[:], in_=sig[:], func=AF.Ln,
                                                                                scale=self.oml[:, j:j + 1], bias=self.lbv[:, j:j + 1]),
                               reads=[sigb, self.misc_b], writes=[lfb])
                    for j in js:
                        (lf, lfb), (cb_, cbb) = T_[j][2], T_[j][3]
                        dve.op(lambda lf=lf, cb_=cb_: nc.vector.tensor_tensor_scan(cb_[:], self.ones512[:], lf[:], 0.0, ALU.mult, ALU.add),
                               reads=[lfb, self.misc_b], writes=[cbb])
                    for j in js:
                        (cb_, cbb) = T_[j][3]
                        dve.op(lambda j=j, cb_=cb_: nc.vector.tensor_tensor(carry[:, j:j + 1], carry[:, j:j + 1], cb_[:, 511:512], ALU.add),
                               reads=[cbb, carry_b], writes=[carry_b])
                    for j in js:
                        (lf, lfb), (cb_, cbb) = T_[j][2], T_[j][3]
                        act.op(lambda j=j, lf=lf, cb_=cb_: nc.scalar.activation(out=lf[:], in_=cb_[:], func=AF.Exp, scale=-1.0,
                                                                                bias=carry[:, j:j + 1]),
                               reads=[cbb, carry_b], writes=[lfb])
                    for j in js:
                        (s2, s2b), (lf, lfb), (wT, wTb) = T_[j][1], T_[j][2], T_[j][4]
                        dve.op(lambda j=j, s2=s2, lf=lf, wT=wT: nc.vector.scalar_tensor_tensor(wT[:], s2[:], self.oml[:, j:j + 1], lf[:],
                                                                                               ALU.mult, ALU.mult),
                               reads=[s2b, lfb, self.misc_b], writes=[wTb])
                    for j in js:
                        pend_tr.append((j, T_[j][4]))
                for (j, (wT, wTb)) in pend_tr:
                    b1, bb1 = self.banks(1)
                    for tt in range(4):
                        o = self.psb[:, b1 * 1024 + tt * 128: b1 * 1024 + (tt + 1) * 128]
                        pe.op(lambda o=o, tt=tt, wT=wT: nc.tensor.transpose(o, wT[:, tt * 128:(tt + 1) * 128], self.ident_bf[:]),
                              reads=[wTb, self.misc_b], writes=bb1, signal=(tt == 3))
                    src = self.psb[:, b1 * 1024: b1 * 1024 + 512].rearrange("p (a b) -> p a b", a=4)
                    msk = self.pm[:, pb * 4:pb * 4 + 4].unsqueeze(2).to_broadcast([128, 4, 128])
                    dve.op(lambda src=src, msk=msk, j=j: nc.vector.tensor_tensor(wk[:, :, j * 128:(j + 1) * 128], src, msk, ALU.mult),
                           reads=bb1 + [self.par_b], writes=[wkb])
                for hb in range(2):
                    b0, bb = self.banks(1)
                    for hh in range(4):
                        h = hb * 4 + hh
                        for tt in range(4):
                            pe.op(lambda h=h, hh=hh, tt=tt, b0=b0: nc.tensor.matmul(
                                self.ps[:, b0 * 512 + hh * 128: b0 * 512 + (hh + 1) * 128],
                                lhsT=wk[:, tt, h * 128:(h + 1) * 128], rhs=vt[:, tt, h * 128:(h + 1) * 128],
                                start=(tt == 0), stop=(tt == 3)),
                                reads=[wkb, vb], writes=bb, signal=(hh == 3 and tt == 3))
                    sl = self.Sin[:, hb * 4:(hb + 1) * 4, :].rearrange("p a b -> p (a b)")
                    dve.op(lambda sl=sl, b0=b0: nc.vector.tensor_tensor(sl, sl, self.ps[:, b0 * 512:(b0 + 1) * 512], ALU.add),
                           reads=bb + [self.Sin_b], writes=[self.Sin_b])
            dve.op(lambda: nc.vector.tensor_copy(self.Sin_bf[:], self.Sin[:]), reads=[self.Sin_b], writes=[self.Sinbf_b])
            self.dump("Sin", self.Sin[:], [128, 8, 128], self.Sin_b)
            self.barrier()
        self.es_cur = self.es

    def fm_job(self, w_t, w_b, src, src_bufs, lo=0, hi=TB, ncol=128, wcol0=0, origin=0):
        nc = self.nc
        nb = (hi - origin + 511) // 512
        b0, bb = self.banks(nb)
        blks = []
        for i in range(nb):
            a, b = max(origin + 512 * i, lo), min(origin + 512 * (i + 1), hi)
            if a < b:
                blks.append((a, b))
        for bi, (a, b) in enumerate(blks):
            for kc in range(KC):
                self.pe.op(lambda a=a, b=b, kc=kc: nc.tensor.matmul(
                    self.ps[0:ncol, b0 * 512 + a - origin: b0 * 512 + b - origin], lhsT=w_t[:, kc, wcol0:wcol0 + ncol], rhs=src[:, kc, a:b],
                    start=(kc == 0), stop=(kc == KC - 1)),
                    reads=src_bufs + [w_b], writes=bb, signal=(bi == len(blks) - 1 and kc == KC - 1))
        return b0, bb

    def phase_mixer(self):
        nc = self.nc
        pe, act, dve = self.pe, self.act, self.dve
        with ExitStack() as esB:
            self.es_cur = esB
            self.hT = self.sb("hT", [128, KC, TB], BF16)
            self.hT_b = [Buf(f"hT{k}") for k in range(KC)]
            self.cat = self.sb("cat", [128, KC, TB], BF16)
            self.cat_b = [Buf(f"cat{k}") for k in range(KC)]
            with ExitStack() as esA:
                self.es_cur = esA
                wring = self.ring("mw", [128, KC, 128], BF16, 4, dma=True)
                vbf = self.sb("vbf", [128, 9, 1024], BF16)
                vbf_b = [Buf(f"vbf{i}") for i in range(9)]
                dve.op(lambda: nc.vector.memset(vbf[:, 0, :], 0.0), writes=[vbf_b[0]])
                with ExitStack() as es1:
                    self.es_cur = es1
                    xt_ring = self.ring("mxt", [128, D], F32, 4, dma=True)
                    st_ring = self.ring("mst", [128, 4, 6], F32, 4)
                    mv_ring = self.ring("mmv", [128, 8], F32, 4)
                    stg = self.sb("mstg", [128, KC, 512], F32)
                    stgb = [Buf(f"mstg{k}") for k in range(KC)]
                    stgs = self.dsem("d_mstg")
                    groups = [[0], [1, 2, 3, 4], [5, 6, 7, 8]]
                    for grp in groups:
                        tiles = []
                        for ti in grp:
                            tlo, ntok = TILES[ti]
                            xt, xb, xs = xt_ring.next()
                            st, stb = st_ring.next()
                            mv, mvb = mv_ring.next()
                            tiles.append((self.x_own[tlo:tlo + ntok, :], ntok, xt, xb, xs, st, stb, mv, mvb))
                        self.ln0_stats(tiles)
                        for (xr, ntok, xt, xb, xs, st, stb, mv, mvb) in tiles:
                            act.op(lambda xt=xt, mv=mv, ntok=ntok: nc.scalar.activation(
                                out=xt[0:ntok, :], in_=xt[0:ntok, :], func=AF.Identity, scale=mv[0:ntok, 4:5], bias=mv[0:ntok, 5:6]),
                                reads=[xb, mvb], writes=[xb])
                        glo = TILES[grp[0]][0]
                        ncols = sum(TILES[ti][1] for ti in grp)
                        for kc in range(KC):
                            b0, bb = self.banks(1)
                            col = 0
                            for gi, (xr, ntok, xt, xb, xs, st, stb, mv, mvb) in enumerate(tiles):
                                pe.op(lambda kc=kc, b0=b0, col=col, xt=xt, ntok=ntok: nc.tensor.transpose(
                                    self.ps[:, b0 * 512 + col: b0 * 512 + col + ntok],
                                    xt[0:ntok, kc * 128:(kc + 1) * 128], self.ident[0:ntok, 0:ntok]),
                                    reads=[xb, self.cst_b], writes=bb, signal=(gi == len(tiles) - 1))
                                col += ntok
                            act.op(lambda kc=kc, b0=b0: nc.scalar.activation(
                                out=stg[:, kc, 0:ncols], in_=self.ps[:, b0 * 512: b0 * 512 + ncols], func=AF.Identity,
                                scale=self.pcol(P_G0, kc), bias=self.pcol(P_B0, kc)),
                                reads=bb + [self.par_b], writes=[stgb[kc]])
                            dve.op(lambda kc=kc: nc.vector.tensor_copy(self.hT[:, kc, glo:glo + ncols], stg[:, kc, 0:ncols]),
                                   reads=[stgb[kc]], writes=[self.hT_b[kc]])
                        self.sp.dma(self.h0s[:, :, glo:glo + ncols], stg[:, :, 0:ncols], stgs, reads=stgb)
                    self.h0s_sems = [stgs]
                    self.dump("hT0", self.hT[:], [128, KC, TB], self.hT_b[KC - 1])
                    self.barrier()
                self.es_cur = esA
                if self.stop_after == "ln0":
                    self.barrier()
                    self.es_cur = self.es
                    return
                with ExitStack() as es3:
                    self.es_cur = es3
                    iring = self.ring("mwi", [128, KC, 512], BF16, 2, dma=True)
                    wi = [self.load_w(iring, self.w_in[:, 4096 + hf * 512:4096 + (hf + 1) * 512], 512) for hf in range(2)]

                    def v_job(ti):
                        tlo, ntok = TILES[ti]
                        b0, bb = self.banks(2)
                        for half in range(2):
                            w_t, w_b = wi[half]
                            for kc in range(KC):
                                pe.op(lambda half=half, kc=kc, w_t=w_t, b0=b0, tlo=tlo, ntok=ntok: nc.tensor.matmul(
                                    self.ps[0:ntok, (b0 + half) * 512:(b0 + half + 1) * 512],
                                    lhsT=self.hT[:, kc, tlo:tlo + ntok], rhs=w_t[:, kc, :],
                                    start=(kc == 0), stop=(kc == KC - 1)),
                                    reads=[self.hT_b[kc], w_b], writes=bb, signal=(half == 1 and kc == KC - 1))
                        act.op(lambda ti=ti, b0=b0, ntok=ntok: nc.scalar.activation(
                            out=vbf[0:ntok, ti, :], in_=self.ps[0:ntok, b0 * 512:(b0 + 2) * 512], func=AF.Copy),
                            reads=bb, writes=[vbf_b[ti]])

                    self.conv_branch(wring, v_job)
                    self.barrier()
                self.es_cur = esA
                if self.stop_after == "conv":
                    self.es_cur = self.es
                    return
                with ExitStack() as es4:
                    self.es_cur = es4
                    self.hgrn_branch(wring, vbf, vbf_b)
                    self.barrier()
                self.es_cur = esA
            self.es_cur = esB
            self.dump("cat", self.cat[:], [128, KC, TB], self.cat_b[KC - 1])
            if self.stop_after == "hgrn":
                self.barrier()
                self.es_cur = self.es
                return
            self.phase_out_ffn()
        self.es_cur = self.es

    def conv_branch(self, wring, v_job):
        nc = self.nc
        pe, act, dve = self.pe, self.act, self.dve
        t_ring = self.ring("ct", [128, TB], F32, 10)
        NO = TB - 30
        loaded = {}

        def ensure(idx):
            if idx < 16 and idx not in loaded:
                jj, which = idx // 2, idx % 2
                c0 = which * 1024 + jj * 128
                loaded[idx] = self.load_w(wring, self.w_in[:, c0:c0 + 128], 128)

        def cw(j, k):
            return self.par[:, P_CW + j * CK + k: P_CW + j * CK + k + 1]

        def front(j):
            ensure(2 * j + 2)
            ensure(2 * j + 3)
            wa, wab = loaded.pop(2 * j)
            wg, wgb = loaded.pop(2 * j + 1)
            ba, bba = self.fm_job(wa, wab, self.hT, self.hT_b)
            bg, bbg = self.fm_job(wg, wgb, self.hT, self.hT_b)
            sg, sgb = t_ring.next()
            act.op(lambda: nc.scalar.activation(out=sg[:], in_=self.ps[:, bg * 512: bg * 512 + TB], func=AF.Sigmoid),
                   reads=bbg, writes=[sgb])
            u, ub = t_ring.next()
            dve.op(lambda: nc.vector.tensor_tensor(u[:], self.ps[:, ba * 512: ba * 512 + TB], sg[:], ALU.mult),
                   reads=bba + [sgb], writes=[ub])
            dve.op(lambda: nc.vector.tensor_scalar(u[:, 0:HALO], u[:, 0:HALO], self.pcol(P_HM), None, ALU.mult),
                   reads=[ub, self.par_b], writes=[ub])
            acc, accb = t_ring.next()
            hparts = split_buf(accb, 2)
            HB = [(30, 543, hparts[0]), (543, TB, hparts[1])]
            for (ha, hb_, hbuf) in HB:
                act.op(lambda ha=ha, hb_=hb_: nc.scalar.activation(out=acc[:, ha:hb_], in_=u[:, ha - 30:hb_ - 30], func=AF.Identity,
                                                                   scale=cw(j, 0), bias=self.pcol(P_CB, j)),
                       reads=[ub, self.par_b], writes=[hbuf])
            if j == 0:
                self.dump("u0", u[:], [128, TB], ub)
            return dict(j=j, u=u, ub=ub, acc=acc, accb=accb, hparts=hparts, HB=HB)

        def taps(C):
            j, u, ub, acc = C["j"], C["u"], C["ub"], C["acc"]
            for k in range(1, CK):
                for (ha, hb_, hbuf) in C["HB"]:
                    dve.op(lambda k=k, ha=ha, hb_=hb_: nc.vector.scalar_tensor_tensor(
                        acc[:, ha:hb_], u[:, ha - 30 + k:hb_ - 30 + k], cw(j, k), acc[:, ha:hb_], ALU.mult, ALU.add),
                        reads=[ub, hbuf, self.par_b], writes=[hbuf])
            join_buf(C["accb"], C["hparts"])

        def stats(C):
            acc, accb = C["acc"], C["accb"]
            sq, sqb = t_ring.next()
            act.op(lambda: nc.scalar.activation(out=sq[:, 30:TB], in_=acc[:, 30:TB], func=AF.Square), reads=[accb], writes=[sqb])
            bm, bbm = self.stat_mm(self.ones128, acc, accb, 30, TB)
            bq, bbq = self.stat_mm(self.ones128, sq, sqb, 30, TB)
            C["mean"] = self.ps[:, bm * 512 + 30: bm * 512 + TB]
            C["bbm"] = bbm
            ex2 = self.ps[:, bq * 512 + 30: bq * 512 + TB]
            m2, m2b = t_ring.next()
            act.op(lambda: nc.scalar.activation(out=m2[:, 0:NO], in_=C["mean"], func=AF.Square), reads=bbm, writes=[m2b])
            C.update(m2=m2, m2b=m2b, ex2=ex2, bbq=bbq)

        def norm(C):
            j, acc, accb, m2, m2b, mean, bbm = C["j"], C["acc"], C["accb"], C["m2"], C["m2b"], C["mean"], C["bbm"]
            dve.op(lambda: nc.vector.scalar_tensor_tensor(m2[:, 0:NO], C["ex2"], LN_EPS, m2[:, 0:NO], ALU.add, ALU.subtract),
                   reads=C["bbq"] + [m2b], writes=[m2b])
            act.op(lambda: nc.scalar.activation(out=m2[:, 0:NO], in_=m2[:, 0:NO], func=AF.Sqrt), reads=[m2b], writes=[m2b])
            dve.op(lambda: nc.vector.tensor_tensor(acc[:, 30:TB], acc[:, 30:TB], mean, ALU.subtract),
                   reads=bbm + [accb], writes=[accb])
            dve.op(lambda: nc.vector.reciprocal(m2[:, 0:NO], m2[:, 0:NO]), reads=[m2b], writes=[m2b])
            dve.op(lambda: nc.vector.scalar_tensor_tensor(acc[:, 30:TB], acc[:, 30:TB], self.pcol(P_CNG, j), m2[:, 0:NO],
                                                          ALU.mult, ALU.mult),
                   reads=[accb, m2b, self.par_b], writes=[accb])
            act.op(lambda: nc.scalar.activation(out=self.cat[:, j, 30:TB], in_=acc[:, 30:TB], func=AF.Silu,
                                                bias=self.pcol(P_CNB, j)),
                   reads=[accb, self.par_b], writes=[self.cat_b[j]])

        ensure(0)
        ensure(1)
        cur = front(0)
        taps(cur)
        for j in range(8):
            nxt = front(j + 1) if j + 1 < 8 else None
            stats(cur)
            if nxt is not None:
                taps(nxt)
            norm(cur)
            cur = nxt
        for ti in range(9):
            v_job(ti)
        for j in range(8):
            dve.op(lambda j=j: nc.vector.memset(self.cat[:, j, 0:30], 0.0), reads=[self.cat_b[j]], writes=[self.cat_b[j]])

    def stat_mm(self, ones, src, srcb, lo, hi):
        nc = self.nc
        b0, bb = self.banks(3)
        blks = [(max(a, lo), min(b, hi)) for (a, b) in BLOCKS if max(a, lo) < min(b, hi)]
        for bi, (a, b) in enumerate(blks):
            self.pe.op(lambda a=a, b=b: nc.tensor.matmul(self.ps[:, b0 * 512 + a: b0 * 512 + b], lhsT=ones[:], rhs=src[:, a:b],
                                                         start=True, stop=True),
                       reads=[srcb, self.misc_b], writes=bb, signal=(bi == len(blks) - 1))
        return b0, bb

    def rstd_from(self, mean, bbm, ex2, bbq, t_ring, n, eps):
        nc = self.nc
        m2, m2b = t_ring.next()
        self.act.op(lambda: nc.scalar.activation(out=m2[:, 0:n], in_=mean, func=AF.Square), reads=bbm, writes=[m2b])
        self.dve.op(lambda: nc.vector.scalar_tensor_tensor(m2[:, 0:n], ex2, eps, m2[:, 0:n], ALU.add, ALU.subtract),
                    reads=bbq + [m2b], writes=[m2b])
        self.act.op(lambda: nc.scalar.activation(out=m2[:, 0:n], in_=m2[:, 0:n], func=AF.Sqrt), reads=[m2b], writes=[m2b])
        self.dve.op(lambda: nc.vector.reciprocal(m2[:, 0:n], m2[:, 0:n]), reads=[m2b], writes=[m2b])
        return m2, m2b

    def hgrn_branch(self, wring, vbf, vbf_b):
        nc = self.nc
        pe, act, dve = self.pe, self.act, self.dve
        t_ring = self.ring("ht", [128, TB], F32, 7)
        q_ring = self.ring("hq", [128, TB], BF16, 2)
        k_ring = self.ring("hk", [128, TB], BF16, 2)
        ktok_ring = self.ring("hkt", [128, 2, 9, 128], BF16, 2)
        at_ring = self.ring("hat", [128, 9, 128], BF16, 2)
        for i_ in range(2):
            dve.op(lambda i_=i_: nc.vector.memset(ktok_ring.tiles[i_][:, 0, 0, :], 0.0), writes=[ktok_ring.bufs[i_]])
            dve.op(lambda i_=i_: nc.vector.memset(at_ring.tiles[i_][:, 0, :], 0.0), writes=[at_ring.bufs[i_]])
        sbf_ring = self.ring("hsb", [128, 17, 128], BF16, 2)
        sf_ring = self.ring("hsf", [128, 16, 128], F32, 1)
        kvs_ring = self.ring("hkv", [128, 16, 128], F32, 1)
        cols = {"q": 2048, "f": 3072, "og": 5120}
        order = []
        for h in range(8):
            order += [(h, "f"), (h, "q"), (h, "og")]
        loaded = {}
        def ensure(idx):
            if idx < len(order) and idx not in loaded:
                hh, nm = order[idx]
                c0 = cols[nm] + hh * 128
                loaded[idx] = self.load_w(wring, self.w_in[:, c0:c0 + 128], 128)
        for idx in range(3):
            ensure(idx)
        for h in range(8):
            wf_, wfb = loaded[3 * h]
            bf_, bbf = self.fm_job(wf_, wfb, self.hT, self.hT_b)
            ensure(3 * h + 3)
            sig, sigb = t_ring.next()
            act.op(lambda: nc.scalar.activation(out=sig[:], in_=self.ps[:, bf_ * 512: bf_ * 512 + TB], func=AF.Sigmoid),
                   reads=bbf, writes=[sigb])
            kk, kkb = t_ring.next()
            dve.op(lambda: nc.vector.tensor_scalar(kk[:], sig[:], self.noml[:, h:h + 1], self.oml[:, h:h + 1], ALU.mult, ALU.add),
                   reads=[sigb, self.misc_b], writes=[kkb])
            lf, lfb = t_ring.next()
            act.op(lambda: nc.scalar.activation(out=lf[:], in_=sig[:], func=AF.Ln, scale=self.oml[:, h:h + 1],
                                                bias=self.lbv[:, h:h + 1]), reads=[sigb, self.misc_b], writes=[lfb])
            cum, cumb = t_ring.next()
            dve.op(lambda: nc.vector.tensor_tensor_scan(cum[:], self.rmask, lf[:], 0.0, ALU.mult, ALU.add),
                   reads=[lfb, self.cst_b], writes=[cumb])
            eq, eqb = t_ring.next()
            act.op(lambda: nc.scalar.activation(out=eq[:], in_=cum[:], func=AF.Exp), reads=[cumb], writes=[eqb])
            act.op(lambda: nc.scalar.activation(out=lf[:], in_=cum[:], func=AF.Exp, scale=-1.0), reads=[cumb], writes=[lfb])
            kT, kTb = k_ring.next()
            dve.op(lambda: nc.vector.tensor_tensor(kT[:], kk[:], lf[:], ALU.mult), reads=[kkb, lfb], writes=[kTb])
            dve.op(lambda: nc.vector.tensor_scalar(kT[:, 0:HALO], kT[:, 0:HALO], self.pcol(P_HM), None, ALU.mult),
                   reads=[kTb, self.par_b], writes=[kTb])
            wq_, wqb = loaded[3 * h + 1]
            bq_, bbq = self.fm_job(wq_, wqb, self.hT, self.hT_b)
            ensure(3 * h + 4)
            qs, qsb = t_ring.next()
            act.op(lambda: nc.scalar.activation(out=qs[:], in_=self.ps[:, bq_ * 512: bq_ * 512 + TB], func=AF.Silu),
                   reads=bbq, writes=[qsb])
            qT, qTb = q_ring.next()
            dve.op(lambda: nc.vector.tensor_tensor(qT[:], qs[:], eq[:], ALU.mult), reads=[qsb, eqb], writes=[qTb])
            wo_, wob = loaded[3 * h + 2]
            bo_, bbo = self.fm_job(wo_, wob, self.hT, self.hT_b)
            ensure(3 * h + 5)
            ogs, ogsb = t_ring.next()
            act.op(lambda: nc.scalar.activation(out=ogs[:], in_=self.ps[:, bo_ * 512: bo_ * 512 + TB], func=AF.Silu),
                   reads=bbo, writes=[ogsb])
            ktok, ktokb = ktok_ring.next()
            b0, bb = self.banks(2)
            for ti, (tlo, ntok) in enumerate(TILES):
                o = self.psb[0:ntok, b0 * 1024 + ti * 128: b0 * 1024 + (ti + 1) * 128]
                pe.op(lambda o=o, tlo=tlo, ntok=ntok: nc.tensor.transpose(o, kT[:, tlo:tlo + ntok], self.ident_bf[:]),
                      reads=[kTb, self.misc_b], writes=bb, signal=(ti == 8))
            act.op(lambda: nc.scalar.activation(out=ktok[0:32, 0, 0, :], in_=self.psb[0:32, b0 * 1024: b0 * 1024 + 128], func=AF.Copy),
                   reads=bb, writes=[ktokb])
            for ab, mcol in ((0, C_MA), (1, C_MB)):
                act.op(lambda ab=ab, mcol=mcol: nc.scalar.activation(
                    out=ktok[:, ab, 1:9, :].rearrange("p a b -> p (a b)"),
                    in_=self.psb[:, b0 * 1024 + 128: b0 * 1024 + 9 * 128], func=AF.Identity,
                    scale=self.cst[:, mcol:mcol + 1]),
                    reads=bb + [self.cst_b], writes=[ktokb])
            at, atb = at_ring.next()
            b0, bba = self.banks(3)
            for ti, (tlo, ntok) in enumerate(TILES):
                pe.op(lambda ti=ti, tlo=tlo, ntok=ntok: nc.tensor.matmul(
                    self.ps[0:ntok, b0 * 512 + ti * 128: b0 * 512 + ti * 128 + ntok],
                    lhsT=kT[:, tlo:tlo + ntok], rhs=qT[:, tlo:tlo + ntok], start=True, stop=True),
                    reads=[kTb, qTb], writes=bba, signal=(ti == 8))
            dve.op(lambda: nc.vector.tensor_tensor(at[0:32, 0, 0:32], self.ps[0:32, b0 * 512: b0 * 512 + 32], self.cmask[0:32, 0:32],
                                                   ALU.mult), reads=bba + [self.cst_b], writes=[atb])
            dve.op(lambda: nc.vector.tensor_tensor(
                at[:, 1:9, :], self.ps[:, b0 * 512 + 128: b0 * 512 + 9 * 128].rearrange("p (a b) -> p a b", a=8),
                self.cmask.unsqueeze(1).to_broadcast([128, 8, 128]), ALU.mult), reads=bba + [self.cst_b], writes=[atb])
            sbf, sbfb = sbf_ring.next()
            sf, sfb = sf_ring.next()
            kvbanks = []
            for g4 in range(4):
                b0k, bbk = self.banks(1)
                kvbanks.append((b0k, bbk))
                for cc in range(4):
                    c = g4 * 4 + cc
                    ti, plo, n, tlo = CHUNKS[c]
                    pe.op(lambda ti=ti, plo=plo, n=n, cc=cc, b0k=b0k: nc.tensor.matmul(
                        self.ps[:, b0k * 512 + cc * 128: b0k * 512 + (cc + 1) * 128],
                        lhsT=ktok[:, (0 if plo == 0 else 1), ti, :], rhs=vbf[:, ti, h * 128:(h + 1) * 128], start=True, stop=True),
                        reads=[ktokb, vbf_b[ti]], writes=bbk, signal=(cc == 3))
            kvs, kvsb = kvs_ring.next()
            kvs_bufs = split_buf(kvsb, 16)
            for c in range(16):
                b0k, bbk = kvbanks[c // 4]
                kv = self.ps[:, b0k * 512 + (c % 4) * 128: b0k * 512 + (c % 4 + 1) * 128]
                tlo_end = CHUNKS[c][3] + CHUNKS[c][2] - 1
                act.op(lambda c=c, kv=kv, tlo_end=tlo_end: nc.scalar.activation(
                    out=kvs[:, c, :], in_=kv, func=AF.Identity, scale=eq[:, tlo_end:tlo_end + 1]),
                    reads=bbk + [eqb], writes=[kvs_bufs[c]])
            prev = self.Sin[:, h, :]
            prevb = self.Sin_b
            for c in range(16):
                tlo_end = CHUNKS[c][3] + CHUNKS[c][2] - 1
                dve.op(lambda c=c, prev=prev, tlo_end=tlo_end: nc.vector.scalar_tensor_tensor(
                    sf[:, c, :], prev, eq[:, tlo_end:tlo_end + 1], kvs[:, c, :], ALU.mult, ALU.add),
                    reads=[prevb, kvs_bufs[c], eqb], writes=[sfb])
                prev = sf[:, c, :]
                prevb = sfb
            join_buf(kvsb, kvs_bufs)
            act.op(lambda: nc.scalar.activation(out=sbf[:, 1:17, :].rearrange("p a b -> p (a b)"),
                                                in_=sf[:, :, :].rearrange("p a b -> p (a b)"), func=AF.Copy),
                   reads=[sfb], writes=[sbfb])
            b0o, bbo2 = self.banks(3)
            for ti, (tlo, ntok) in enumerate(TILES):
                oc = b0o * 512 + ti * 128
                pe.op(lambda ti=ti, ntok=ntok, oc=oc: nc.tensor.matmul(
                    self.ps[:, oc: oc + ntok], lhsT=vbf[:, ti, h * 128:(h + 1) * 128], rhs=at[:, ti, 0:ntok],
                    start=True, stop=False, skip_group_check=True), reads=[vbf_b[ti], atb], writes=bbo2, signal=False)
                cs = [c for c in range(17) if CHUNKS[c][0] == ti]
                for ci, c in enumerate(cs):
                    _, plo, n, ctlo = CHUNKS[c]
                    lhs = self.Sin_bf[:, h, :] if c == 0 else sbf[:, c, :]
                    rb = [self.Sinbf_b] if c == 0 else [sbfb]
                    pe.op(lambda lhs=lhs, oc=oc, plo=plo, n=n, ctlo=ctlo: nc.tensor.matmul(
                        self.ps[:, oc + plo: oc + plo + n], lhsT=lhs, rhs=qT[:, ctlo:ctlo + n], start=False,
                        stop=True, skip_group_check=True), reads=rb + [qTb], writes=bbo2,
                        signal=(ti == 8 and ci == len(cs) - 1))
            o_sb, osb = t_ring.next()
            act.op(lambda: nc.scalar.activation(out=o_sb[:, 0:32], in_=self.ps[:, b0o * 512: b0o * 512 + 32], func=AF.Copy),
                   reads=bbo2, writes=[osb])
            act.op(lambda: nc.scalar.activation(out=o_sb[:, 32:TB], in_=self.ps[:, b0o * 512 + 128: b0o * 512 + 9 * 128], func=AF.Copy),
                   reads=bbo2, writes=[osb])
            if h == 0:
                self.dump("o0", o_sb[:], [128, TB], osb)
            act.op(lambda: nc.scalar.activation(out=cum[:], in_=o_sb[:], func=AF.Square), reads=[osb], writes=[cumb])
            bm, bbm = self.stat_mm(self.ones128, cum, cumb, 0, TB)
            dve.op(lambda: nc.vector.tensor_scalar(cum[:], self.ps[:, bm * 512: bm * 512 + TB], RMS_EPS, None, ALU.add),
                   reads=bbm, writes=[cumb])
            act.op(lambda: nc.scalar.activation(out=cum[:], in_=cum[:], func=AF.Sqrt), reads=[cumb], writes=[cumb])
            dve.op(lambda: nc.vector.reciprocal(cum[:], cum[:]), reads=[cumb], writes=[cumb])
            dve.op(lambda: nc.vector.scalar_tensor_tensor(o_sb[:], o_sb[:], self.pcol(P_HG, h), cum[:], ALU.mult, ALU.mult),
                   reads=[osb, cumb, self.par_b], writes=[osb])
            dve.op(lambda: nc.vector.tensor_tensor(self.cat[:, 8 + h, :], o_sb[:], ogs[:], ALU.mult),
                   reads=[osb, ogsb], writes=[self.cat_b[8 + h]])

    def ln_begin(self, lo, hi, nsq=2):
        t_ring = self.ring("lnt", [128, TB], F32, 3)
        sq_ring = self.ring("lnq", [128, TB], F32, nsq)
        sy, syb = t_ring.next()
        ss, ssb = t_ring.next()
        return dict(lo=lo, hi=hi, t_ring=t_ring, sq_ring=sq_ring, sy=sy, syb=syb, ss=ss, ssb=ssb)

    def ln_accum(self, L, Y, Yb, m, first):
        nc = self.nc
        act, dve = self.act, self.dve
        lo, hi, sy, syb, ss, ssb = L["lo"], L["hi"], L["sy"], L["syb"], L["ss"], L["ssb"]
        sq, sqb = L["sq_ring"].next()
        act.op(lambda: nc.scalar.activation(out=sq[:, lo:hi], in_=Y[:, m, lo:hi], func=AF.Square),
               reads=[Yb[m]], writes=[sqb])
        if first:
            dve.op(lambda: nc.vector.tensor_copy(sy[:, lo:hi], Y[:, m, lo:hi]), reads=[Yb[m]], writes=[syb])
            dve.op(lambda: nc.vector.tensor_copy(ss[:, lo:hi], sq[:, lo:hi]), reads=[sqb], writes=[ssb])
        else:
            dve.op(lambda: nc.vector.tensor_tensor(sy[:, lo:hi], sy[:, lo:hi], Y[:, m, lo:hi], ALU.add),
                   reads=[Yb[m], syb], writes=[syb])
            dve.op(lambda: nc.vector.tensor_tensor(ss[:, lo:hi], ss[:, lo:hi], sq[:, lo:hi], ALU.add),
                   reads=[sqb, ssb], writes=[ssb])

    def ln_finish(self, L, Y, Yb, g_off, b_off, write_bf=None, mask_halo=False):
        nc = self.nc
        pe, act, dve = self.pe, self.act, self.dve
        lo, hi, sy, syb, ss, ssb = L["lo"], L["hi"], L["sy"], L["syb"], L["ss"], L["ssb"]
        n = hi - lo
        bm, bbm = self.stat_mm(self.onesD, sy, syb, lo, hi)
        bq, bbq = self.stat_mm(self.onesD, ss, ssb, lo, hi)
        mean = self.ps[:, bm * 512 + lo: bm * 512 + hi]
        ex2 = self.ps[:, bq * 512 + lo: bq * 512 + hi]
        rstd, rstdb = self.rstd_from(mean, bbm, ex2, bbq, L["t_ring"], n, LN_EPS)
        dve.op(lambda: nc.vector.scalar_tensor_tensor(sy[:, lo:hi], mean, -1.0, rstd[:, 0:n], ALU.mult, ALU.mult),
               reads=bbm + [rstdb, syb], writes=[syb])
        for m0 in range(0, KC, 4):
            ms = range(m0, m0 + 4)
            for m in ms:
                dve.op(lambda m=m: nc.vector.tensor_tensor(Y[:, m, lo:hi], Y[:, m, lo:hi], rstd[:, 0:n], ALU.mult),
                       reads=[Yb[m], rstdb], writes=[Yb[m]])
            for m in ms:
                dve.op(lambda m=m: nc.vector.tensor_tensor(Y[:, m, lo:hi], Y[:, m, lo:hi], sy[:, lo:hi], ALU.add),
                       reads=[Yb[m], syb], writes=[Yb[m]])
            for m in ms:
                act.op(lambda m=m: nc.scalar.activation(out=Y[:, m, lo:hi], in_=Y[:, m, lo:hi], func=AF.Identity,
                                                        scale=self.pcol(g_off, m), bias=self.pcol(b_off, m)),
                       reads=[Yb[m], self.par_b], writes=[Yb[m]])
            if write_bf is not None:
                wt, wb = write_bf
                for m in ms:
                    dve.op(lambda m=m: nc.vector.tensor_copy(wt[:, m, lo:hi], Y[:, m, lo:hi]), reads=[Yb[m]], writes=[wb[m]])
                if mask_halo:
                    for m in ms:
                        dve.op(lambda m=m: nc.vector.tensor_scalar(wt[:, m, lo:HALO], wt[:, m, lo:HALO], self.pcol(P_HM), None, ALU.mult),
                               reads=[wb[m], self.par_b], writes=[wb[m]])

    def phase_out_ffn(self):
        nc = self.nc
        pe, act, dve = self.pe, self.act, self.dve
        with ExitStack() as esO:
            self.es_cur = esO
            Y = self.sb("Y", [128, KC, TB], F32)
            Yb = [Buf(f"Y{m}") for m in range(KC)]
            wring = self.ring("ow", [128, KC, 128], BF16, 4, dma=True)
            dring = self.ring("od", [128, GC, 128], BF16, 3, dma=True)
            with ExitStack() as e5:
                self.es_cur = e5
                L1 = self.ln_begin(30, TB)
                h0_ring = self.ring("oh0", [128, TB], F32, 2, dma=True)
                loaded = {}
                def ensure(m):
                    if m < KC and m not in loaded:
                        loaded[m] = self.load_w(wring, self.w_out[:, m * 128:(m + 1) * 128], 128)
                for m in range(3):
                    ensure(m)
                for s_ in self.h0s_sems:
                    self.sp.wait(s_, s_.count)
                for m in range(KC):
                    ensure(m + 3)
                    h0t, h0b, h0sem = h0_ring.next()
                    self.sp.dma(h0t[:], self.h0s[:, m, :], h0sem, writes=[h0b])
                    w_t, w_b = loaded[m]
                    b0, bb = self.fm_job(w_t, w_b, self.cat, self.cat_b, lo=30)
                    dve.op(lambda m=m, b0=b0, h0t=h0t: nc.vector.scalar_tensor_tensor(
                        Y[:, m, 30:TB], h0t[:, 30:TB], ALPHA, self.ps[:, b0 * 512 + 30: b0 * 512 + TB], ALU.mult, ALU.add),
                        reads=bb + [h0b], writes=[Yb[m]])
                    self.ln_accum(L1, Y, Yb, m, m == 0)
                self.dump("y1", Y[:], [128, KC, TB], Yb[KC - 1])
                self.ln_finish(L1, Y, Yb, P_G1, P_B1, write_bf=(self.hT, self.hT_b), mask_halo=True)
                self.dump("h1", Y[:], [128, KC, TB], Yb[KC - 1])
                self.barrier()
            self.es_cur = esO
            if self.stop_after == "mixer":
                self.es_cur = self.es
                return
            actb = self.cat[:, :, :].rearrange("p a b -> p (a b)")
            act_bufs = [Buf(f"act{j}") for j in range(GC)]
            with ExitStack() as e7:
                self.es_cur = e7
                t_ring = self.ring("ft", [128, TB], F32, 4)
                L2 = self.ln_begin(HALO, TB, nsq=1)
                jobs = [(g, jj) for g in range(NG) for jj in range(GC)]
                upl = {}
                def ensure_up(idx):
                    if idx < len(jobs) and idx not in upl:
                        j = idx
                        wg = self.load_w(wring, self.w_up[:, j * 128:(j + 1) * 128], 128)
                        wv = self.load_w(wring, self.w_up[:, DFF + j * 128: DFF + (j + 1) * 128], 128)
                        upl[idx] = (wg, wv)
                dnl = {}
                def ensure_dn(g, m):
                    if g < NG and m < KC and (g, m) not in dnl:
                        dnl[(g, m)] = self.load_w(dring, self.w_down[g * GC * 128:(g + 1) * GC * 128, m * 128:(m + 1) * 128], 128, nk=GC)
                ensure_up(0)
                for g in range(NG):
                    for jj in range(GC):
                        idx = g * GC + jj
                        j = idx
                        ensure_up(idx + 1)
                        if jj == GC - 1:
                            ensure_dn(g, 0)
                            ensure_dn(g, 1)
                        (wg, wgb), (wv, wvb) = upl.pop(idx)
                        bg, bbg = self.fm_job(wg, wgb, self.hT, self.hT_b, lo=30, origin=30)
                        bv, bbv = self.fm_job(wv, wvb, self.hT, self.hT_b, lo=HALO, origin=HALO)
                        gs, gsb = t_ring.next()
                        act.op(lambda bg=bg: nc.scalar.activation(out=gs[:, 30:TB], in_=self.ps[:, bg * 512: bg * 512 + TB - 30], func=AF.Copy),
                               reads=bbg, writes=[gsb])
                        c, cb = t_ring.next()
                        fw = lambda k, j=j: self.par[:, P_FW + j * 3 + k: P_FW + j * 3 + k + 1]
                        dve.op(lambda: nc.vector.tensor_scalar(c[:, 0:T], gs[:, 32:TB], fw(2), self.pcol(P_FB, j), ALU.mult, ALU.add),
                               reads=[gsb, self.par_b], writes=[cb])
                        dve.op(lambda: nc.vector.scalar_tensor_tensor(c[:, 0:T], gs[:, 31:TB - 1], fw(1), c[:, 0:T], ALU.mult, ALU.add),
                               reads=[gsb, cb, self.par_b], writes=[cb])
                        dve.op(lambda: nc.vector.scalar_tensor_tensor(c[:, 0:T], gs[:, 30:TB - 2], fw(0), c[:, 0:T], ALU.mult, ALU.add),
                               reads=[gsb, cb, self.par_b], writes=[cb])
                        act.op(lambda: nc.scalar.activation(out=c[:, 0:T], in_=c[:, 0:T], func=AF.Silu), reads=[cb], writes=[cb])
                        dve.op(lambda jj=jj, bv=bv: nc.vector.tensor_tensor(actb[:, jj * T:(jj + 1) * T], c[:, 0:T],
                                                                            self.ps[:, bv * 512: bv * 512 + T], ALU.mult),
                               reads=bbv + [cb], writes=[act_bufs[jj]])
                    for m in range(KC):
                        ensure_dn(g, m + 2)
                        w_t, w_b = dnl.pop((g, m))
                        b0, bb = self.banks(2)
                        for half in range(2):
                            for jj in range(GC):
                                pe.op(lambda half=half, jj=jj, m=m, w_t=w_t, b0=b0: nc.tensor.matmul(
                                    self.ps[:, (b0 + half) * 512:(b0 + half + 1) * 512], lhsT=w_t[:, jj, :],
                                    rhs=actb[:, jj * T + half * 512: jj * T + (half + 1) * 512],
                                    start=(jj == 0), stop=(jj == GC - 1)),
                                    reads=[act_bufs[jj], w_b], writes=bb, signal=(half == 1 and jj == GC - 1))
                        if g == 0:
                            dve.op(lambda m=m, b0=b0: nc.vector.scalar_tensor_tensor(
                                Y[:, m, HALO:TB], Y[:, m, HALO:TB], ALPHA, self.ps[:, b0 * 512:(b0 + 2) * 512], ALU.mult, ALU.add),
                                reads=bb + [Yb[m]], writes=[Yb[m]])
                        else:
                            dve.op(lambda m=m, b0=b0: nc.vector.tensor_tensor(
                                Y[:, m, HALO:TB], Y[:, m, HALO:TB], self.ps[:, b0 * 512:(b0 + 2) * 512], ALU.add),
                                reads=bb + [Yb[m]], writes=[Yb[m]])
                        if g == NG - 1:
                            self.ln_accum(L2, Y, Yb, m, m == 0)
                self.ln_finish(L2, Y, Yb, P_G2, P_B2)
                self.barrier()
            self.es_cur = esO
            with ExitStack() as e9:
                self.es_cur = e9
                o_ring = self.ring("oo", [128, D], F32, 2, dma=True)
                for ti in range(8):
                    ot, otb, osem = o_ring.next()
                    for q in range(4):
                        b0, bb = self.banks(1)
                        for kk in range(4):
                            kc = q * 4 + kk
                            pe.op(lambda kc=kc, kk=kk, b0=b0, ti=ti: nc.tensor.transpose(
                                self.ps[:, b0 * 512 + kk * 128: b0 * 512 + (kk + 1) * 128],
                                Y[:, kc, HALO + ti * 128: HALO + (ti + 1) * 128], self.ident),
                                reads=[Yb[kc], self.cst_b], writes=bb, signal=(kk == 3))
                        if q % 2 == 0:
                            act.op(lambda q=q, b0=b0, ot=ot: nc.scalar.activation(out=ot[:, q * 512:(q + 1) * 512],
                                                                                  in_=self.ps[:, b0 * 512:(b0 + 1) * 512], func=AF.Copy),
                                   reads=bb, writes=[otb])
                        else:
                            dve.op(lambda q=q, b0=b0, ot=ot: nc.vector.tensor_copy(ot[:, q * 512:(q + 1) * 512], self.ps[:, b0 * 512:(b0 + 1) * 512]),
                                   reads=bb, writes=[otb])
                    self.sp.dma(self.out_d[ti * 128:(ti + 1) * 128, :], ot[:], osem, reads=[otb])
                self.final_waits.extend(o_ring.sems)
                self.barrier()
            self.es_cur = esO
        self.es_cur = self.es


def _pack_params(inp, hmask):
    P = np.zeros((128, NPP), np.float32)
    fm = lambda v, n: np.ascontiguousarray(np.asarray(v, np.float32).reshape(n, 128).T)
    P[:, P_G0:P_G0 + 16] = fm(inp["emb_ln_g"], 16)
    P[:, P_B0:P_B0 + 16] = fm(inp["emb_ln_b"], 16)
    cw = np.asarray(inp["conv_w"], np.float32)[0]
    P[:, P_CW:P_CW + 248] = cw.reshape(CK, 8, 128).transpose(2, 1, 0).reshape(128, 248)
    P[:, P_CB:P_CB + 8] = fm(inp["conv_b"][0], 8)
    P[:, P_CNG:P_CNG + 8] = fm(inp["conv_norm_g"][0], 8)
    P[:, P_CNB:P_CNB + 8] = fm(inp["conv_norm_b"][0], 8)
    lbl = np.asarray(inp["lb_logits"], np.float32)
    P[:, P_LBL:P_LBL + 16] = lbl.reshape(2, 8, 128).transpose(2, 1, 0).reshape(128, 16)
    P[:, P_HG:P_HG + 8] = fm(inp["hgrn_norm_g"][0], 8)
    P[:, P_G1:P_G1 + 16] = fm(inp["ln1_g"][0], 16)
    P[:, P_B1:P_B1 + 16] = fm(inp["ln1_b"][0], 16)
    fw = np.asarray(inp["ffn_conv_w"], np.float32)[0]
    P[:, P_FW:P_FW + 132] = fw.reshape(3, FC, 128).transpose(2, 1, 0).reshape(128, 132)
    P[:, P_FB:P_FB + FC] = fm(inp["ffn_conv_b"][0], FC)
    P[:, P_G2:P_G2 + 16] = fm(inp["ln2_g"][0], 16)
    P[:, P_B2:P_B2 + 16] = fm(inp["ln2_b"][0], 16)
    P[:, P_HM] = hmask
    return P


def _consts():
    C = np.zeros((128, NCC), np.float32)
    C[:, C_ID:C_ID + 128] = np.eye(128, dtype=np.float32)
    s = np.arange(128)[:, None]
    t = np.arange(128)[None, :]
    C[:, C_CM:C_CM + 128] = ((s // 64 == t // 64) & (s <= t)).astype(np.float32)
    rm = np.ones(TB, np.float32)
    for (_, _, _, tlo) in CHUNKS:
        rm[tlo] = 0.0
    C[:, C_RM:C_RM + TB] = rm[None, :]
    C[0:64, C_MA] = 1.0
    C[64:128, C_MB] = 1.0
    return C


_CACHE = {}


def make_in_maps(inputs, cores=range(8)):
    x = np.asarray(inputs["x"], np.float32)
    w_in = np.ascontiguousarray(np.asarray(inputs["w_in"], np.float32)[0])
    w_out = np.ascontiguousarray(np.asarray(inputs["w_out"], np.float32)[0])
    w_up = np.ascontiguousarray(np.asarray(inputs["w_ffn_up"], np.float32)[0])
    w_down = np.ascontiguousarray(np.asarray(inputs["w_ffn_down"], np.float32)[0])
    consts = _consts()
    maps = []
    for c in cores:
        b, p = c // 4, c % 4
        t0 = p * T
        xo = np.zeros((TB, D), np.float32)
        lo = t0 - HALO
        if lo >= 0:
            xo[:] = x[b, lo:t0 + T]
        else:
            xo[HALO:] = x[b, 0:T]
        xp = np.zeros((NPRE, D), np.float32)
        pm = np.zeros(NPRE, np.float32)
        end = t0 - HALO
        nval = max(0, end)
        if nval > 0:
            xp[NPRE - nval:] = x[b, 0:end]
            pm[NPRE - nval:] = 1.0
        pmask = np.ascontiguousarray(pm.reshape(NPRE // 128, 128).T)
        maps.append({
            "x_own": xo, "x_pre": xp, "pmask": pmask,
            "params": _pack_params(inputs, 0.0 if p == 0 else 1.0), "consts": consts,
            "w_in": w_in, "w_out": w_out, "w_up": w_up, "w_down": w_down,
        })
    return maps


def kernel(**inputs):
    if "nc" not in _CACHE:
        _CACHE["nc"] = KB().build()
    nc = _CACHE["nc"]
    maps = make_in_maps(inputs)
    res = run_bass_kernel_spmd(nc, maps, core_ids=list(range(8)))
    out = np.zeros((2, 4 * T, D), np.float32)
    for c in range(8):
        b, p = c // 4, c % 4
        out[b, p * T:(p + 1) * T] = res.results[c]["out"]
    return out
```

```python
import numpy as np
from contextlib import ExitStack
import concourse.bass as bass
import concourse.mybir as mybir
from concourse.bass_utils import run_bass_kernel_spmd

F32 = mybir.dt.float32
BF16 = mybir.dt.bfloat16
AF = mybir.ActivationFunctionType
ALU = mybir.AluOpType

D = 2048
KC = 16
T = 1024
HALO = 32
TB = T + HALO
NPRE = 3072
DFF = 5632
FC = 44
NG = 4
GC = FC // NG
ALPHA = 2.0 ** 0.25
LN_EPS = 1e-5
RMS_EPS = 1e-6
CK = 31

P_G0, P_B0, P_CW, P_CB, P_CNG, P_CNB, P_LBL, P_HG = 0, 16, 32, 280, 288, 296, 304, 320
P_G1, P_B1, P_FW, P_FB, P_G2, P_B2, P_HM = 328, 344, 360, 492, 536, 552, 568
NPP = 576
C_ID, C_CM, C_RM = 0, 128, 256
C_MA = 256 + TB
C_MB = C_MA + 1
NCC = 256 + TB + 2

BLOCKS = [(0, 512), (512, 1024), (1024, TB)]
TILES = [(0, 32)] + [(32 + 128 * i, 128) for i in range(8)]
CHUNKS = [(0, 0, 32, 0)]
for _i in range(8):
    CHUNKS.append((_i + 1, 0, 64, 32 + 128 * _i))
    CHUNKS.append((_i + 1, 64, 64, 32 + 128 * _i + 64))


import os
SAME_ENGINE_WAITS = True
POOL_TAPS = int(os.environ.get("POOL_TAPS", "0"))


class Sem:
    def __init__(self, nc, es, name):
        self.h = es.enter_context(nc.semaphore(name))
        self.count = 0
        self.name = name


class Buf:
    __slots__ = ("name", "w", "r", "excl")

    def __init__(self, name, excl=False):
        self.name = name
        self.w = {}
        self.r = {}
        self.excl = excl


def _merge(d, src):
    for k, v in src.items():
        if d.get(k, 0) < v:
            d[k] = v


def split_buf(buf, n):
    parts = []
    for i in range(n):
        p = Buf(f"{buf.name}.{i}", buf.excl)
        p.w = dict(buf.w)
        p.r = dict(buf.r)
        parts.append(p)
    return parts


def join_buf(buf, parts):
    buf.w = {}
    buf.r = {}
    for p in parts:
        _merge(buf.w, p.w)
        _merge(buf.r, p.r)


class Eng:
    def __init__(self, kb, name, beng, is_pe=False):
        self.kb = kb
        self.name = name
        self.e = beng
        self.sem = Sem(kb.nc, kb.es, "s_" + name)
        self.known = {}
        self.is_pe = is_pe
        self.pending = False

    def wait(self, sem, val):
        if val <= 0 or self.known.get(sem, 0) >= val:
            return
        self.e.wait_ge(sem.h, val)
        self.known[sem] = val

    def _deps(self, reads, writes):
        deps = {}
        for b in reads:
            _merge(deps, b.w)
            if b.excl:
                for k, v in b.r.items():
                    if k is not self.sem and deps.get(k, 0) < v:
                        deps[k] = v
        for b in writes:
            _merge(deps, b.w)
            _merge(deps, b.r)
        for sem, val in deps.items():
            if sem is self.sem:
                if self.is_pe or not SAME_ENGINE_WAITS:
                    continue
                if val > sem.count:
                    continue
            self.wait(sem, val)

    def op(self, fn, reads=(), writes=(), signal=True):
        self._deps(reads, writes)
        inst = fn()
        val = self.sem.count + 1
        if signal:
            self.sem.count = val
            inst.then_inc(self.sem.h, 1)
            self.pending = False
        else:
            self.pending = True
        for b in reads:
            if b.r.get(self.sem, 0) < val:
                b.r[self.sem] = val
        for b in writes:
            b.w = {self.sem: val}
            b.r = {}
        return inst

    def dma(self, out, in_, sem, reads=(), writes=()):
        self._deps(reads, writes)
        self.wait(sem, sem.count)
        inst = self.e.dma_start(out=out, in_=in_)
        sem.count += 16
        inst.then_inc(sem.h, 16)
        for b in reads:
            b.r[sem] = sem.count
        for b in writes:
            b.w = {sem: sem.count}
            b.r = {}
        return inst


class Ring:
    def __init__(self, kb, name, shape, dtype, n, dma=False):
        self.tiles = [kb.sb(f"{name}{i}", shape, dtype) for i in range(n)]
        self.bufs = [Buf(f"{name}{i}") for i in range(n)]
        self.sems = [Sem(kb.nc, kb.es, f"d_{name}{i}") for i in range(n)] if dma else None
        self.i = 0
        self.n = n

    def next(self):
        k = self.i % self.n
        self.i += 1
        if self.sems:
            return self.tiles[k], self.bufs[k], self.sems[k]
        return self.tiles[k], self.bufs[k]


class KB:
    def __init__(self, dbg=()):
        self.dbg = list(dbg)
        self.dbg_out = {}

    def sb(self, name, shape, dtype):
        self._uid = getattr(self, "_uid", 0) + 1
        return self.es_cur.enter_context(self.nc.sbuf_tensor(f"{name}_{self._uid}", list(shape), dtype))

    def banks(self, n):
        if self.bank_ptr + n > 8:
            self.bank_ptr = 0
        b0 = self.bank_ptr
        self.bank_ptr = (self.bank_ptr + n) % 8
        return b0, self.bank_bufs[b0:b0 + n]

    def barrier(self):
        sems = [e.sem for e in self.engs] + self.all_dma_sems
        for e in self.engs:
            assert not e.pending, e.name
            for s in sems:
                if s is e.sem:
                    continue
                e.wait(s, s.count)

    def dsem(self, name):
        s = Sem(self.nc, self.es, name)
        self.all_dma_sems.append(s)
        return s

    def ring(self, name, shape, dtype, n, dma=False):
        r = Ring(self, name, shape, dtype, n, dma)
        if dma:
            self.all_dma_sems.extend(r.sems)
        return r

    def dump(self, name, tile_ap, shape, buf):
        if name not in self.dbg:
            return
        dt = tile_ap.dtype
        o = self.nc.dram_tensor("dbg_" + name, list(shape), dt, kind="ExternalOutput").ap()
        self.dbg_out[name] = (list(shape), dt)
        s = self.dsem("dbg_" + name)
        idx = tuple(slice(None) for _ in shape)
        self.sp.dma(o[idx], tile_ap, s, reads=[buf])
        self.final_waits.append(s)

    def build(self, stop_after=None):
        nc = bass.Bass("TRN2", target_bir_lowering=False)
        self.nc = nc
        self.stop_after = stop_after
        dr = lambda name, shape, kind="ExternalInput", dt=F32: nc.dram_tensor(name, list(shape), dt, kind=kind).ap()
        self.x_own = dr("x_own", [TB, D])
        self.x_pre = dr("x_pre", [NPRE, D])
        self.pmask_d = dr("pmask", [128, NPRE // 128])
        self.params_d = dr("params", [128, NPP])
        self.consts_d = dr("consts", [128, NCC])
        self.w_in = dr("w_in", [D, 6144])
        self.w_out = dr("w_out", [D, D])
        self.w_up = dr("w_up", [D, 2 * DFF])
        self.w_down = dr("w_down", [DFF, D])
        self.out_d = dr("out", [T, D], kind="ExternalOutput")
        self.h0s = nc.dram_tensor("h0s", [128, KC, TB], F32).ap()
        self.final_waits = []
        self.all_dma_sems = []
        with ExitStack() as es:
            self.es = es
            self.es_cur = es
            self.pe = Eng(self, "pe", nc.tensor, is_pe=True)
            self.act = Eng(self, "act", nc.scalar)
            self.dve = Eng(self, "dve", nc.vector)
            self.gp = Eng(self, "gp", nc.gpsimd)
            self.sp = Eng(self, "sp", nc.sync)
            self.engs = [self.pe, self.act, self.dve, self.gp, self.sp]
            self.ps = es.enter_context(nc.psum_tensor("ps", [128, 8 * 512], F32))
            self.psb = self.ps[:, :].bitcast(BF16)
            self.bank_bufs = [Buf(f"bank{i}", excl=True) for i in range(8)]
            self.bank_ptr = 0
            self.setup()
            if stop_after != "setup":
                self.run_phases()
            for s in self.final_waits:
                self.sp.wait(s, s.count)
            for e in self.engs:
                assert not e.pending, e.name
        return nc

    def setup(self):
        nc = self.nc
        self.par = self.sb("par", [128, NPP], F32)
        self.par_b = Buf("par")
        self.cst = self.sb("cst", [128, NCC], F32)
        self.cst_b = Buf("cst")
        self.pm = self.sb("pm", [128, NPRE // 128], F32)
        s = self.dsem("ld_setup")
        self.sp.dma(self.par[:], self.params_d[:, :], s, writes=[self.par_b])
        self.sp.dma(self.cst[:], self.consts_d[:, :], s, writes=[self.cst_b])
        self.sp.dma(self.pm[:], self.pmask_d[:, :], s, writes=[self.par_b])
        self.ident = self.cst[:, C_ID:C_ID + 128]
        self.cmask = self.cst[:, C_CM:C_CM + 128]
        self.rmask = self.cst[:, C_RM:C_RM + TB]
        self.ident_bf = self.sb("ident_bf", [128, 128], BF16)
        self.ones128 = self.sb("ones128", [128, 128], F32)
        self.onesD = self.sb("onesD", [128, 128], F32)
        self.ones512 = self.sb("ones512", [128, 512], F32)
        self.lbv = self.sb("lbv", [128, 8], F32)
        self.oml = self.sb("oml", [128, 8], F32)
        self.noml = self.sb("noml", [128, 8], F32)
        self.misc_b = Buf("misc")
        self.Sin = self.sb("Sin", [128, 8, 128], F32)
        self.Sin_b = Buf("Sin")
        self.Sin_bf = self.sb("Sin_bf", [128, 8, 128], BF16)
        self.Sinbf_b = Buf("Sin_bf")
        d = self.dve
        d.op(lambda: nc.vector.tensor_copy(self.ident_bf[:], self.ident), reads=[self.cst_b], writes=[self.misc_b])
        d.op(lambda: nc.vector.memset(self.ones128[:], 1.0 / 128.0), writes=[self.misc_b])
        d.op(lambda: nc.vector.memset(self.onesD[:], 1.0 / D), writes=[self.misc_b])
        d.op(lambda: nc.vector.memset(self.ones512[:], 1.0), writes=[self.misc_b])
        d.op(lambda: nc.vector.memset(self.Sin[:], 0.0), writes=[self.Sin_b])
        lbl = self.par[:, P_LBL:P_LBL + 16].rearrange("p (j two) -> p j two", two=2)
        d.op(lambda: nc.vector.tensor_tensor(self.noml[:], lbl[:, :, 0], lbl[:, :, 1], ALU.subtract),
             reads=[self.par_b], writes=[self.misc_b])
        self.act.op(lambda: nc.scalar.activation(out=self.lbv[:], in_=self.noml[:], func=AF.Sigmoid),
                    reads=[self.misc_b], writes=[self.misc_b])
        d.op(lambda: nc.vector.tensor_scalar(self.oml[:], self.lbv[:], -1.0, 1.0, ALU.mult, ALU.add),
             reads=[self.misc_b], writes=[self.misc_b])
        d.op(lambda: nc.vector.tensor_scalar(self.noml[:], self.oml[:], -1.0, None, ALU.mult),
             reads=[self.misc_b], writes=[self.misc_b])

    def pcol(self, off, j=0):
        return self.par[:, off + j:off + j + 1]

    def load_w(self, ring, dram_ap_2d, ncols, nk=KC):
        t, b, s = ring.next()
        self.gp.dma(t[:, 0:nk, 0:ncols], dram_ap_2d.rearrange("(kc p) c -> p kc c", p=128), s, writes=[b])
        return t, b

    def ln0_stats(self, tiles):
        nc = self.nc
        for (xr, ntok, xt, xb, xs, st, stb, mv, mvb) in tiles:
            self.sp.dma(xt[0:ntok, :], xr, xs, writes=[xb])
        for (xr, ntok, xt, xb, xs, st, stb, mv, mvb) in tiles:
            for q in range(4):
                self.dve.op(lambda q=q, st=st, xt=xt, ntok=ntok: nc.vector.bn_stats(st[0:ntok, q, :], xt[0:ntok, q * 512:(q + 1) * 512]),
                            reads=[xb], writes=[stb])
        for (xr, ntok, xt, xb, xs, st, stb, mv, mvb) in tiles:
            self.dve.op(lambda st=st, mv=mv, ntok=ntok: nc.vector.bn_aggr(mv[0:ntok, 0:2], st[0:ntok, :, :].rearrange("p a b -> p (a b)")),
                        reads=[stb], writes=[mvb])
        for (xr, ntok, xt, xb, xs, st, stb, mv, mvb) in tiles:
            self.dve.op(lambda mv=mv, ntok=ntok: nc.vector.tensor_scalar(mv[0:ntok, 2:3], mv[0:ntok, 1:2], LN_EPS, None, ALU.add),
                        reads=[mvb], writes=[mvb])
        for (xr, ntok, xt, xb, xs, st, stb, mv, mvb) in tiles:
            self.act.op(lambda mv=mv, ntok=ntok: nc.scalar.activation(out=mv[0:ntok, 3:4], in_=mv[0:ntok, 2:3], func=AF.Sqrt),
                        reads=[mvb], writes=[mvb])
        for (xr, ntok, xt, xb, xs, st, stb, mv, mvb) in tiles:
            self.dve.op(lambda mv=mv, ntok=ntok: nc.vector.reciprocal(mv[0:ntok, 4:5], mv[0:ntok, 3:4]), reads=[mvb], writes=[mvb])
        for (xr, ntok, xt, xb, xs, st, stb, mv, mvb) in tiles:
            self.dve.op(lambda mv=mv, ntok=ntok: nc.vector.tensor_scalar(mv[0:ntok, 5:6], mv[0:ntok, 0:1], -1.0, mv[0:ntok, 4:5],
                                                                         ALU.mult, ALU.mult), reads=[mvb], writes=[mvb])

    def run_phases(self):
        self.phase_prefix()
        if self.stop_after == "prefix":
            return
        self.phase_mixer()

    def phase_prefix(self):
        nc = self.nc
        pe, act, dve = self.pe, self.act, self.dve
        with ExitStack() as es:
            self.es_cur = es
            wring = self.ring("pw", [128, KC, 512], BF16, 4, dma=True)
            wts = []
            for c0 in (3072, 3072 + 512, 4096, 4096 + 512):
                wts.append(self.load_w(wring, self.w_in[:, c0:c0 + 512], 512))
            xt_ring = self.ring("pxt", [128, D], F32, 4, dma=True)
            st_ring = self.ring("pst", [128, 4, 6], F32, 4)
            mv_ring = self.ring("pmv", [128, 8], F32, 4)
            xn = self.sb("pxn", [128, 4, D], BF16)
            xnb = Buf("pxn")
            h0p = self.sb("ph0", [128, KC, 512], BF16)
            h0b = [Buf(f"ph0_{k}") for k in range(KC)]
            NF = 4
            tsig = self.ring("psg", [128, 512], F32, NF)
            ts2 = self.ring("ps2", [128, 512], F32, NF)
            tlf = self.ring("plf", [128, 512], F32, NF)
            tcb = self.ring("pcb", [128, 512], F32, NF)
            wT_ring = self.ring("pwT", [128, 512], BF16, 2 * NF)
            wk = self.sb("pwk", [128, 4, 1024], BF16)
            wkb = Buf("pwk")
            vt = self.sb("pv", [128, 4, 1024], BF16)
            vb = Buf("pv")
            carry = self.sb("pcarry", [128, 8], F32)
            carry_b = Buf("carry")
            dve.op(lambda: nc.vector.memset(carry[:], 0.0), writes=[carry_b])
            order = list(range(NPRE // 512 - 1, -1, -1))

            def ln0_block(pb):
                tiles = []
                for tt in range(4):
                    g = pb * 4 + tt
                    xt, xb, xs = xt_ring.next()
                    st, stb = st_ring.next()
                    mv, mvb = mv_ring.next()
                    tiles.append((self.x_pre[g * 128:(g + 1) * 128, :], 128, xt, xb, xs, st, stb, mv, mvb))
                self.ln0_stats(tiles)
                for tt, (xr, ntok, xt, xb, xs, st, stb, mv, mvb) in enumerate(tiles):
                    act.op(lambda tt=tt, xt=xt, mv=mv: nc.scalar.activation(out=xn[:, tt, :], in_=xt[:, :], func=AF.Identity,
                                                                            scale=mv[:, 4:5], bias=mv[:, 5:6]),
                           reads=[xb, mvb], writes=[xnb])

            ln0_block(order[0])
            for bi, pb in enumerate(order):
                for kg in range(4):
                    b0, bb = self.banks(2)
                    for kk in range(4):
                        kc = kg * 4 + kk
                        for tt in range(4):
                            last = (kk == 3 and tt == 3)
                            o = self.psb[:, b0 * 1024 + kk * 512 + tt * 128: b0 * 1024 + kk * 512 + (tt + 1) * 128]
                            pe.op(lambda o=o, kc=kc, tt=tt: nc.tensor.transpose(o, xn[:, tt, kc * 128:(kc + 1) * 128],
                                                                                 self.ident_bf[:]),
                                  reads=[xnb, self.misc_b], writes=bb, signal=last)
                    for kk in range(4):
                        kc = kg * 4 + kk
                        src = self.psb[:, b0 * 1024 + kk * 512: b0 * 1024 + (kk + 1) * 512]
                        act.op(lambda src=src, kc=kc: nc.scalar.activation(
                            out=h0p[:, kc, :], in_=src, func=AF.Identity,
                            scale=self.pcol(P_G0, kc), bias=self.pcol(P_B0, kc)),
                            reads=bb + [self.par_b], writes=[h0b[kc]])
                if bi + 1 < len(order):
                    ln0_block(order[bi + 1])

                def v_tiles(tts):
                    for tt in tts:
                        b0, bb = self.banks(2)
                        for half in range(2):
                            w_t, w_b = wts[2 + half]
                            for kc in range(KC):
                                pe.op(lambda half=half, kc=kc, w_t=w_t, tt=tt, b0=b0: nc.tensor.matmul(
                                    self.ps[:, (b0 + half) * 512:(b0 + half + 1) * 512],
                                    lhsT=h0p[:, kc, tt * 128:(tt + 1) * 128], rhs=w_t[:, kc, :],
                                    start=(kc == 0), stop=(kc == KC - 1)),
                                    reads=[h0b[kc], w_b], writes=bb, signal=(half == 1 and kc == KC - 1))
                        dve.op(lambda tt=tt, b0=b0: nc.vector.tensor_copy(vt[:, tt, :], self.ps[:, b0 * 512:(b0 + 2) * 512]),
                               reads=bb, writes=[vb])

                pend_tr = []
                for fg in range(2):
                    js = list(range(fg * NF, (fg + 1) * NF))
                    fb = {}
                    for j in js:
                        w_t, w_b = wts[j // 4]
                        b0, bb = self.banks(1)
                        fb[j] = (b0, bb)
                        for kc in range(KC):
                            pe.op(lambda kc=kc, w_t=w_t, j=j, b0=b0: nc.tensor.matmul(
                                self.ps[:, b0 * 512:(b0 + 1) * 512],
                                lhsT=w_t[:, kc, (j % 4) * 128:(j % 4 + 1) * 128], rhs=h0p[:, kc, :],
                                start=(kc == 0), stop=(kc == KC - 1)),
                                reads=[h0b[kc], w_b], writes=bb, signal=(kc == KC - 1))
                    T_ = {}
                    for j in js:
                        T_[j] = (tsig.next(), ts2.next(), tlf.next(), tcb.next(), wT_ring.next())
                    for j in js:
                        (sig, sigb) = T_[j][0]
                        b0, bb = fb[j]
                        act.op(lambda b0=b0, sig=sig: nc.scalar.activation(out=sig[:], in_=self.ps[:, b0 * 512:(b0 + 1) * 512],
                                                                           func=AF.Sigmoid), reads=bb, writes=[sigb])
                    for j in js:
                        (s2, s2b) = T_[j][1]
                        b0, bb = fb[j]
                        act.op(lambda b0=b0, s2=s2: nc.scalar.activation(out=s2[:], in_=self.ps[:, b0 * 512:(b0 + 1) * 512],
                                                                         func=AF.Sigmoid, scale=-1.0), reads=bb, writes=[s2b])
                    v_tiles([2 * fg, 2 * fg + 1])
                    for j in js:
                        (sig, sigb), _, (lf, lfb) = T_[j][0], None, T_[j][2]
                        act.op(lambda j=j, sig=sig, lf=lf: nc.scalar.activation(out=lf[:], in_=sig[:], func=AF.Ln,
                                                                                scale=self.oml[:, j:j + 1], bias=self.lbv[:, j:j + 1]),
                               reads=[sigb, self.misc_b], writes=[lfb])
                    for j in js:
                        (lf, lfb), (cb_, cbb) = T_[j][2], T_[j][3]
                        dve.op(lambda lf=lf, cb_=cb_: nc.vector.tensor_tensor_scan(cb_[:], self.ones512[:], lf[:], 0.0, ALU.mult, ALU.add),
                               reads=[lfb, self.misc_b], writes=[cbb])
                    for j in js:
                        (cb_, cbb) = T_[j][3]
                        dve.op(lambda j=j, cb_=cb_: nc.vector.tensor_tensor(carry[:, j:j + 1], carry[:, j:j + 1], cb_[:, 511:512], ALU.add),
                               reads=[cbb, carry_b], writes=[carry_b])
                    for j in js:
                        (lf, lfb), (cb_, cbb) = T_[j][2], T_[j][3]
                        act.op(lambda j=j, lf=lf, cb_=cb_: nc.scalar.activation(out=lf[:], in_=cb_[:], func=AF.Exp, scale=-1.0,
                                                                                bias=carry[:, j:j + 1]),
                               reads=[cbb, carry_b], writes=[lfb])
                    for j in js:
                        (s2, s2b), (lf, lfb), (wT, wTb) = T_[j][1], T_[j][2], T_[j][4]
                        dve.op(lambda j=j, s2=s2, lf=lf, wT=wT: nc.vector.scalar_tensor_tensor(wT[:], s2[:], self.oml[:, j:j + 1], lf[:],
                                                                                               ALU.mult, ALU.mult),
                               reads=[s2b, lfb, self.misc_b], writes=[wTb])
                    for j in js:
                        pend_tr.append((j, T_[j][4]))
                for (j, (wT, wTb)) in pend_tr:
                    b1, bb1 = self.banks(1)
                    for tt in range(4):
                        o = self.psb[:, b1 * 1024 + tt * 128: b1 * 1024 + (tt + 1) * 128]
                        pe.op(lambda o=o, tt=tt, wT=wT: nc.tensor.transpose(o, wT[:, tt * 128:(tt + 1) * 128], self.ident_bf[:]),
                              reads=[wTb, self.misc_b], writes=bb1, signal=(tt == 3))
                    src = self.psb[:, b1 * 1024: b1 * 1024 + 512].rearrange("p (a b) -> p a b", a=4)
                    msk = self.pm[:, pb * 4:pb * 4 + 4].unsqueeze(2).to_broadcast([128, 4, 128])
                    dve.op(lambda src=src, msk=msk, j=j: nc.vector.tensor_tensor(wk[:, :, j * 128:(j + 1) * 128], src, msk, ALU.mult),
                           reads=bb1 + [self.par_b], writes=[wkb])
                for hb in range(2):
                    b0, bb = self.banks(1)
                    for hh in range(4):
                        h = hb * 4 + hh
                        for tt in range(4):
                            pe.op(lambda h=h, hh=hh, tt=tt, b0=b0: nc.tensor.matmul(
                                self.ps[:, b0 * 512 + hh * 128: b0 * 512 + (hh + 1) * 128],
                                lhsT=wk[:, tt, h * 128:(h + 1) * 128], rhs=vt[:, tt, h * 128:(h + 1) * 128],
                                start=(tt == 0), stop=(tt == 3)),
                                reads=[wkb, vb], writes=bb, signal=(hh == 3 and tt == 3))
                    sl = self.Sin[:, hb * 4:(hb + 1) * 4, :].rearrange("p a b -> p (a b)")
                    dve.op(lambda sl=sl, b0=b0: nc.vector.tensor_tensor(sl, sl, self.ps[:, b0 * 512:(b0 + 1) * 512], ALU.add),
                           reads=bb + [self.Sin_b], writes=[self.Sin_b])
            dve.op(lambda: nc.vector.tensor_copy(self.Sin_bf[:], self.Sin[:]), reads=[self.Sin_b], writes=[self.Sinbf_b])
            self.dump("Sin", self.Sin[:], [128, 8, 128], self.Sin_b)
            self.barrier()
        self.es_cur = self.es

    def fm_job(self, w_t, w_b, src, src_bufs, lo=0, hi=TB, ncol=128, wcol0=0, origin=0):
        nc = self.nc
        nb = (hi - origin + 511) // 512
        b0, bb = self.banks(nb)
        blks = []
        for i in range(nb):
            a, b = max(origin + 512 * i, lo), min(origin + 512 * (i + 1), hi)
            if a < b:
                blks.append((a, b))
        for bi, (a, b) in enumerate(blks):
            for kc in range(KC):
                self.pe.op(lambda a=a, b=b, kc=kc: nc.tensor.matmul(
                    self.ps[0:ncol, b0 * 512 + a - origin: b0 * 512 + b - origin], lhsT=w_t[:, kc, wcol0:wcol0 + ncol], rhs=src[:, kc, a:b],
                    start=(kc == 0), stop=(kc == KC - 1)),
                    reads=src_bufs + [w_b], writes=bb, signal=(bi == len(blks) - 1 and kc == KC - 1))
        return b0, bb

    def phase_mixer(self):
        nc = self.nc
        pe, act, dve = self.pe, self.act, self.dve
        with ExitStack() as esB:
            self.es_cur = esB
            self.hT = self.sb("hT", [128, KC, TB], BF16)
            self.hT_b = [Buf(f"hT{k}") for k in range(KC)]
            self.cat = self.sb("cat", [128, KC, TB], BF16)
            self.cat_b = [Buf(f"cat{k}") for k in range(KC)]
            with ExitStack() as esA:
                self.es_cur = esA
                wring = self.ring("mw", [128, KC, 128], BF16, 4, dma=True)
                vbf = self.sb("vbf", [128, 9, 1024], BF16)
                vbf_b = [Buf(f"vbf{i}") for i in range(9)]
                dve.op(lambda: nc.vector.memset(vbf[:, 0, :], 0.0), writes=[vbf_b[0]])
                with ExitStack() as es1:
                    self.es_cur = es1
                    xt_ring = self.ring("mxt", [128, D], F32, 4, dma=True)
                    st_ring = self.ring("mst", [128, 4, 6], F32, 4)
                    mv_ring = self.ring("mmv", [128, 8], F32, 4)
                    stg = self.sb("mstg", [128, KC, 512], F32)
                    stgb = [Buf(f"mstg{k}") for k in range(KC)]
                    stgs = self.dsem("d_mstg")
                    groups = [[0], [1, 2, 3, 4], [5, 6, 7, 8]]
                    for grp in groups:
                        tiles = []
                        for ti in grp:
                            tlo, ntok = TILES[ti]
                            xt, xb, xs = xt_ring.next()
                            st, stb = st_ring.next()
                            mv, mvb = mv_ring.next()
                            tiles.append((self.x_own[tlo:tlo + ntok, :], ntok, xt, xb, xs, st, stb, mv, mvb))
                        self.ln0_stats(tiles)
                        for (xr, ntok, xt, xb, xs, st, stb, mv, mvb) in tiles:
                            act.op(lambda xt=xt, mv=mv, ntok=ntok: nc.scalar.activation(
                                out=xt[0:ntok, :], in_=xt[0:ntok, :], func=AF.Identity, scale=mv[0:ntok, 4:5], bias=mv[0:ntok, 5:6]),
                                reads=[xb, mvb], writes=[xb])
                        glo = TILES[grp[0]][0]
                        ncols = sum(TILES[ti][1] for ti in grp)
                        for kc in range(KC):
                            b0, bb = self.banks(1)
                            col = 0
                            for gi, (xr, ntok, xt, xb, xs, st, stb, mv, mvb) in enumerate(tiles):
                                pe.op(lambda kc=kc, b0=b0, col=col, xt=xt, ntok=ntok: nc.tensor.transpose(
                                    self.ps[:, b0 * 512 + col: b0 * 512 + col + ntok],
                                    xt[0:ntok, kc * 128:(kc + 1) * 128], self.ident[0:ntok, 0:ntok]),
                                    reads=[xb, self.cst_b], writes=bb, signal=(gi == len(tiles) - 1))
                                col += ntok
                            act.op(lambda kc=kc, b0=b0: nc.scalar.activation(
                                out=stg[:, kc, 0:ncols], in_=self.ps[:, b0 * 512: b0 * 512 + ncols], func=AF.Identity,
                                scale=self.pcol(P_G0, kc), bias=self.pcol(P_B0, kc)),
                                reads=bb + [self.par_b], writes=[stgb[kc]])
                            dve.op(lambda kc=kc: nc.vector.tensor_copy(self.hT[:, kc, glo:glo + ncols], stg[:, kc, 0:ncols]),
                                   reads=[stgb[kc]], writes=[self.hT_b[kc]])
                        self.sp.dma(self.h0s[:, :, glo:glo + ncols], stg[:, :, 0:ncols], stgs, reads=stgb)
                    self.h0s_sems = [stgs]
                    self.dump("hT0", self.hT[:], [128, KC, TB], self.hT_b[KC - 1])
                    self.barrier()
                self.es_cur = esA
                if self.stop_after == "ln0":
                    self.barrier()
                    self.es_cur = self.es
                    return
                with ExitStack() as es3:
                    self.es_cur = es3
                    iring = self.ring("mwi", [128, KC, 512], BF16, 2, dma=True)
                    wi = [self.load_w(iring, self.w_in[:, 4096 + hf * 512:4096 + (hf + 1) * 512], 512) for hf in range(2)]

                    def v_job(ti):
                        tlo, ntok = TILES[ti]
                        b0, bb = self.banks(2)
                        for half in range(2):
                            w_t, w_b = wi[half]
                            for kc in range(KC):
                                pe.op(lambda half=half, kc=kc, w_t=w_t, b0=b0, tlo=tlo, ntok=ntok: nc.tensor.matmul(
                                    self.ps[0:ntok, (b0 + half) * 512:(b0 + half + 1) * 512],
                                    lhsT=self.hT[:, kc, tlo:tlo + ntok], rhs=w_t[:, kc, :],
                                    start=(kc == 0), stop=(kc == KC - 1)),
                                    reads=[self.hT_b[kc], w_b], writes=bb, signal=(half == 1 and kc == KC - 1))
                        act.op(lambda ti=ti, b0=b0, ntok=ntok: nc.scalar.activation(
                            out=vbf[0:ntok, ti, :], in_=self.ps[0:ntok, b0 * 512:(b0 + 2) * 512], func=AF.Copy),
                            reads=bb, writes=[vbf_b[ti]])

                    self.conv_branch(wring, v_job)
                    self.barrier()
                self.es_cur = esA
                if self.stop_after == "conv":
                    self.es_cur = self.es
                    return
                with ExitStack() as es4:
                    self.es_cur = es4
                    self.hgrn_branch(wring, vbf, vbf_b)
                    self.barrier()
                self.es_cur = esA
            self.es_cur = esB
            self.dump("cat", self.cat[:], [128, KC, TB], self.cat_b[KC - 1])
            if self.stop_after == "hgrn":
                self.barrier()
                self.es_cur = self.es
                return
            self.phase_out_ffn()
        self.es_cur = self.es

    def conv_branch(self, wring, v_job):
        nc = self.nc
        pe, act, dve = self.pe, self.act, self.dve
        NPT = POOL_TAPS
        t_ring = self.ring("ct", [128, TB], F32, 12 if NPT else 10)
        NO = TB - 30
        gp = self.gp
        loaded = {}

        def ensure(idx):
            if idx < 16 and idx not in loaded:
                jj, which = idx // 2, idx % 2
                c0 = which * 1024 + jj * 128
                loaded[idx] = self.load_w(wring, self.w_in[:, c0:c0 + 128], 128)

        def cw(j, k):
            return self.par[:, P_CW + j * CK + k: P_CW + j * CK + k + 1]

        def front(j):
            ensure(2 * j + 2)
            ensure(2 * j + 3)
            wa, wab = loaded.pop(2 * j)
            wg, wgb = loaded.pop(2 * j + 1)
            ba, bba = self.fm_job(wa, wab, self.hT, self.hT_b)
            bg, bbg = self.fm_job(wg, wgb, self.hT, self.hT_b)
            sg, sgb = t_ring.next()
            act.op(lambda: nc.scalar.activation(out=sg[:], in_=self.ps[:, bg * 512: bg * 512 + TB], func=AF.Sigmoid),
                   reads=bbg, writes=[sgb])
            u, ub = t_ring.next()
            dve.op(lambda: nc.vector.tensor_tensor(u[:], self.ps[:, ba * 512: ba * 512 + TB], sg[:], ALU.mult),
                   reads=bba + [sgb], writes=[ub])
            dve.op(lambda: nc.vector.tensor_scalar(u[:, 0:HALO], u[:, 0:HALO], self.pcol(P_HM), None, ALU.mult),
                   reads=[ub, self.par_b], writes=[ub])
            acc, accb = t_ring.next()
            hparts = split_buf(accb, 2)
            HB = [(30, 543, hparts[0]), (543, TB, hparts[1])]
            for (ha, hb_, hbuf) in HB:
                act.op(lambda ha=ha, hb_=hb_: nc.scalar.activation(out=acc[:, ha:hb_], in_=u[:, ha - 30:hb_ - 30], func=AF.Identity,
                                                                   scale=cw(j, 0), bias=self.pcol(P_CB, j)),
                       reads=[ub, self.par_b], writes=[hbuf])
            if j == 0:
                self.dump("u0", u[:], [128, TB], ub)
            return dict(j=j, u=u, ub=ub, acc=acc, accb=accb, hparts=hparts, HB=HB)

        def taps(C):
            j, u, ub, acc = C["j"], C["u"], C["ub"], C["acc"]
            if NPT:
                acc2, acc2b = t_ring.next()
                C["acc2"], C["acc2b"] = acc2, acc2b
                k0 = CK - NPT
                gp.op(lambda: nc.gpsimd.tensor_scalar(acc2[:, 30:TB], u[:, k0:k0 + NO], cw(j, k0), None, ALU.mult),
                      reads=[ub, self.par_b], writes=[acc2b])
                for k in range(k0 + 1, CK):
                    gp.op(lambda k=k: nc.gpsimd.scalar_tensor_tensor(acc2[:, 30:TB], u[:, k:k + NO], cw(j, k), acc2[:, 30:TB],
                                                                     ALU.mult, ALU.add),
                          reads=[ub, acc2b, self.par_b], writes=[acc2b])
            for k in range(1, CK - NPT):
                for (ha, hb_, hbuf) in C["HB"]:
                    dve.op(lambda k=k, ha=ha, hb_=hb_: nc.vector.scalar_tensor_tensor(
                        acc[:, ha:hb_], u[:, ha - 30 + k:hb_ - 30 + k], cw(j, k), acc[:, ha:hb_], ALU.mult, ALU.add),
                        reads=[ub, hbuf, self.par_b], writes=[hbuf])
            join_buf(C["accb"], C["hparts"])

        def stats(C):
            acc, accb = C["acc"], C["accb"]
            if NPT:
                dve.op(lambda: nc.vector.tensor_tensor(acc[:, 30:TB], acc[:, 30:TB], C["acc2"][:, 30:TB], ALU.add),
                       reads=[accb, C["acc2b"]], writes=[accb])
            sq, sqb = t_ring.next()
            act.op(lambda: nc.scalar.activation(out=sq[:, 30:TB], in_=acc[:, 30:TB], func=AF.Square), reads=[accb], writes=[sqb])
            bm, bbm = self.stat_mm(self.ones128, acc, accb, 30, TB)
            bq, bbq = self.stat_mm(self.ones128, sq, sqb, 30, TB)
            C["mean"] = self.ps[:, bm * 512 + 30: bm * 512 + TB]
            C["bbm"] = bbm
            ex2 = self.ps[:, bq * 512 + 30: bq * 512 + TB]
            m2, m2b = t_ring.next()
            act.op(lambda: nc.scalar.activation(out=m2[:, 0:NO], in_=C["mean"], func=AF.Square), reads=bbm, writes=[m2b])
            C.update(m2=m2, m2b=m2b, ex2=ex2, bbq=bbq)

        def norm(C):
            j, acc, accb, m2, m2b, mean, bbm = C["j"], C["acc"], C["accb"], C["m2"], C["m2b"], C["mean"], C["bbm"]
            dve.op(lambda: nc.vector.scalar_tensor_tensor(m2[:, 0:NO], C["ex2"], LN_EPS, m2[:, 0:NO], ALU.add, ALU.subtract),
                   reads=C["bbq"] + [m2b], writes=[m2b])
            act.op(lambda: nc.scalar.activation(out=m2[:, 0:NO], in_=m2[:, 0:NO], func=AF.Sqrt), reads=[m2b], writes=[m2b])
            dve.op(lambda: nc.vector.tensor_tensor(acc[:, 30:TB], acc[:, 30:TB], mean, ALU.subtract),
                   reads=bbm + [accb], writes=[accb])
            dve.op(lambda: nc.vector.reciprocal(m2[:, 0:NO], m2[:, 0:NO]), reads=[m2b], writes=[m2b])
            dve.op(lambda: nc.vector.scalar_tensor_tensor(acc[:, 30:TB], acc[:, 30:TB], self.pcol(P_CNG, j), m2[:, 0:NO],
                                                          ALU.mult, ALU.mult),
                   reads=[accb, m2b, self.par_b], writes=[accb])
            act.op(lambda: nc.scalar.activation(out=self.cat[:, j, 30:TB], in_=acc[:, 30:TB], func=AF.Silu,
                                                bias=self.pcol(P_CNB, j)),
                   reads=[accb, self.par_b], writes=[self.cat_b[j]])

        ensure(0)
        ensure(1)
        cur = front(0)
        taps(cur)
        for j in range(8):
            nxt = front(j + 1) if j + 1 < 8 else None
            stats(cur)
            if nxt is not None:
                taps(nxt)
            norm(cur)
            cur = nxt
        for ti in range(9):
            v_job(ti)
        for j in range(8):
            dve.op(lambda j=j: nc.vector.memset(self.cat[:, j, 0:30], 0.0), reads=[self.cat_b[j]], writes=[self.cat_b[j]])

    def stat_mm(self, ones, src, srcb, lo, hi):
        nc = self.nc
        b0, bb = self.banks(3)
        blks = [(max(a, lo), min(b, hi)) for (a, b) in BLOCKS if max(a, lo) < min(b, hi)]
        for bi, (a, b) in enumerate(blks):
            self.pe.op(lambda a=a, b=b: nc.tensor.matmul(self.ps[:, b0 * 512 + a: b0 * 512 + b], lhsT=ones[:], rhs=src[:, a:b],
                                                         start=True, stop=True),
                       reads=[srcb, self.misc_b], writes=bb, signal=(bi == len(blks) - 1))
        return b0, bb

    def rstd_from(self, mean, bbm, ex2, bbq, t_ring, n, eps):
        nc = self.nc
        m2, m2b = t_ring.next()
        self.act.op(lambda: nc.scalar.activation(out=m2[:, 0:n], in_=mean, func=AF.Square), reads=bbm, writes=[m2b])
        self.dve.op(lambda: nc.vector.scalar_tensor_tensor(m2[:, 0:n], ex2, eps, m2[:, 0:n], ALU.add, ALU.subtract),
                    reads=bbq + [m2b], writes=[m2b])
        self.act.op(lambda: nc.scalar.activation(out=m2[:, 0:n], in_=m2[:, 0:n], func=AF.Sqrt), reads=[m2b], writes=[m2b])
        self.dve.op(lambda: nc.vector.reciprocal(m2[:, 0:n], m2[:, 0:n]), reads=[m2b], writes=[m2b])
        return m2, m2b

    def hgrn_branch(self, wring, vbf, vbf_b):
        nc = self.nc
        pe, act, dve = self.pe, self.act, self.dve
        t_ring = self.ring("ht", [128, TB], F32, 7)
        q_ring = self.ring("hq", [128, TB], BF16, 2)
        k_ring = self.ring("hk", [128, TB], BF16, 2)
        ktok_ring = self.ring("hkt", [128, 2, 9, 128], BF16, 2)
        at_ring = self.ring("hat", [128, 9, 128], BF16, 2)
        for i_ in range(2):
            dve.op(lambda i_=i_: nc.vector.memset(ktok_ring.tiles[i_][:, 0, 0, :], 0.0), writes=[ktok_ring.bufs[i_]])
            dve.op(lambda i_=i_: nc.vector.memset(at_ring.tiles[i_][:, 0, :], 0.0), writes=[at_ring.bufs[i_]])
        sbf_ring = self.ring("hsb", [128, 17, 128], BF16, 2)
        sf_ring = self.ring("hsf", [128, 16, 128], F32, 1)
        kvs_ring = self.ring("hkv", [128, 16, 128], F32, 1)
        cols = {"q": 2048, "f": 3072, "og": 5120}
        order = []
        for h in range(8):
            order += [(h, "f"), (h, "q"), (h, "og")]
        loaded = {}
        def ensure(idx):
            if idx < len(order) and idx not in loaded:
                hh, nm = order[idx]
                c0 = cols[nm] + hh * 128
                loaded[idx] = self.load_w(wring, self.w_in[:, c0:c0 + 128], 128)
        for idx in range(3):
            ensure(idx)
        for h in range(8):
            wf_, wfb = loaded[3 * h]
            bf_, bbf = self.fm_job(wf_, wfb, self.hT, self.hT_b)
            ensure(3 * h + 3)
            sig, sigb = t_ring.next()
            act.op(lambda: nc.scalar.activation(out=sig[:], in_=self.ps[:, bf_ * 512: bf_ * 512 + TB], func=AF.Sigmoid),
                   reads=bbf, writes=[sigb])
            kk, kkb = t_ring.next()
            dve.op(lambda: nc.vector.tensor_scalar(kk[:], sig[:], self.noml[:, h:h + 1], self.oml[:, h:h + 1], ALU.mult, ALU.add),
                   reads=[sigb, self.misc_b], writes=[kkb])
            lf, lfb = t_ring.next()
            act.op(lambda: nc.scalar.activation(out=lf[:], in_=sig[:], func=AF.Ln, scale=self.oml[:, h:h + 1],
                                                bias=self.lbv[:, h:h + 1]), reads=[sigb, self.misc_b], writes=[lfb])
            cum, cumb = t_ring.next()
            dve.op(lambda: nc.vector.tensor_tensor_scan(cum[:], self.rmask, lf[:], 0.0, ALU.mult, ALU.add),
                   reads=[lfb, self.cst_b], writes=[cumb])
            eq, eqb = t_ring.next()
            act.op(lambda: nc.scalar.activation(out=eq[:], in_=cum[:], func=AF.Exp), reads=[cumb], writes=[eqb])
            act.op(lambda: nc.scalar.activation(out=lf[:], in_=cum[:], func=AF.Exp, scale=-1.0), reads=[cumb], writes=[lfb])
            kT, kTb = k_ring.next()
            dve.op(lambda: nc.vector.tensor_tensor(kT[:], kk[:], lf[:], ALU.mult), reads=[kkb, lfb], writes=[kTb])
            dve.op(lambda: nc.vector.tensor_scalar(kT[:, 0:HALO], kT[:, 0:HALO], self.pcol(P_HM), None, ALU.mult),
                   reads=[kTb, self.par_b], writes=[kTb])
            wq_, wqb = loaded[3 * h + 1]
            bq_, bbq = self.fm_job(wq_, wqb, self.hT, self.hT_b)
            ensure(3 * h + 4)
            qs, qsb = t_ring.next()
            act.op(lambda: nc.scalar.activation(out=qs[:], in_=self.ps[:, bq_ * 512: bq_ * 512 + TB], func=AF.Silu),
                   reads=bbq, writes=[qsb])
            qT, qTb = q_ring.next()
            dve.op(lambda: nc.vector.tensor_tensor(qT[:], qs[:], eq[:], ALU.mult), reads=[qsb, eqb], writes=[qTb])
            wo_, wob = loaded[3 * h + 2]
            bo_, bbo = self.fm_job(wo_, wob, self.hT, self.hT_b)
            ensure(3 * h + 5)
            ogs, ogsb = t_ring.next()
            act.op(lambda: nc.scalar.activation(out=ogs[:], in_=self.ps[:, bo_ * 512: bo_ * 512 + TB], func=AF.Silu),
                   reads=bbo, writes=[ogsb])
            ktok, ktokb = ktok_ring.next()
            b0, bb = self.banks(2)
            for ti, (tlo, ntok) in enumerate(TILES):
                o = self.psb[0:ntok, b0 * 1024 + ti * 128: b0 * 1024 + (ti + 1) * 128]
                pe.op(lambda o=o, tlo=tlo, ntok=ntok: nc.tensor.transpose(o, kT[:, tlo:tlo + ntok], self.ident_bf[:]),
                      reads=[kTb, self.misc_b], writes=bb, signal=(ti == 8))
            act.op(lambda: nc.scalar.activation(out=ktok[0:32, 0, 0, :], in_=self.psb[0:32, b0 * 1024: b0 * 1024 + 128], func=AF.Copy),
                   reads=bb, writes=[ktokb])
            for ab, mcol in ((0, C_MA), (1, C_MB)):
                act.op(lambda ab=ab, mcol=mcol: nc.scalar.activation(
                    out=ktok[:, ab, 1:9, :].rearrange("p a b -> p (a b)"),
                    in_=self.psb[:, b0 * 1024 + 128: b0 * 1024 + 9 * 128], func=AF.Identity,
                    scale=self.cst[:, mcol:mcol + 1]),
                    reads=bb + [self.cst_b], writes=[ktokb])
            at, atb = at_ring.next()
            b0, bba = self.banks(3)
            for ti, (tlo, ntok) in enumerate(TILES):
                pe.op(lambda ti=ti, tlo=tlo, ntok=ntok: nc.tensor.matmul(
                    self.ps[0:ntok, b0 * 512 + ti * 128: b0 * 512 + ti * 128 + ntok],
                    lhsT=kT[:, tlo:tlo + ntok], rhs=qT[:, tlo:tlo + ntok], start=True, stop=True),
                    reads=[kTb, qTb], writes=bba, signal=(ti == 8))
            dve.op(lambda: nc.vector.tensor_tensor(at[0:32, 0, 0:32], self.ps[0:32, b0 * 512: b0 * 512 + 32], self.cmask[0:32, 0:32],
                                                   ALU.mult), reads=bba + [self.cst_b], writes=[atb])
            dve.op(lambda: nc.vector.tensor_tensor(
                at[:, 1:9, :], self.ps[:, b0 * 512 + 128: b0 * 512 + 9 * 128].rearrange("p (a b) -> p a b", a=8),
                self.cmask.unsqueeze(1).to_broadcast([128, 8, 128]), ALU.mult), reads=bba + [self.cst_b], writes=[atb])
            sbf, sbfb = sbf_ring.next()
            sf, sfb = sf_ring.next()
            kvbanks = []
            for g4 in range(4):
                b0k, bbk = self.banks(1)
                kvbanks.append((b0k, bbk))
                for cc in range(4):
                    c = g4 * 4 + cc
                    ti, plo, n, tlo = CHUNKS[c]
                    pe.op(lambda ti=ti, plo=plo, n=n, cc=cc, b0k=b0k: nc.tensor.matmul(
                        self.ps[:, b0k * 512 + cc * 128: b0k * 512 + (cc + 1) * 128],
                        lhsT=ktok[:, (0 if plo == 0 else 1), ti, :], rhs=vbf[:, ti, h * 128:(h + 1) * 128], start=True, stop=True),
                        reads=[ktokb, vbf_b[ti]], writes=bbk, signal=(cc == 3))
            kvs, kvsb = kvs_ring.next()
            kvs_bufs = split_buf(kvsb, 16)
            for c in range(16):
                b0k, bbk = kvbanks[c // 4]
                kv = self.ps[:, b0k * 512 + (c % 4) * 128: b0k * 512 + (c % 4 + 1) * 128]
                tlo_end = CHUNKS[c][3] + CHUNKS[c][2] - 1
                act.op(lambda c=c, kv=kv, tlo_end=tlo_end: nc.scalar.activation(
                    out=kvs[:, c, :], in_=kv, func=AF.Identity, scale=eq[:, tlo_end:tlo_end + 1]),
                    reads=bbk + [eqb], writes=[kvs_bufs[c]])
            prev = self.Sin[:, h, :]
            prevb = self.Sin_b
            sparts = split_buf(sfb, 2)
            for c in range(16):
                tlo_end = CHUNKS[c][3] + CHUNKS[c][2] - 1
                for hf in range(2):
                    pb_ = prevb if c == 0 else sparts[hf]
                    dve.op(lambda c=c, prev=prev, tlo_end=tlo_end, hf=hf: nc.vector.scalar_tensor_tensor(
                        sf[:, c, hf * 64:(hf + 1) * 64], prev[:, hf * 64:(hf + 1) * 64], eq[:, tlo_end:tlo_end + 1],
                        kvs[:, c, hf * 64:(hf + 1) * 64], ALU.mult, ALU.add),
                        reads=[pb_, kvs_bufs[c], eqb], writes=[sparts[hf]])
                prev = sf[:, c, :]
            join_buf(sfb, sparts)
            join_buf(kvsb, kvs_bufs)
            act.op(lambda: nc.scalar.activation(out=sbf[:, 1:17, :].rearrange("p a b -> p (a b)"),
                                                in_=sf[:, :, :].rearrange("p a b -> p (a b)"), func=AF.Copy),
                   reads=[sfb], writes=[sbfb])
            b0o, bbo2 = self.banks(3)
            for ti, (tlo, ntok) in enumerate(TILES):
                oc = b0o * 512 + ti * 128
                pe.op(lambda ti=ti, ntok=ntok, oc=oc: nc.tensor.matmul(
                    self.ps[:, oc: oc + ntok], lhsT=vbf[:, ti, h * 128:(h + 1) * 128], rhs=at[:, ti, 0:ntok],
                    start=True, stop=False, skip_group_check=True), reads=[vbf_b[ti], atb], writes=bbo2, signal=False)
                cs = [c for c in range(17) if CHUNKS[c][0] == ti]
                for ci, c in enumerate(cs):
                    _, plo, n, ctlo = CHUNKS[c]
                    lhs = self.Sin_bf[:, h, :] if c == 0 else sbf[:, c, :]
                    rb = [self.Sinbf_b] if c == 0 else [sbfb]
                    pe.op(lambda lhs=lhs, oc=oc, plo=plo, n=n, ctlo=ctlo: nc.tensor.matmul(
                        self.ps[:, oc + plo: oc + plo + n], lhsT=lhs, rhs=qT[:, ctlo:ctlo + n], start=False,
                        stop=True, skip_group_check=True), reads=rb + [qTb], writes=bbo2,
                        signal=(ti == 8 and ci == len(cs) - 1))
            o_sb, osb = t_ring.next()
            act.op(lambda: nc.scalar.activation(out=o_sb[:, 0:32], in_=self.ps[:, b0o * 512: b0o * 512 + 32], func=AF.Copy),
                   reads=bbo2, writes=[osb])
            act.op(lambda: nc.scalar.activation(out=o_sb[:, 32:TB], in_=self.ps[:, b0o * 512 + 128: b0o * 512 + 9 * 128], func=AF.Copy),
                   reads=bbo2, writes=[osb])
            if h == 0:
                self.dump("o0", o_sb[:], [128, TB], osb)
            act.op(lambda: nc.scalar.activation(out=cum[:], in_=o_sb[:], func=AF.Square), reads=[osb], writes=[cumb])
            bm, bbm = self.stat_mm(self.ones128, cum, cumb, 0, TB)
            dve.op(lambda: nc.vector.tensor_scalar(cum[:], self.ps[:, bm * 512: bm * 512 + TB], RMS_EPS, None, ALU.add),
                   reads=bbm, writes=[cumb])
            act.op(lambda: nc.scalar.activation(out=cum[:], in_=cum[:], func=AF.Sqrt), reads=[cumb], writes=[cumb])
            dve.op(lambda: nc.vector.reciprocal(cum[:], cum[:]), reads=[cumb], writes=[cumb])
            dve.op(lambda: nc.vector.scalar_tensor_tensor(o_sb[:], o_sb[:], self.pcol(P_HG, h), cum[:], ALU.mult, ALU.mult),
                   reads=[osb, cumb, self.par_b], writes=[osb])
            dve.op(lambda: nc.vector.tensor_tensor(self.cat[:, 8 + h, :], o_sb[:], ogs[:], ALU.mult),
                   reads=[osb, ogsb], writes=[self.cat_b[8 + h]])

    def ln_begin(self, lo, hi, nsq=2):
        t_ring = self.ring("lnt", [128, TB], F32, 3)
        sq_ring = self.ring("lnq", [128, TB], F32, nsq)
        sy, syb = t_ring.next()
        ss, ssb = t_ring.next()
        return dict(lo=lo, hi=hi, t_ring=t_ring, sq_ring=sq_ring, sy=sy, syb=syb, ss=ss, ssb=ssb)

    def ln_accum(self, L, Y, Yb, m, first):
        nc = self.nc
        act, dve = self.act, self.dve
        lo, hi, sy, syb, ss, ssb = L["lo"], L["hi"], L["sy"], L["syb"], L["ss"], L["ssb"]
        sq, sqb = L["sq_ring"].next()
        act.op(lambda: nc.scalar.activation(out=sq[:, lo:hi], in_=Y[:, m, lo:hi], func=AF.Square),
               reads=[Yb[m]], writes=[sqb])
        if first:
            dve.op(lambda: nc.vector.tensor_copy(sy[:, lo:hi], Y[:, m, lo:hi]), reads=[Yb[m]], writes=[syb])
            dve.op(lambda: nc.vector.tensor_copy(ss[:, lo:hi], sq[:, lo:hi]), reads=[sqb], writes=[ssb])
        else:
            dve.op(lambda: nc.vector.tensor_tensor(sy[:, lo:hi], sy[:, lo:hi], Y[:, m, lo:hi], ALU.add),
                   reads=[Yb[m], syb], writes=[syb])
            dve.op(lambda: nc.vector.tensor_tensor(ss[:, lo:hi], ss[:, lo:hi], sq[:, lo:hi], ALU.add),
                   reads=[sqb, ssb], writes=[ssb])

    def ln_finish(self, L, Y, Yb, g_off, b_off, write_bf=None, mask_halo=False):
        nc = self.nc
        pe, act, dve = self.pe, self.act, self.dve
        lo, hi, sy, syb, ss, ssb = L["lo"], L["hi"], L["sy"], L["syb"], L["ss"], L["ssb"]
        n = hi - lo
        bm, bbm = self.stat_mm(self.onesD, sy, syb, lo, hi)
        bq, bbq = self.stat_mm(self.onesD, ss, ssb, lo, hi)
        mean = self.ps[:, bm * 512 + lo: bm * 512 + hi]
        ex2 = self.ps[:, bq * 512 + lo: bq * 512 + hi]
        rstd, rstdb = self.rstd_from(mean, bbm, ex2, bbq, L["t_ring"], n, LN_EPS)
        dve.op(lambda: nc.vector.scalar_tensor_tensor(sy[:, lo:hi], mean, -1.0, rstd[:, 0:n], ALU.mult, ALU.mult),
               reads=bbm + [rstdb, syb], writes=[syb])
        for m0 in range(0, KC, 4):
            ms = range(m0, m0 + 4)
            for m in ms:
                dve.op(lambda m=m: nc.vector.tensor_tensor(Y[:, m, lo:hi], Y[:, m, lo:hi], rstd[:, 0:n], ALU.mult),
                       reads=[Yb[m], rstdb], writes=[Yb[m]])
            for m in ms:
                dve.op(lambda m=m: nc.vector.tensor_tensor(Y[:, m, lo:hi], Y[:, m, lo:hi], sy[:, lo:hi], ALU.add),
                       reads=[Yb[m], syb], writes=[Yb[m]])
            for m in ms:
                act.op(lambda m=m: nc.scalar.activation(out=Y[:, m, lo:hi], in_=Y[:, m, lo:hi], func=AF.Identity,
                                                        scale=self.pcol(g_off, m), bias=self.pcol(b_off, m)),
                       reads=[Yb[m], self.par_b], writes=[Yb[m]])
            if write_bf is not None:
                wt, wb = write_bf
                for m in ms:
                    dve.op(lambda m=m: nc.vector.tensor_copy(wt[:, m, lo:hi], Y[:, m, lo:hi]), reads=[Yb[m]], writes=[wb[m]])
                if mask_halo:
                    for m in ms:
                        dve.op(lambda m=m: nc.vector.tensor_scalar(wt[:, m, lo:HALO], wt[:, m, lo:HALO], self.pcol(P_HM), None, ALU.mult),
                               reads=[wb[m], self.par_b], writes=[wb[m]])

    def phase_out_ffn(self):
        nc = self.nc
        pe, act, dve = self.pe, self.act, self.dve
        with ExitStack() as esO:
            self.es_cur = esO
            Y = self.sb("Y", [128, KC, TB], F32)
            Yb = [Buf(f"Y{m}") for m in range(KC)]
            wring = self.ring("ow", [128, KC, 128], BF16, 4, dma=True)
            dring = self.ring("od", [128, GC, 128], BF16, 3, dma=True)
            with ExitStack() as e5:
                self.es_cur = e5
                L1 = self.ln_begin(30, TB)
                h0_ring = self.ring("oh0", [128, TB], F32, 2, dma=True)
                loaded = {}
                def ensure(m):
                    if m < KC and m not in loaded:
                        loaded[m] = self.load_w(wring, self.w_out[:, m * 128:(m + 1) * 128], 128)
                for m in range(3):
                    ensure(m)
                for s_ in self.h0s_sems:
                    self.sp.wait(s_, s_.count)
                for m in range(KC):
                    ensure(m + 3)
                    h0t, h0b, h0sem = h0_ring.next()
                    self.sp.dma(h0t[:], self.h0s[:, m, :], h0sem, writes=[h0b])
                    w_t, w_b = loaded[m]
                    b0, bb = self.fm_job(w_t, w_b, self.cat, self.cat_b, lo=30)
                    dve.op(lambda m=m, b0=b0, h0t=h0t: nc.vector.scalar_tensor_tensor(
                        Y[:, m, 30:TB], h0t[:, 30:TB], ALPHA, self.ps[:, b0 * 512 + 30: b0 * 512 + TB], ALU.mult, ALU.add),
                        reads=bb + [h0b], writes=[Yb[m]])
                    self.ln_accum(L1, Y, Yb, m, m == 0)
                self.dump("y1", Y[:], [128, KC, TB], Yb[KC - 1])
                self.ln_finish(L1, Y, Yb, P_G1, P_B1, write_bf=(self.hT, self.hT_b), mask_halo=True)
                self.dump("h1", Y[:], [128, KC, TB], Yb[KC - 1])
                self.barrier()
            self.es_cur = esO
            if self.stop_after == "mixer":
                self.es_cur = self.es
                return
            actb = self.cat[:, :, :].rearrange("p a b -> p (a b)")
            act_bufs = [Buf(f"act{j}") for j in range(GC)]
            with ExitStack() as e7:
                self.es_cur = e7
                t_ring = self.ring("ft", [128, TB], F32, 4)
                L2 = self.ln_begin(HALO, TB, nsq=1)
                jobs = [(g, jj) for g in range(NG) for jj in range(GC)]
                upl = {}
                def ensure_up(idx):
                    if idx < len(jobs) and idx not in upl:
                        j = idx
                        wg = self.load_w(wring, self.w_up[:, j * 128:(j + 1) * 128], 128)
                        wv = self.load_w(wring, self.w_up[:, DFF + j * 128: DFF + (j + 1) * 128], 128)
                        upl[idx] = (wg, wv)
                dnl = {}
                def ensure_dn(g, m):
                    if g < NG and m < KC and (g, m) not in dnl:
                        dnl[(g, m)] = self.load_w(dring, self.w_down[g * GC * 128:(g + 1) * GC * 128, m * 128:(m + 1) * 128], 128, nk=GC)
                ensure_up(0)
                for g in range(NG):
                    for jj in range(GC):
                        idx = g * GC + jj
                        j = idx
                        ensure_up(idx + 1)
                        if jj == GC - 1:
                            ensure_dn(g, 0)
                            ensure_dn(g, 1)
                        (wg, wgb), (wv, wvb) = upl.pop(idx)
                        bg, bbg = self.fm_job(wg, wgb, self.hT, self.hT_b, lo=30, origin=30)
                        bv, bbv = self.fm_job(wv, wvb, self.hT, self.hT_b, lo=HALO, origin=HALO)
                        gs, gsb = t_ring.next()
                        act.op(lambda bg=bg: nc.scalar.activation(out=gs[:, 30:TB], in_=self.ps[:, bg * 512: bg * 512 + TB - 30], func=AF.Copy),
                               reads=bbg, writes=[gsb])
                        c, cb = t_ring.next()
                        fw = lambda k, j=j: self.par[:, P_FW + j * 3 + k: P_FW + j * 3 + k + 1]
                        dve.op(lambda: nc.vector.tensor_scalar(c[:, 0:T], gs[:, 32:TB], fw(2), self.pcol(P_FB, j), ALU.mult, ALU.add),
                               reads=[gsb, self.par_b], writes=[cb])
                        dve.op(lambda: nc.vector.scalar_tensor_tensor(c[:, 0:T], gs[:, 31:TB - 1], fw(1), c[:, 0:T], ALU.mult, ALU.add),
                               reads=[gsb, cb, self.par_b], writes=[cb])
                        dve.op(lambda: nc.vector.scalar_tensor_tensor(c[:, 0:T], gs[:, 30:TB - 2], fw(0), c[:, 0:T], ALU.mult, ALU.add),
                               reads=[gsb, cb, self.par_b], writes=[cb])
                        act.op(lambda: nc.scalar.activation(out=c[:, 0:T], in_=c[:, 0:T], func=AF.Silu), reads=[cb], writes=[cb])
                        dve.op(lambda jj=jj, bv=bv: nc.vector.tensor_tensor(actb[:, jj * T:(jj + 1) * T], c[:, 0:T],
                                                                            self.ps[:, bv * 512: bv * 512 + T], ALU.mult),
                               reads=bbv + [cb], writes=[act_bufs[jj]])
                    for m in range(KC):
                        ensure_dn(g, m + 2)
                        w_t, w_b = dnl.pop((g, m))
                        b0, bb = self.banks(2)
                        for half in range(2):
                            for jj in range(GC):
                                pe.op(lambda half=half, jj=jj, m=m, w_t=w_t, b0=b0: nc.tensor.matmul(
                                    self.ps[:, (b0 + half) * 512:(b0 + half + 1) * 512], lhsT=w_t[:, jj, :],
                                    rhs=actb[:, jj * T + half * 512: jj * T + (half + 1) * 512],
                                    start=(jj == 0), stop=(jj == GC - 1)),
                                    reads=[act_bufs[jj], w_b], writes=bb, signal=(half == 1 and jj == GC - 1))
                        if g == 0:
                            dve.op(lambda m=m, b0=b0: nc.vector.scalar_tensor_tensor(
                                Y[:, m, HALO:TB], Y[:, m, HALO:TB], ALPHA, self.ps[:, b0 * 512:(b0 + 2) * 512], ALU.mult, ALU.add),
                                reads=bb + [Yb[m]], writes=[Yb[m]])
                        else:
                            dve.op(lambda m=m, b0=b0: nc.vector.tensor_tensor(
                                Y[:, m, HALO:TB], Y[:, m, HALO:TB], self.ps[:, b0 * 512:(b0 + 2) * 512], ALU.add),
                                reads=bb + [Yb[m]], writes=[Yb[m]])
                        if g == NG - 1:
                            self.ln_accum(L2, Y, Yb, m, m == 0)
                self.ln_finish(L2, Y, Yb, P_G2, P_B2)
                self.barrier()
            self.es_cur = esO
            with ExitStack() as e9:
                self.es_cur = e9
                o_ring = self.ring("oo", [128, D], F32, 2, dma=True)
                for ti in range(8):
                    ot, otb, osem = o_ring.next()
                    for q in range(4):
                        b0, bb = self.banks(1)
                        for kk in range(4):
                            kc = q * 4 + kk
                            pe.op(lambda kc=kc, kk=kk, b0=b0, ti=ti: nc.tensor.transpose(
                                self.ps[:, b0 * 512 + kk * 128: b0 * 512 + (kk + 1) * 128],
                                Y[:, kc, HALO + ti * 128: HALO + (ti + 1) * 128], self.ident),
                                reads=[Yb[kc], self.cst_b], writes=bb, signal=(kk == 3))
                        if q % 2 == 0:
                            act.op(lambda q=q, b0=b0, ot=ot: nc.scalar.activation(out=ot[:, q * 512:(q + 1) * 512],
                                                                                  in_=self.ps[:, b0 * 512:(b0 + 1) * 512], func=AF.Copy),
                                   reads=bb, writes=[otb])
                        else:
                            dve.op(lambda q=q, b0=b0, ot=ot: nc.vector.tensor_copy(ot[:, q * 512:(q + 1) * 512], self.ps[:, b0 * 512:(b0 + 1) * 512]),
                                   reads=bb, writes=[otb])
                    self.sp.dma(self.out_d[ti * 128:(ti + 1) * 128, :], ot[:], osem, reads=[otb])
                self.final_waits.extend(o_ring.sems)
                self.barrier()
            self.es_cur = esO
        self.es_cur = self.es


def _pack_params(inp, hmask):
    P = np.zeros((128, NPP), np.float32)
    fm = lambda v, n: np.ascontiguousarray(np.asarray(v, np.float32).reshape(n, 128).T)
    P[:, P_G0:P_G0 + 16] = fm(inp["emb_ln_g"], 16)
    P[:, P_B0:P_B0 + 16] = fm(inp["emb_ln_b"], 16)
    cw = np.asarray(inp["conv_w"], np.float32)[0]
    P[:, P_CW:P_CW + 248] = cw.reshape(CK, 8, 128).transpose(2, 1, 0).reshape(128, 248)
    P[:, P_CB:P_CB + 8] = fm(inp["conv_b"][0], 8)
    P[:, P_CNG:P_CNG + 8] = fm(inp["conv_norm_g"][0], 8)
    P[:, P_CNB:P_CNB + 8] = fm(inp["conv_norm_b"][0], 8)
    lbl = np.asarray(inp["lb_logits"], np.float32)
    P[:, P_LBL:P_LBL + 16] = lbl.reshape(2, 8, 128).transpose(2, 1, 0).reshape(128, 16)
    P[:, P_HG:P_HG + 8] = fm(inp["hgrn_norm_g"][0], 8)
    P[:, P_G1:P_G1 + 16] = fm(inp["ln1_g"][0], 16)
    P[:, P_B1:P_B1 + 16] = fm(inp["ln1_b"][0], 16)
    fw = np.asarray(inp["ffn_conv_w"], np.float32)[0]
    P[:, P_FW:P_FW + 132] = fw.reshape(3, FC, 128).transpose(2, 1, 0).reshape(128, 132)
    P[:, P_FB:P_FB + FC] = fm(inp["ffn_conv_b"][0], FC)
    P[:, P_G2:P_G2 + 16] = fm(inp["ln2_g"][0], 16)
    P[:, P_B2:P_B2 + 16] = fm(inp["ln2_b"][0], 16)
    P[:, P_HM] = hmask
    return P


def _consts():
    C = np.zeros((128, NCC), np.float32)
    C[:, C_ID:C_ID + 128] = np.eye(128, dtype=np.float32)
    s = np.arange(128)[:, None]
    t = np.arange(128)[None, :]
    C[:, C_CM:C_CM + 128] = ((s // 64 == t // 64) & (s <= t)).astype(np.float32)
    rm = np.ones(TB, np.float32)
    for (_, _, _, tlo) in CHUNKS:
        rm[tlo] = 0.0
    C[:, C_RM:C_RM + TB] = rm[None, :]
    C[0:64, C_MA] = 1.0
    C[64:128, C_MB] = 1.0
    return C


_CACHE = {}


def make_in_maps(inputs, cores=range(8)):
    x = np.asarray(inputs["x"], np.float32)
    w_in = np.ascontiguousarray(np.asarray(inputs["w_in"], np.float32)[0])
    w_out = np.ascontiguousarray(np.asarray(inputs["w_out"], np.float32)[0])
    w_up = np.ascontiguousarray(np.asarray(inputs["w_ffn_up"], np.float32)[0])
    w_down = np.ascontiguousarray(np.asarray(inputs["w_ffn_down"], np.float32)[0])
    consts = _consts()
    maps = []
    for c in cores:
        b, p = c // 4, c % 4
        t0 = p * T
        xo = np.zeros((TB, D), np.float32)
        lo = t0 - HALO
        if lo >= 0:
            xo[:] = x[b, lo:t0 + T]
        else:
            xo[HALO:] = x[b, 0:T]
        xp = np.zeros((NPRE, D), np.float32)
        pm = np.zeros(NPRE, np.float32)
        end = t0 - HALO
        nval = max(0, end)
        if nval > 0:
            xp[NPRE - nval:] = x[b, 0:end]
            pm[NPRE - nval:] = 1.0
        pmask = np.ascontiguousarray(pm.reshape(NPRE // 128, 128).T)
        maps.append({
            "x_own": xo, "x_pre": xp, "pmask": pmask,
            "params": _pack_params(inputs, 0.0 if p == 0 else 1.0), "consts": consts,
            "w_in": w_in, "w_out": w_out, "w_up": w_up, "w_down": w_down,
        })
    return maps


def kernel(**inputs):
    if "nc" not in _CACHE:
        _CACHE["nc"] = KB().build()
    nc = _CACHE["nc"]
    maps = make_in_maps(inputs)
    res = run_bass_kernel_spmd(nc, maps, core_ids=list(range(8)))
    out = np.zeros((2, 4 * T, D), np.float32)
    for c in range(8):
        b, p = c // 4, c % 4
        out[b, p * T:(p + 1) * T] = res.results[c]["out"]
    return out
```
